# Optimizing a Trainium2 kernel written in Bass

```python
import jax, jax.numpy as jnp
from jax import lax
import numpy as np

D_MODEL = 2048
BATCH = 4
SEQ = 2048
DEPTH = 4
DEC_BATCH = 16
DEC_SEQ = 64
PAST_LEN = 1024

CHUNK = 64
N_MIXERS = 4
D_FF = 5632
N_MOD = 9
MACARON = 0.5
EPS = 1e-6
CONV_WIDTH = 31
CONV_STATE = CONV_WIDTH - 1
POOL_WINDOWS = (2, 4, 8, 16)
POOL_GROUPS = len(POOL_WINDOWS)
POOL_GW = D_MODEL // POOL_GROUPS
POOL_STATE = max(POOL_WINDOWS) - 1
HEAD_DIM = 64
N_HEADS = D_MODEL // HEAD_DIM
N_KV_HEADS = N_HEADS // 8
Q_PER_KV = N_HEADS // N_KV_HEADS
WINDOW = 128
SWA_BAND = -(-WINDOW // CHUNK) * CHUNK
ATTN_SCALE = HEAD_DIM ** -0.5
GMLP_CHUNK = 128
GMLP_WIDTH = 2 * D_MODEL
GMLP_GROUPS = 8
GMLP_GW = GMLP_WIDTH // GMLP_GROUPS

kernel_name = 'hybrid_streaming_encoder_step'


def _rmsnorm(x, g):
    xf = x.astype(jnp.float32)
    y = xf * lax.rsqrt(jnp.mean(xf * xf, axis=-1, keepdims=True) + EPS)
    return (y * g.astype(jnp.float32)).astype(x.dtype)


def _layernorm(x, g, b):
    xf = x.astype(jnp.float32)
    xc = xf - jnp.mean(xf, axis=-1, keepdims=True)
    var = jnp.mean(xc * xc, axis=-1, keepdims=True)
    y = xc * lax.rsqrt(var + EPS) * g.astype(jnp.float32) + b.astype(jnp.float32)
    return y.astype(x.dtype)


def _modulate(h, shift, scale):
    return h * (1 + scale[:, None, :]) + shift[:, None, :]


def _swiglu(h, w1, w3, w2):
    return (jax.nn.silu(h @ w1) * (h @ w3)) @ w2


def _conv_module(h, buf, w_pw1, b_pw1, w_dw, b_dw, ln_g, ln_b, w_pw2, b_pw2):
    a = h @ w_pw1 + b_pw1
    g = a[..., :D_MODEL] * jax.nn.sigmoid(a[..., D_MODEL:])
    ext = jnp.concatenate([buf.astype(g.dtype), g], axis=1)
    y = lax.conv_general_dilated(ext, w_dw[:, None, :].astype(ext.dtype), (1,), 'VALID',
                                 dimension_numbers=('NWC', 'WIO', 'NWC'),
                                 feature_group_count=D_MODEL) + b_dw
    y = jax.nn.silu(_layernorm(y, ln_g, ln_b))
    return y @ w_pw2 + b_pw2, ext[:, -CONV_STATE:]


def _pool_mixer(h, buf, pos0, w_in, w_grp, scale, w_out):
    B, T, _ = h.shape
    p = h @ w_in
    ext = jnp.concatenate([buf.astype(p.dtype), p], axis=1)
    cs = jnp.pad(jnp.cumsum(ext.astype(jnp.float32), axis=1), ((0, 0), (1, 0), (0, 0)))
    hi = cs[:, POOL_STATE + 1:]
    pos = pos0 + jnp.arange(T)
    pf = p.astype(jnp.float32)
    groups = []
    for gi, w in enumerate(POOL_WINDOWS):
        c0, c1 = gi * POOL_GW, (gi + 1) * POOL_GW
        lo = cs[:, POOL_STATE + 1 - w: POOL_STATE + 1 - w + T, c0:c1]
        cnt = jnp.minimum(pos + 1, w).astype(jnp.float32)[None, :, None]
        groups.append((hi[..., c0:c1] - lo) / cnt - pf[..., c0:c1])
    pooled = jnp.stack(groups, axis=2).astype(p.dtype)
    z = jnp.einsum('btgc,gcd->btgd', pooled, w_grp).reshape(B, T, D_MODEL) * scale
    return z @ w_out, ext[:, -POOL_STATE:]


def _alibi_slopes():
    return jnp.exp2(-8.0 * jnp.arange(1, N_HEADS + 1, dtype=jnp.float32) / N_HEADS)


def _attend(qb, kb, vb, dist, valid, sinks):
    slopes = _alibi_slopes().reshape(N_KV_HEADS, Q_PER_KV)
    s = jnp.einsum('bnqkgd,bnskd->bnkgqs', qb, kb, preferred_element_type=jnp.float32) * ATTN_SCALE
    s = s - slopes[None, None, :, :, None, None] * dist[None, :, None, None, :, :]
    s = jnp.where(valid[None, :, None, None, None, :], s, -jnp.inf)
    sk = sinks.astype(jnp.float32).reshape(N_KV_HEADS, Q_PER_KV)[None, None, :, :, None, None]
    m = jnp.maximum(jnp.max(s, axis=-1, keepdims=True), sk)
    e = jnp.exp(s - m)
    p = e / (jnp.sum(e, axis=-1, keepdims=True) + jnp.exp(sk - m))
    return jnp.einsum('bnkgqs,bnskd->bnqkgd', p.astype(vb.dtype), vb)


def _swa_prompt(h, wq, wk, wv, wo, sinks):
    B, T, _ = h.shape
    nb = T // CHUNK
    q = (h @ wq).reshape(B, nb, CHUNK, N_KV_HEADS, Q_PER_KV, HEAD_DIM)
    k = (h @ wk).reshape(B, T, N_KV_HEADS, HEAD_DIM)
    v = (h @ wv).reshape(B, T, N_KV_HEADS, HEAD_DIM)
    padw = ((0, 0), (SWA_BAND, 0), (0, 0), (0, 0))
    kp, vp = jnp.pad(k, padw), jnp.pad(v, padw)
    idx = jnp.arange(nb)[:, None] * CHUNK + jnp.arange(SWA_BAND + CHUNK)[None, :]
    kpos = idx - SWA_BAND
    qpos = jnp.arange(T).reshape(nb, CHUNK)
    dist = jnp.abs(qpos[:, :, None] - kpos[:, None, :]).astype(jnp.float32)
    o = _attend(q, kp[:, idx], vp[:, idx], dist, kpos >= 0, sinks)
    r = min(SWA_BAND, PAST_LEN)
    return o.reshape(B, T, N_HEADS * HEAD_DIM) @ wo, k[:, T - r:], v[:, T - r:]


def _swa_sample(h, k_buf, v_buf, wq, wk, wv, wo, sinks):
    B, T, _ = h.shape
    q = (h @ wq).reshape(B, 1, T, N_KV_HEADS, Q_PER_KV, HEAD_DIM)
    k = (h @ wk).reshape(B, T, N_KV_HEADS, HEAD_DIM)
    v = (h @ wv).reshape(B, T, N_KV_HEADS, HEAD_DIM)
    kk = jnp.concatenate([k_buf.astype(k.dtype), k], axis=1)
    vv = jnp.concatenate([v_buf.astype(v.dtype), v], axis=1)
    r = k_buf.shape[1]
    kpos = PAST_LEN - r + jnp.arange(r + T)
    qpos = PAST_LEN + jnp.arange(T)
    dist = jnp.abs(qpos[:, None] - kpos[None, :]).astype(jnp.float32)[None]
    valid = jnp.ones((1, r + T), dtype=bool)
    o = _attend(q, kk[:, None], vv[:, None], dist, valid, sinks)
    return o.reshape(B, T, N_HEADS * HEAD_DIM) @ wo, kk[:, T:], vv[:, T:]


def _gmlp(h, w_in, b_in, ln_g, ln_b, w_s, b_s, w_out, b_out, block):
    B, T, _ = h.shape
    z = jax.nn.gelu(h @ w_in + b_in)
    u, v = z[..., :GMLP_WIDTH], z[..., GMLP_WIDTH:]
    vn = _layernorm(v, ln_g, ln_b)
    L = block
    nc = T // L
    pos = jnp.arange(L)
    mask = (pos[None, :] // CHUNK) <= (pos[:, None] // CHUNK)
    ws = jnp.where(mask[None], w_s[:, :L, :L], 0)
    mixed = jnp.einsum('gts,bnsgc->bntgc', ws, vn.reshape(B, nc, L, GMLP_GROUPS, GMLP_GW))
    mixed = mixed + b_s[:, :L].T[None, None, :, :, None]
    out = (u * mixed.reshape(B, T, GMLP_WIDTH)) @ w_out + b_out
    return out, vn


def setup_inputs(seed: int = 0) -> dict:
    key = jax.random.key(seed)
    keys = iter(jax.random.split(key, 48))

    def nrm(shape, scale=1.0):
        return jax.random.normal(next(keys), shape, jnp.float32) * scale

    def gain(shape):
        return 1.0 + nrm(shape, 0.01)

    d, f = D_MODEL, D_FF
    r = min(SWA_BAND, PAST_LEN)
    hw = N_HEADS * HEAD_DIM
    kvw = N_KV_HEADS * HEAD_DIM
    return {
        'x_prompt': nrm((BATCH, SEQ, d)),
        'x_sample': nrm((DEC_BATCH, DEC_SEQ, d)),
        'c_prompt': nrm((BATCH, d)),
        'c_sample': nrm((DEC_BATCH, d)),
        'cache_conv': nrm((DEC_BATCH, CONV_STATE, d), 0.5),
        'cache_pool': nrm((DEC_BATCH, POOL_STATE, d)),
        'cache_swa_k': nrm((DEC_BATCH, r, N_KV_HEADS, HEAD_DIM)),
        'cache_swa_v': nrm((DEC_BATCH, r, N_KV_HEADS, HEAD_DIM)),
        'ada_w': nrm((DEPTH, d, N_MOD * d), 0.5 * d ** -0.5),
        'ada_b': nrm((DEPTH, N_MOD * d), 0.01),
        'norm_g': gain((DEPTH, 3, d)),
        'ffn_w1': nrm((DEPTH, 2, d, f), d ** -0.5),
        'ffn_w3': nrm((DEPTH, 2, d, f), d ** -0.5),
        'ffn_w2': nrm((DEPTH, 2, f, d), f ** -0.5),
        'conv_w_pw1': nrm((d, 2 * d), d ** -0.5),
        'conv_b_pw1': nrm((2 * d,), 0.01),
        'conv_w_dw': nrm((CONV_WIDTH, d), CONV_WIDTH ** -0.5),
        'conv_b_dw': nrm((d,), 0.01),
        'conv_ln_g': gain((d,)),
        'conv_ln_b': nrm((d,), 0.01),
        'conv_w_pw2': nrm((d, d), d ** -0.5),
        'conv_b_pw2': nrm((d,), 0.01),
        'pool_w_in': nrm((d, d), d ** -0.5),
        'pool_w_grp': nrm((POOL_GROUPS, POOL_GW, POOL_GW), POOL_GW ** -0.5),
        'pool_scale': 1.0 + nrm((d,), 0.1),
        'pool_w_out': nrm((d, d), d ** -0.5),
        'swa_wq': nrm((d, hw), d ** -0.5),
        'swa_wk': nrm((d, kvw), d ** -0.5),
        'swa_wv': nrm((d, kvw), d ** -0.5),
        'swa_wo': nrm((hw, d), hw ** -0.5),
        'swa_sinks': nrm((N_HEADS,), 0.5),
        'gmlp_w_in': nrm((d, 2 * GMLP_WIDTH), d ** -0.5),
        'gmlp_b_in': nrm((2 * GMLP_WIDTH,), 0.01),
        'gmlp_ln_g': gain((GMLP_WIDTH,)),
        'gmlp_ln_b': nrm((GMLP_WIDTH,), 0.01),
        'gmlp_w_s': nrm((GMLP_GROUPS, GMLP_CHUNK, GMLP_CHUNK), GMLP_CHUNK ** -0.5),
        'gmlp_b_s': 1.0 + nrm((GMLP_GROUPS, GMLP_CHUNK), 0.01),
        'gmlp_w_out': nrm((GMLP_WIDTH, d), GMLP_WIDTH ** -0.5),
        'gmlp_b_out': nrm((d,), 0.01),
        'final_g': gain((d,)),
    }


def reference(x_prompt, x_sample, c_prompt, c_sample,
              cache_conv, cache_pool, cache_swa_k, cache_swa_v,
              ada_w, ada_b, norm_g, ffn_w1, ffn_w3, ffn_w2,
              conv_w_pw1, conv_b_pw1, conv_w_dw, conv_b_dw, conv_ln_g, conv_ln_b, conv_w_pw2, conv_b_pw2,
              pool_w_in, pool_w_grp, pool_scale, pool_w_out,
              swa_wq, swa_wk, swa_wv, swa_wo, swa_sinks,
              gmlp_w_in, gmlp_b_in, gmlp_ln_g, gmlp_ln_b, gmlp_w_s, gmlp_b_s, gmlp_w_out, gmlp_b_out,
              final_g):

    def trunk(x, c, conv_buf, pool_buf, k_buf, v_buf, is_sample):
        pos0 = PAST_LEN if is_sample else 0
        cond = jax.nn.silu(c)
        conv_new = pool_new = k_new = v_new = gmlp_new = None
        for i in range(DEPTH):
            mod = cond @ ada_w[i] + ada_b[i]
            sh1, sc1, g1, sh2, sc2, g2, sh3, sc3, g3 = jnp.split(mod, N_MOD, axis=-1)
            h = _modulate(_rmsnorm(x, norm_g[i, 0]), sh1, sc1)
            x = x + MACARON * g1[:, None, :] * _swiglu(h, ffn_w1[i, 0], ffn_w3[i, 0], ffn_w2[i, 0])
            h = _modulate(_rmsnorm(x, norm_g[i, 1]), sh2, sc2)
            kind = i % N_MIXERS
            if kind == 0:
                out, conv_new = _conv_module(h, conv_buf, conv_w_pw1, conv_b_pw1, conv_w_dw, conv_b_dw,
                                             conv_ln_g, conv_ln_b, conv_w_pw2, conv_b_pw2)
            elif kind == 1:
                out, pool_new = _pool_mixer(h, pool_buf, pos0, pool_w_in, pool_w_grp, pool_scale, pool_w_out)
            elif kind == 2:
                if is_sample:
                    out, k_new, v_new = _swa_sample(h, k_buf, v_buf, swa_wq, swa_wk, swa_wv, swa_wo, swa_sinks)
                else:
                    out, k_new, v_new = _swa_prompt(h, swa_wq, swa_wk, swa_wv, swa_wo, swa_sinks)
            else:
                block = h.shape[1] if is_sample else GMLP_CHUNK
                out, gmlp_new = _gmlp(h, gmlp_w_in, gmlp_b_in, gmlp_ln_g, gmlp_ln_b, gmlp_w_s, gmlp_b_s,
                                      gmlp_w_out, gmlp_b_out, block)
            x = x + g2[:, None, :] * out
            h = _modulate(_rmsnorm(x, norm_g[i, 2]), sh3, sc3)
            x = x + MACARON * g3[:, None, :] * _swiglu(h, ffn_w1[i, 1], ffn_w3[i, 1], ffn_w2[i, 1])
        return _rmsnorm(x, final_g), conv_new, pool_new, k_new, v_new, gmlp_new

    bp = x_prompt.shape[0]
    zero_conv = jnp.zeros((bp, CONV_STATE, D_MODEL), x_prompt.dtype)
    zero_pool = jnp.zeros((bp, POOL_STATE, D_MODEL), x_prompt.dtype)
    y_prompt, conv_p, pool_p, k_p, v_p, _ = trunk(x_prompt, c_prompt, zero_conv, zero_pool, None, None, False)
    y_sample, conv_s, pool_s, k_s, v_s, gmlp_v_s = trunk(x_sample, c_sample, cache_conv, cache_pool,
                                                         cache_swa_k, cache_swa_v, True)
    return (y_prompt, y_sample, conv_p, conv_s, pool_p, pool_s, k_p, v_p, k_s, v_s, gmlp_v_s)
```

```python
import numpy as np
import concourse.bass as bass
import concourse.mybir as mybir
from contextlib import ExitStack
from concourse.bass_utils import run_bass_kernel_spmd
import ml_dtypes

F32 = mybir.dt.float32
BF16 = mybir.dt.bfloat16
AF = mybir.ActivationFunctionType
ALU = mybir.AluOpType
AX = mybir.AxisListType

ENGS = ['pe', 'act', 'dve', 'pool', 'sp']


_UNIQ = [0]


def SBT(nc, name, shape, dt):
    _UNIQ[0] += 1
    return nc.sbuf_tensor(f"sb_{name}_{_UNIQ[0]}", shape, dt)


class Prog:
    def __init__(self, nc, es):
        self.nc = nc
        self.es = es
        self.thunks = {e: [] for e in ENGS}
        self.sem = {}
        self.val = {}
        self.waited = {}
        self.res = {}
        self.nbank = 0
        self.ninstr = 0
        self.fence_deps = {}
        self.reserved = set()
        for e in ['pe', 'act', 'dve', 'pool']:
            self.new_sem(e)

    def new_sem(self, name):
        self.sem[name] = self.es.enter_context(self.nc.semaphore(name))
        self.val[name] = 0

    def sb(self, name, shape, dt):
        return self.es.enter_context(SBT(self.nc, name, shape, dt))

    def op(self, eng, meth, kw, reads=(), writes=(), sem=None, inc=1):
        fn = (meth, kw if isinstance(kw, list) else [kw])
        semname = sem or eng
        deps = dict(self.fence_deps)
        for r in reads:
            st = self.res.get(r)
            if st and st[0]:
                s, v = st[0]
                deps[s] = max(deps.get(s, 0), v)
        for w in writes:
            st = self.res.get(w)
            if st:
                if st[0]:
                    s, v = st[0]
                    deps[s] = max(deps.get(s, 0), v)
                for s, v in st[1].items():
                    deps[s] = max(deps.get(s, 0), v)
        waits = []
        for s, v in deps.items():
            if s == 'pe' and eng == 'pe':
                continue
            if self.waited.get((eng, s), 0) < v:
                self.waited[(eng, s)] = v
                waits.append((s, v))
        self.val[semname] += inc
        tok = (semname, self.val[semname])
        for r in reads:
            st = self.res.setdefault(r, [None, {}])
            st[1][tok[0]] = max(st[1].get(tok[0], 0), tok[1])
        for w in writes:
            self.res[w] = [tok, {}]
        self.thunks[eng].append((waits, fn, semname, inc))
        return tok

    def bank(self):
        while True:
            b = self.nbank % 8
            self.nbank += 1
            if b not in self.reserved:
                return b

    def reserve(self, n):
        out = []
        for _ in range(n):
            b = self.bank()
            self.reserved.add(b)
            out.append(b)
        return out

    def unreserve(self, banks):
        for b in banks:
            self.reserved.discard(b)

    def fence(self):
        for s in ['pe', 'act', 'dve', 'out']:
            if s in self.val and self.val[s] > 0:
                self.fence_deps[s] = self.val[s]

    def run_engine(self, eng, e):
        for waits, fn, semname, inc in self.thunks[eng]:
            for s, v in waits:
                e.wait_ge(self.sem[s], v)
            m = getattr(e, fn[0])
            for kw in fn[1]:
                inst = m(**kw)
                self.ninstr += 1
            inst.then_inc(self.sem[semname], inc)

    def emit(self):
        nc = self.nc
        with nc.Block() as block:
            @block.tensor
            def _(e):
                self.run_engine('pe', e)

            @block.scalar
            def _(e):
                self.run_engine('act', e)

            @block.vector
            def _(e):
                self.run_engine('dve', e)

            @block.gpsimd
            def _(e):
                self.run_engine('pool', e)

            @block.sync
            def _(e):
                self.run_engine('sp', e)
                for s in self.final_waits:
                    e.wait_ge(self.sem[s], self.val[s])


class WStream:
    def __init__(self, P, dram, npieces, R=12, eng='pool'):
        self.P = P
        self.dram = dram
        self.np = npieces
        self.R = R
        self.eng = eng
        self.slots = [P.sb(f"wslot{i}", [128, 2048], BF16) for i in range(R)]
        for i in range(R):
            P.new_sem(f"w{i}")
        self.cur = -1
        self.col = 2048
        self.loaded = 0
        self.released = 0
        self.rec = []
        for i in range(min(R, npieces)):
            self._load(i)

    def _load(self, i):
        s = i % self.R
        slot = self.slots[s]
        src = self.dram[i]
        self.P.op(self.eng, 'dma_start', dict(out=slot[:], in_=src),
                  writes=(f"ws{s}",), sem=f"w{s}", inc=16)
        self.loaded = i + 1

    def take(self, ncols, spec):
        if self.col + ncols > 2048:
            assert self.col == 2048, (self.col, ncols)
            self.cur += 1
            assert self.cur < self.np
            self.col = 0
        s = self.cur % self.R
        ap = self.slots[s][:, self.col:self.col + ncols]
        self.rec.append((self.cur, self.col, ncols, spec))
        self.col += ncols
        return ap, f"ws{s}"

    def commit(self):
        upto = self.cur if self.col == 2048 else self.cur - 1
        while self.released <= upto:
            j = self.released
            self.released += 1
            if j + self.R < self.np:
                self._load(j + self.R)

    def skip_to(self, align):
        if self.col % align:
            self.col += align - self.col % align

    def done(self):
        assert self.cur == self.np - 1, (self.cur, self.np, self.col)

T = 1344
NPR = 1216
HALO = 192
SEQ_BOUNDS = [(0, 1216), (1216, 1280), (1280, 1344)]
TILES_A = [(0, 448), (448, 896), (896, 1344)]
TILES_B = [(192, 576), (576, 960), (960, 1344)]
D = 2048
DEPTH = 4
RING = 6
EPS = 1e-6
ATT_SCALE = 0.125


def spans_of(c0, c1):
    out = []
    for si, (s0, s1) in enumerate(SEQ_BOUNDS):
        lo, hi = max(c0, s0), min(c1, s1)
        if lo < hi:
            out.append((lo, hi, si))
    return out


class Layout:
    def __init__(self):
        self.off = {}
        self.n = 0

    def add(self, name, n):
        self.off[name] = (self.n, n)
        self.n += n


SM = Layout()
SM.add('cT', 48)
SM.add('norm_g', 4 * 3 * 16)
SM.add('ada_b', 4 * 144)
SM.add('final_g', 16)
CV = Layout()
for nm, n in [('b1', 32), ('wdw', 16 * 31), ('bdw', 16), ('lng', 16), ('lnb', 16), ('b2', 16), ('coremask', 1)]:
    CV.add(nm, n)
PL = Layout()
for nm, n in [('scale', 16), ('coremask', 1), ('pcorr', 64)]:
    PL.add(nm, n)
AT = Layout()
for nm, n in [('sinks', 32), ('dist', 192), ('amask', 384), ('ident', 128)]:
    AT.add(nm, n)
GM = Layout()
for nm, n in [('bin_u', 32), ('bout', 16), ('gmask', 128), ('wsT', 1024), ('wsTs', 512), ('bs', 1024)]:
    GM.add(nm, n)


class Ctx:
    pass


def vv(tile, lay, name, a=None, b=None):
    o, n = lay.off[name]
    ap = tile[:, o:o + n]
    if a is not None:
        ap = ap.rearrange("p (a b) -> p a b", a=a)
    return ap


def setup_common(P, C, ins):
    nc = P.nc
    C.ps = [P.es.enter_context(nc.psum_tensor(f"ps{i}", [128, 512], F32)) for i in range(8)]
    C.x = P.sb("x", [128, 16, T], F32)
    C.h = P.sb("h", [128, 16, T], BF16)
    C.ones = P.sb("ones_bf", [128, 128], BF16)
    C.eps = P.sb("eps_t", [128, 1], F32)
    C.small = P.sb("small", [128, SM.n], F32)
    C.modT = P.sb("modT", [128, 9, 16, 3], F32)
    C.aT = P.sb("aT", [128, 3, 16, 3], F32)
    C.gT = P.sb("gT", [128, 3, 16, 3], F32)
    C.condb = P.sb("condb", [128, 16, 3], BF16)
    P.op('dve', 'memset', dict(ap=C.ones[:], constant=1.0), writes=('ones',))
    P.op('dve', 'memset', dict(ap=C.eps[:], constant=EPS), writes=('eps',))
    P.op('sp', 'dma_start', dict(out=C.small[:], in_=ins['small']), writes=('small',), sem='in', inc=16)
    xres = []
    for c in range(16):
        P.op('sp', 'dma_start', dict(out=C.x[:, c, :], in_=ins['xT'][:, c, :]), sem='in', inc=16)
        xres += [f"x{c}_{s}" for s in range(len(SEG))]
    tot = ('in', P.val['in'])
    for r in ['small'] + xres:
        P.res[r] = [tot, {}]
    cT = vv(C.small, SM, 'cT', 16)
    P.op('act', 'activation', dict(out=C.condb[:], in_=cT, func=AF.Silu), reads=('small',), writes=('condb',))


SEG = [(0, 192), (192, 448), (448, 576), (576, 704), (704, 896), (896, 960), (960, 1216), (1216, 1280), (1280, 1344)]


def xr(c, c0, c1, pre='x'):
    return tuple(f"{pre}{c}_{i}" for i, (a, b) in enumerate(SEG) if a < c1 and b > c0)


def hr_all(c0, c1):
    out = ()
    for k in range(16):
        out += xr(k, c0, c1, 'h')
    return out


def norm_mod(P, C, j, tiles, final=None):
    x, h = C.x, C.h
    with ExitStack() as es:
        nc = P.nc
        C.rs = es.enter_context(SBT(nc, "rstd", [128, T], F32))
        sq = [es.enter_context(SBT(nc, f"sq{i}", [128, 448], BF16)) for i in range(3)]
        tmp = [es.enter_context(SBT(nc, f"tmpf{i}", [128, 448], F32)) for i in range(3)]
        yo = [es.enter_context(SBT(nc, f"yo{i}", [128, 448], F32)) for i in range(3)] if final else None
        nsq = ntmp = 0
        for ti, (c0, c1) in enumerate(tiles):
            w = c1 - c0
            bk = P.bank()
            for c in range(16):
                jj = nsq % 3
                nsq += 1
                P.op('act', 'activation', dict(out=sq[jj][:, :w], in_=x[:, c, c0:c1], func=AF.Square),
                     reads=xr(c, c0, c1), writes=(f"sq{jj}",))
                P.op('pe', 'matmul', dict(out=C.ps[bk][:, :w], lhsT=C.ones[:], rhs=sq[jj][:, :w],
                                          start=(c == 0), stop=(c == 15)),
                     reads=(f"sq{jj}", 'ones'), writes=(f"ps{bk}",))
            P.op('act', 'activation', dict(out=C.rs[:, c0:c1], in_=C.ps[bk][:, :w], func=AF.Sqrt,
                                           bias=C.eps[:, 0:1], scale=1.0 / 2048.0),
                 reads=(f"ps{bk}", 'eps'), writes=(f"rs{ti}",))
            P.op('dve', 'reciprocal', dict(out=C.rs[:, c0:c1], in_=C.rs[:, c0:c1]),
                 reads=(f"rs{ti}",), writes=(f"rs{ti}",))
            for c in range(16):
                for (s0, s1, si) in spans_of(c0, c1):
                    jj = ntmp % 3
                    ntmp += 1
                    sw = s1 - s0
                    P.op('dve', 'tensor_tensor', dict(out=tmp[jj][:, :sw], in0=x[:, c, s0:s1], in1=C.rs[:, s0:s1],
                                                      op=ALU.mult),
                         reads=xr(c, s0, s1) + (f"rs{ti}",), writes=(f"tmp{jj}",))
                    if final is None:
                        P.op('act', 'activation', dict(out=h[:, c, s0:s1], in_=tmp[jj][:, :sw], func=AF.Identity,
                                                       bias=C.modT[:, 3 * j, c, si:si + 1],
                                                       scale=C.aT[:, j, c, si:si + 1]),
                             reads=(f"tmp{jj}", 'mod'), writes=xr(c, s0, s1, 'h'))
                    else:
                        fg = vv(C.small, SM, 'final_g')
                        P.op('act', 'activation', dict(out=yo[jj][:, :sw], in_=tmp[jj][:, :sw], func=AF.Identity,
                                                       bias=0.0, scale=fg[:, c:c + 1]),
                             reads=(f"tmp{jj}", 'small'), writes=(f"yo{jj}",))
                        P.op('sp', 'dma_start', dict(out=final[:, c, s0 - HALO:s1 - HALO], in_=yo[jj][:, :sw]),
                             reads=(f"yo{jj}",), sem='out', inc=16)
        P.fence()


def ffn(P, C, ws, layer, which, j, tiles):
    G = 4
    x, h = C.x, C.h
    nc = P.nc
    with ExitStack() as es:
        g = es.enter_context(SBT(nc, "ffn_g", [128, G, T], BF16))
        sl = [es.enter_context(SBT(nc, f"ffn_s{i}", [128, 448], F32)) for i in range(2)]
        nsl = 0
        idx = (layer, which)
        for grp in range(11):
            for fl in range(G):
                f = (grp * G + fl) * 128
                blk1 = [ws.take(128, ('ffn_w1', idx, k * 128, [(f, 128)])) for k in range(16)]
                blk3 = [ws.take(128, ('ffn_w3', idx, k * 128, [(f, 128)])) for k in range(16)]
                for ti, (c0, c1) in enumerate(tiles):
                    w = c1 - c0
                    b1 = P.bank()
                    b3 = P.bank()
                    for (blks, bk) in ((blk1, b1), (blk3, b3)):
                        P.op('pe', 'matmul',
                             [dict(out=C.ps[bk][:, :w], lhsT=blks[k][0], rhs=h[:, k, c0:c1],
                                   start=(k == 0), stop=(k == 15)) for k in range(16)],
                             reads=hr_all(c0, c1) + tuple(set(b[1] for b in blks)), writes=(f"ps{bk}",))
                    jj = nsl % 2
                    nsl += 1
                    P.op('act', 'activation', dict(out=sl[jj][:, :w], in_=C.ps[b1][:, :w], func=AF.Silu),
                         reads=(f"ps{b1}",), writes=(f"sl{jj}",))
                    P.op('dve', 'tensor_tensor', dict(out=g[:, fl, c0:c1], in0=C.ps[b3][:, :w], in1=sl[jj][:, :w],
                                                      op=ALU.mult),
                         reads=(f"ps{b3}", f"sl{jj}"), writes=(f"g{fl}_{ti}",))
                ws.commit()
            for m in range(16):
                blks = [ws.take(128, ('ffn_w2', idx, (grp * G + fl) * 128, [(m * 128, 128)])) for fl in range(G)]
                for ti, (c0, c1) in enumerate(tiles):
                    w = c1 - c0
                    bk = P.bank()
                    P.op('pe', 'matmul',
                         [dict(out=C.ps[bk][:, :w], lhsT=blks[fl][0], rhs=g[:, fl, c0:c1],
                               start=(fl == 0), stop=(fl == G - 1)) for fl in range(G)],
                         reads=tuple(f"g{fl}_{ti}" for fl in range(G)) + tuple(set(b[1] for b in blks)),
                         writes=(f"ps{bk}",))
                    for (s0, s1, si) in spans_of(c0, c1):
                        P.op('dve', 'scalar_tensor_tensor',
                             dict(out=x[:, m, s0:s1], in0=C.ps[bk][:, s0 - c0:s1 - c0],
                                  scalar=C.gT[:, j, m, si:si + 1], in1=x[:, m, s0:s1], op0=ALU.mult, op1=ALU.add),
                             reads=(f"ps{bk}", 'mod') + xr(m, s0, s1), writes=xr(m, s0, s1))
                ws.commit()
        P.fence()


def ada(P, C, ws, layer):
    adab = vv(C.small, SM, 'ada_b', 4 * 9)
    ng = vv(C.small, SM, 'norm_g', 12)
    for q in range(9):
        bk = P.bank()
        for c in range(16):
            m = q * 16 + c
            blks = [ws.take(128, ('ada_w', (layer,), k * 128, [(m * 128, 128)])) for k in range(16)]
            P.op('pe', 'matmul',
                 [dict(out=C.ps[bk][:, 3 * c:3 * c + 3], lhsT=blks[k][0], rhs=C.condb[:, k, :],
                       start=(k == 0), stop=(k == 15)) for k in range(16)],
                 reads=('condb',) + tuple(set(b[1] for b in blks)), writes=(f"ps{bk}",))
            ws.commit()
        bb = adab[:, layer * 9 + q, :].unsqueeze(2).to_broadcast([128, 16, 3])
        P.op('dve', 'tensor_tensor', dict(out=C.modT[:, q], in0=C.ps[bk][:, 0:48].rearrange("p (c s) -> p c s", s=3),
                                          in1=bb, op=ALU.add),
             reads=(f"ps{bk}", 'small'), writes=('mod',))
    for j in range(3):
        P.op('dve', 'tensor_scalar', dict(out=C.aT[:, j], in0=C.modT[:, 3 * j + 1], scalar1=1.0, scalar2=None,
                                          op0=ALU.add), reads=('mod',), writes=('mod',))
        gb = ng[:, layer * 3 + j, :].unsqueeze(2).to_broadcast([128, 16, 3])
        P.op('dve', 'tensor_tensor', dict(out=C.aT[:, j], in0=C.aT[:, j], in1=gb, op=ALU.mult),
             reads=('mod', 'small'), writes=('mod',))
        P.op('act', 'activation', dict(out=C.gT[:, j], in_=C.modT[:, 3 * j + 2], func=AF.Copy,
                                       scale=(1.0 if j == 1 else 0.5)), reads=('mod',), writes=('mod',))
    P.fence()


def linear_fm(P, C, ws, wname, idx, nk, ms, src, src_res, ncols, epi, colmap=None):
    for mi, m in enumerate(ms):
        cm = colmap(m) if colmap else [(m * 128, 128)]
        blks = [ws.take(128, (wname, idx, k * 128, cm)) for k in range(nk)]
        bk = P.bank()
        P.op('pe', 'matmul',
             [dict(out=C.ps[bk][:, :ncols], lhsT=blks[k][0], rhs=src(k), start=(k == 0), stop=(k == nk - 1))
              for k in range(nk)],
             reads=tuple(src_res) + tuple(set(b[1] for b in blks)), writes=(f"ps{bk}",))
        epi(mi, m, bk)
        ws.commit()


def resid_epi(P, C, j, bias, tmp, colspans, names=None):
    cnt = [0]

    def epi(mi, m, bk):
        for (q0, q1, x0, si) in colspans:
            n = q1 - q0
            if bias is None:
                P.op('dve', 'scalar_tensor_tensor',
                     dict(out=C.x[:, m, x0:x0 + n], in0=C.ps[bk][:, q0:q1], scalar=C.gT[:, j, m, si:si + 1],
                          in1=C.x[:, m, x0:x0 + n], op0=ALU.mult, op1=ALU.add),
                     reads=(f"ps{bk}", 'mod') + xr(m, x0, x0 + n), writes=xr(m, x0, x0 + n))
                continue
            jj = cnt[0] % len(tmp)
            cnt[0] += 1
            rn = names[jj] if names else f"rtmp{jj}"
            P.op('act', 'activation', dict(out=tmp[jj][:, :n], in_=C.ps[bk][:, q0:q1], func=AF.Identity,
                                           bias=bias[:, m:m + 1], scale=1.0),
                 reads=(f"ps{bk}", 'mixc'), writes=(rn,))
            P.op('dve', 'scalar_tensor_tensor',
                 dict(out=C.x[:, m, x0:x0 + n], in0=tmp[jj][:, :n], scalar=C.gT[:, j, m, si:si + 1],
                      in1=C.x[:, m, x0:x0 + n], op0=ALU.mult, op1=ALU.add),
                 reads=(rn, 'mod') + xr(m, x0, x0 + n), writes=xr(m, x0, x0 + n))
    return epi

TILES_C = [(192, 448), (448, 704), (704, 960), (960, 1216), (1216, 1344)]


def ext_layout(c0, c1, hl):
    out = []
    e = 0
    for (s0, s1, si) in spans_of(c0, c1):
        out.append((e, s0, s1 - s0, si))
        e += hl + (s1 - s0)
    return out, e


def conv_mixer(P, C, ws, ins, outs):
    nc = P.nc
    HL = 30
    with ExitStack() as es:
        S = lambda name, shape, dt: es.enter_context(SBT(nc, name, shape, dt))
        cvt = S("cvt", [128, CV.n], F32)
        E = S("cvE", [128, 16, 544], BF16)
        halo = S("cvhalo", [128, 16, HL], BF16)
        co32 = S("cvo32", [128, 16, 3, HL], F32)
        tf = [S(f"cvtf{i}", [128, 512], F32) for i in range(3)]
        sqb = [S(f"cvsq{i}", [128, 512], BF16) for i in range(2)]
        rt = [S(f"cvrt{i}", [128, 512], F32) for i in range(2)]
        rtmp = [S(f"cvrtmp{i}", [128, 512], F32) for i in range(2)]
        P.op('sp', 'dma_start', dict(out=cvt[:], in_=ins['cv']), writes=('mixc',), sem='in2', inc=16)
        b1 = vv(cvt, CV, 'b1')
        wdw = vv(cvt, CV, 'wdw', 16)
        bdw, lng, lnb, b2 = (vv(cvt, CV, n) for n in ('bdw', 'lng', 'lnb', 'b2'))
        cmask = vv(cvt, CV, 'coremask')
        ntf = 0
        for ti, (c0, c1) in enumerate(TILES_A):
            w = c1 - c0
            lay, NE = ext_layout(c0, c1, HL)
            NQ = NE - HL
            if ti == 0:
                P.op('dve', 'memset', dict(ap=E[:], constant=0.0), writes=tuple(f"E{c}" for c in range(16)))
            else:
                P.op('act', 'activation', dict(out=E[:, :, 0:HL], in_=halo[:], func=AF.Copy),
                     reads=('cvhalo',), writes=tuple(f"E{c}" for c in range(16)))
            if ti == 2:
                for (e0, x0, n, si) in lay[1:]:
                    P.op('pool', 'dma_start', dict(out=E[:, :, e0:e0 + HL], in_=ins['convc'][:, :, si - 1, :]),
                         writes=tuple(f"E{c}" for c in range(16)), sem='in3', inc=16)
            for c in range(16):
                blka = [ws.take(128, ('conv_w_pw1', (), k * 128, [(c * 128, 128)])) for k in range(16)]
                blkb = [ws.take(128, ('conv_w_pw1', (), k * 128, [(2048 + c * 128, 128)])) for k in range(16)]
                ba, bb = P.bank(), P.bank()
                for blks, bk in ((blka, ba), (blkb, bb)):
                    P.op('pe', 'matmul', [dict(out=C.ps[bk][:, :w], lhsT=blks[k][0], rhs=C.h[:, k, c0:c1],
                                               start=(k == 0), stop=(k == 15)) for k in range(16)],
                         reads=hr_all(c0, c1) + tuple(set(b[1] for b in blks)), writes=(f"ps{bk}",))
                ws.commit()
                jj = ntf % 3
                ntf += 1
                sg = tf[jj]
                P.op('act', 'activation', dict(out=sg[:, :w], in_=C.ps[bb][:, :w], func=AF.Sigmoid,
                                               bias=b1[:, 16 + c:17 + c], scale=1.0),
                     reads=(f"ps{bb}", 'mixc'), writes=(f"cvtf{jj}",))
                for (e0, x0, n, si) in lay:
                    P.op('dve', 'scalar_tensor_tensor',
                         dict(out=E[:, c, e0 + HL:e0 + HL + n], in0=C.ps[ba][:, x0 - c0:x0 - c0 + n],
                              scalar=b1[:, c:c + 1], in1=sg[:, x0 - c0:x0 - c0 + n], op0=ALU.add, op1=ALU.mult),
                         reads=(f"ps{ba}", 'mixc', f"cvtf{jj}"), writes=(f"E{c}",))
                    if ti == 2:
                        P.op('dve', 'scalar_tensor_tensor',
                             dict(out=co32[:, c, si, :], in0=C.ps[ba][:, x0 - c0 + n - HL:x0 - c0 + n],
                                  scalar=b1[:, c:c + 1], in1=sg[:, x0 - c0 + n - HL:x0 - c0 + n],
                                  op0=ALU.add, op1=ALU.mult),
                             reads=(f"ps{ba}", 'mixc', f"cvtf{jj}"), writes=('co32',))
                if ti == 0:
                    P.op('dve', 'tensor_scalar', dict(out=E[:, c, HL:HL + HALO], in0=E[:, c, HL:HL + HALO],
                                                      scalar1=cmask[:, 0:1], scalar2=None, op0=ALU.mult),
                         reads=(f"E{c}", 'mixc'), writes=(f"E{c}",))
            if ti < 2:
                P.op('act', 'activation', dict(out=halo[:], in_=E[:, :, NE - HL:NE], func=AF.Copy),
                     reads=tuple(f"E{c}" for c in range(16)), writes=('cvhalo',))
            s1b, s2b = P.reserve(2)
            for c in range(16):
                jj = ntf % 3
                ntf += 1
                acc = tf[jj]
                P.op('dve', 'tensor_scalar', dict(out=acc[:, :NQ], in0=E[:, c, 0:NQ], scalar1=wdw[:, c, 0:1],
                                                  scalar2=bdw[:, c:c + 1], op0=ALU.mult, op1=ALU.add),
                     reads=(f"E{c}", 'mixc'), writes=(f"cvtf{jj}",))
                for t in range(1, 31):
                    P.op('dve', 'scalar_tensor_tensor',
                         dict(out=acc[:, :NQ], in0=E[:, c, t:t + NQ], scalar=wdw[:, c, t:t + 1], in1=acc[:, :NQ],
                              op0=ALU.mult, op1=ALU.add),
                         reads=(f"E{c}", 'mixc', f"cvtf{jj}"), writes=(f"cvtf{jj}",))
                P.op('act', 'activation', dict(out=E[:, c, 0:NQ], in_=acc[:, :NQ], func=AF.Copy),
                     reads=(f"cvtf{jj}",), writes=(f"E{c}",))
                sj = c % 2
                P.op('act', 'activation', dict(out=sqb[sj][:, :NQ], in_=acc[:, :NQ], func=AF.Square),
                     reads=(f"cvtf{jj}",), writes=(f"cvsq{sj}",))
                P.op('pe', 'matmul', dict(out=C.ps[s1b][:, :NQ], lhsT=C.ones[:], rhs=E[:, c, 0:NQ],
                                          start=(c == 0), stop=(c == 15)),
                     reads=(f"E{c}", 'ones'), writes=(f"ps{s1b}",))
                P.op('pe', 'matmul', dict(out=C.ps[s2b][:, :NQ], lhsT=C.ones[:], rhs=sqb[sj][:, :NQ],
                                          start=(c == 0), stop=(c == 15)),
                     reads=(f"cvsq{sj}", 'ones'), writes=(f"ps{s2b}",))
            mean, rstd = rt
            P.op('act', 'activation', dict(out=mean[:, :NQ], in_=C.ps[s1b][:, :NQ], func=AF.Copy, scale=1.0 / 2048),
                 reads=(f"ps{s1b}",), writes=('cvmean',))
            msq = tf[ntf % 3]
            mj = ntf % 3
            ntf += 1
            P.op('dve', 'tensor_tensor', dict(out=msq[:, :NQ], in0=mean[:, :NQ], in1=mean[:, :NQ], op=ALU.mult),
                 reads=('cvmean',), writes=(f"cvtf{mj}",))
            P.op('dve', 'scalar_tensor_tensor', dict(out=rstd[:, :NQ], in0=C.ps[s2b][:, :NQ], scalar=1.0 / 2048,
                                                     in1=msq[:, :NQ], op0=ALU.mult, op1=ALU.subtract),
                 reads=(f"ps{s2b}", f"cvtf{mj}"), writes=('cvrstd',))
            P.op('act', 'activation', dict(out=rstd[:, :NQ], in_=rstd[:, :NQ], func=AF.Sqrt, bias=C.eps[:, 0:1],
                                           scale=1.0), reads=('cvrstd', 'eps'), writes=('cvrstd',))
            P.op('dve', 'reciprocal', dict(out=rstd[:, :NQ], in_=rstd[:, :NQ]), reads=('cvrstd',), writes=('cvrstd',))
            P.unreserve([s1b, s2b])
            for c in range(16):
                jj = ntf % 3
                ntf += 1
                t1 = tf[jj]
                P.op('dve', 'tensor_tensor', dict(out=t1[:, :NQ], in0=E[:, c, 0:NQ], in1=mean[:, :NQ],
                                                  op=ALU.subtract),
                     reads=(f"E{c}", 'cvmean'), writes=(f"cvtf{jj}",))
                P.op('dve', 'tensor_tensor', dict(out=t1[:, :NQ], in0=t1[:, :NQ], in1=rstd[:, :NQ], op=ALU.mult),
                     reads=(f"cvtf{jj}", 'cvrstd'), writes=(f"cvtf{jj}",))
                P.op('act', 'activation', dict(out=E[:, c, 0:NQ], in_=t1[:, :NQ], func=AF.Silu,
                                               bias=lnb[:, c:c + 1], scale=lng[:, c:c + 1]),
                     reads=(f"cvtf{jj}", 'mixc'), writes=(f"E{c}",))
            linear_fm(P, C, ws, 'conv_w_pw2', (), 16, range(16), lambda k: E[:, k, 0:NQ],
                      [f"E{c}" for c in range(16)], NQ,
                      resid_epi(P, C, 1, b2, rtmp, [(e0, e0 + n, x0, si) for (e0, x0, n, si) in lay]))
        P.op('sp', 'dma_start', dict(out=outs['convo'], in_=co32[:]), reads=('co32',), sem='out', inc=16)
        P.fence()


def sqb_f32(rt, tf):
    return [tf[0], tf[1]]


def pool_mixer(P, C, ws, ins, outs):
    nc = P.nc
    HL = 15
    with ExitStack() as es:
        S = lambda name, shape, dt: es.enter_context(SBT(nc, name, shape, dt))
        plt = S("plt", [128, PL.n], F32)
        Pb = S("plPb", [128, 16, 480], BF16)
        Ec = [S(f"plE{i}", [128, 512], F32) for i in range(2)]
        B = [S(f"plB{i}", [128, 512], F32) for i in range(2)]
        halo = S("plhalo", [128, 16, HL], F32)
        po32 = S("plo32", [128, 16, 3, HL], F32)
        cch = S("plcch", [128, 16, 2, HL], F32)
        rtmp = [S(f"plrt{i}", [128, 512], F32) for i in range(2)]
        P.op('sp', 'dma_start', dict(out=plt[:], in_=ins['pl']), writes=('mixc',), sem='in2', inc=16)
        P.op('sp', 'dma_start', dict(out=cch[:], in_=ins['poolc']), writes=('plcch',), sem='in3', inc=16)
        scale = vv(plt, PL, 'scale')
        cmask = vv(plt, PL, 'coremask')
        pcorr = vv(plt, PL, 'pcorr', 4)
        for i in range(2):
            P.op('dve', 'memset', dict(ap=B[i][:], constant=0.0), writes=(f"plB{i}",))
            P.op('dve', 'memset', dict(ap=Ec[i][:], constant=0.0), writes=(f"plE{i}",))
        for ti, (c0, c1) in enumerate(TILES_A):
            w = c1 - c0
            lay, NE = ext_layout(c0, c1, HL)
            NQ = NE - HL
            for c in range(16):
                gi = c // 4
                win = 2 << gi
                ej = c % 2
                e = Ec[ej]
                blks = [ws.take(128, ('pool_w_in', (), k * 128, [(c * 128, 128)])) for k in range(16)]
                bk = P.bank()
                P.op('pe', 'matmul', [dict(out=C.ps[bk][:, :w], lhsT=blks[k][0], rhs=C.h[:, k, c0:c1],
                                           start=(k == 0), stop=(k == 15)) for k in range(16)],
                     reads=hr_all(c0, c1) + tuple(set(b[1] for b in blks)), writes=(f"ps{bk}",))
                ws.commit()
                for (e0, x0, n, si) in lay:
                    P.op('act', 'activation', dict(out=e[:, e0 + HL:e0 + HL + n], in_=C.ps[bk][:, x0 - c0:x0 - c0 + n],
                                                   func=AF.Copy), reads=(f"ps{bk}",), writes=(f"plE{ej}",))
                    if si == 0:
                        if ti == 0:
                            P.op('dve', 'memset', dict(ap=e[:, 0:HL], constant=0.0), writes=(f"plE{ej}",))
                            P.op('dve', 'tensor_scalar', dict(out=e[:, HL:HL + HALO], in0=e[:, HL:HL + HALO],
                                                              scalar1=cmask[:, 0:1], scalar2=None, op0=ALU.mult),
                                 reads=(f"plE{ej}", 'mixc'), writes=(f"plE{ej}",))
                        else:
                            P.op('act', 'activation', dict(out=e[:, 0:HL], in_=halo[:, c, :], func=AF.Copy),
                                 reads=('plhalo%d' % c,), writes=(f"plE{ej}",))
                    else:
                        P.op('act', 'activation', dict(out=e[:, e0:e0 + HL], in_=cch[:, c, si - 1, :], func=AF.Copy),
                             reads=('plcch',), writes=(f"plE{ej}",))
                    if ti == 2:
                        P.op('act', 'activation', dict(out=po32[:, c, si, :], in_=e[:, e0 + n:e0 + n + HL],
                                                       func=AF.Copy), reads=(f"plE{ej}",), writes=('po32',))
                if ti < 2:
                    P.op('act', 'activation', dict(out=halo[:, c, :], in_=e[:, NE - HL:NE], func=AF.Copy),
                         reads=(f"plE{ej}",), writes=('plhalo%d' % c,))
                cur, curr = e, f"plE{ej}"
                d = 1
                pj = 0
                while d < win:
                    nb = B[pj]
                    P.op('dve', 'tensor_tensor', dict(out=nb[:, d:NE], in0=cur[:, d:NE], in1=cur[:, 0:NE - d],
                                                      op=ALU.add), reads=(curr,), writes=(f"plB{pj}",))
                    cur, curr = nb, f"plB{pj}"
                    pj ^= 1
                    d *= 2
                if ti == 0:
                    q0 = HL + HALO
                    P.op('dve', 'tensor_tensor', dict(out=cur[:, q0:q0 + 16], in0=cur[:, q0:q0 + 16],
                                                      in1=pcorr[:, gi, :], op=ALU.mult),
                         reads=(curr, 'mixc'), writes=(curr,))
                P.op('dve', 'scalar_tensor_tensor', dict(out=Pb[:, c, 0:NQ], in0=cur[:, HL:NE], scalar=1.0 / win,
                                                         in1=e[:, HL:NE], op0=ALU.mult, op1=ALU.subtract),
                     reads=(curr, f"plE{ej}"), writes=(f"Pb{c}",))
            for gi in range(4):
                def epi(mi, m, bk, gi=gi):
                    cc = 4 * gi + m
                    for (e0, x0, n, si) in lay:
                        P.op('act', 'activation', dict(out=C.h[:, cc, x0:x0 + n], in_=C.ps[bk][:, e0:e0 + n],
                                                       func=AF.Identity, bias=0.0, scale=scale[:, cc:cc + 1]),
                             reads=(f"ps{bk}", 'mixc'), writes=xr(cc, x0, x0 + n, 'h'))
                linear_fm(P, C, ws, 'pool_w_grp', (gi,), 4, range(4), lambda k, gi=gi: Pb[:, 4 * gi + k, 0:NQ],
                          [f"Pb{4 * gi + k}" for k in range(4)], NQ, epi)
            linear_fm(P, C, ws, 'pool_w_out', (), 16, range(16), lambda k: C.h[:, k, c0:c1], hr_all(c0, c1), w,
                      resid_epi(P, C, 1, None, rtmp, [(s0 - c0, s1 - c0, s0, si) for (s0, s1, si) in spans_of(c0, c1)]))
        P.op('sp', 'dma_start', dict(out=outs['poolo'], in_=po32[:]), reads=('po32',), sem='out', inc=16)
        P.fence()

def swa_mixer(P, C, ws, ins, outs):
    nc = P.nc
    import os
    SK = os.environ.get('SWA_SKIP', '').split(',')
    with ExitStack() as es:
        S = lambda name, shape, dt: es.enter_context(SBT(nc, name, shape, dt))
        att = S("att", [128, AT.n], F32)
        sink8 = S("sink8", [128, 32], F32)
        kTd = S("kTd", [128, 4, 384], BF16)
        Vt = S("Vt", [64, 6, 4, 128], BF16)
        o = S("att_o", [128, 16, 256], BF16)
        qz = [S(f"qz{i}", [128, 256], BF16) for i in range(4)]
        tt = [S(f"att_t{i}", [64, 196], F32) for i in range(2)]
        ee = [S(f"att_e{i}", [64, 196], F32) for i in range(2)]
        pT = [S(f"att_pT{i}", [64, 192], BF16) for i in range(2)]
        st = [S(f"att_st{i}", [64, 4], F32) for i in range(2)]
        kvo = [S(f"att_kvo{i}", [64, 512], F32) for i in range(2)]
        rtmp = [S(f"att_rt{i}", [128, 256], F32) for i in range(2)]
        P.op('sp', 'dma_start', dict(out=att[:], in_=ins['at']), writes=('mixc',), sem='in2', inc=16)
        sinks = vv(att, AT, 'sinks')
        dist = vv(att, AT, 'dist')
        amask = vv(att, AT, 'amask', 2)
        ident = vv(att, AT, 'ident')
        P.op('act', 'activation', dict(out=sink8[:], in_=sinks, func=AF.Copy, scale=1.0 / ATT_SCALE),
             reads=('mixc',), writes=('sink8',))
        for i in range(4):
            P.op('dve', 'memset', dict(ap=qz[i][:], constant=0.0), writes=(f"qz{i}",))
        slopes = [2.0 ** (-8.0 * (hd + 1) / 32.0) for hd in range(32)]
        nq = 0
        nst = 0
        nkvo = 0
        for ti, (c0, c1) in enumerate(TILES_C):
            w = c1 - c0
            sample = (ti == 4)
            if not sample:
                kc0, kc1 = c0 - 128, c1
                kdst = [(0, 0, 384)]
            else:
                kc0, kc1 = c0, c1
                kdst = [(0, 128, 64), (64, 320, 64)]
                for s in range(2):
                    if 'kcdma' in SK:
                        continue
                    P.op('pool', 'dma_start', dict(out=kTd[:, :, s * 192:s * 192 + 128], in_=ins['kcT'][:, s]),
                         writes=tuple(f"kTd{kv}" for kv in range(4)), sem='in3', inc=16)
            nkc = kc1 - kc0

            def kepi(mi, m, bk):
                for (pc, kc, n) in kdst:
                    P.op('act', 'activation', dict(out=kTd[:, m, kc:kc + n], in_=C.ps[bk][:, pc:pc + n], func=AF.Copy),
                         reads=(f"ps{bk}",), writes=(f"kTd{m}",))
            if 'kt' not in SK:
              linear_fm(P, C, ws, 'swa_wk', (), 16, range(4), lambda k: C.h[:, k, kc0:kc1], hr_all(kc0, kc1), nkc, kepi,
                      colmap=lambda m: [(m * 64, 64), (m * 64, 64)])
            if not sample:
                blocks = [(b, kc0 + 64 * b) for b in range(6)]
            else:
                blocks = [(2, 1216), (5, 1280)]
                for s in range(2):
                    for bb in range(2):
                        src = ins['vc'][s, bb * 64:(bb + 1) * 64, :].rearrange("p (k d) -> p k d", k=4)
                        for half in range(2):
                            if 'vcdma' in SK:
                                continue
                            P.op('pool', 'dma_start', dict(out=Vt[:, 3 * s + bb, :, half * 64:(half + 1) * 64], in_=src),
                                 writes=(f"Vt{3 * s + bb}",), sem='in3', inc=16)
            if 'kv' in SK:
                blocks = []
            kvblks = [ws.take(512, ('swa_wkv', (), k * 128, [(0, 512)])) for k in range(16)] if blocks else []
            for i, (b, xc) in enumerate(blocks):
                bk = P.bank()
                P.op('pe', 'matmul', [dict(out=C.ps[bk][0:64, 0:512], lhsT=C.h[:, k, xc:xc + 64], rhs=kvblks[k][0],
                                           start=(k == 0), stop=(k == 15)) for k in range(16)],
                     reads=hr_all(xc, xc + 64) + tuple(set(bb_[1] for bb_ in kvblks)), writes=(f"ps{bk}",))
                vsrc = C.ps[bk][0:64, 256:512].rearrange("p (k d) -> p k d", k=4)
                for half in range(2):
                    if 'kvepi' in SK:
                        continue
                    P.op('act', 'activation', dict(out=Vt[:, b, :, half * 64:(half + 1) * 64], in_=vsrc, func=AF.Copy),
                         reads=(f"ps{bk}",), writes=(f"Vt{b}",))
                orow = None
                if sample:
                    orow = (1 + i, 64)
                elif xc >= 1088 and ti == 3:
                    orow = (0, xc - 1088)
                if orow is not None and 'kvout' not in SK:
                    jj = nkvo % 2
                    nkvo += 1
                    P.op('act', 'activation', dict(out=kvo[jj][:], in_=C.ps[bk][0:64, 0:512], func=AF.Copy),
                         reads=(f"ps{bk}",), writes=(f"kvo{jj}",))
                    P.op('sp', 'dma_start', dict(out=outs['ko'][orow[0], orow[1]:orow[1] + 64, :], in_=kvo[jj][:, 0:256]),
                         reads=(f"kvo{jj}",), sem='out', inc=16)
                    P.op('sp', 'dma_start', dict(out=outs['vo'][orow[0], orow[1]:orow[1] + 64, :], in_=kvo[jj][:, 256:512]),
                         reads=(f"kvo{jj}",), sem='out', inc=16)
            ws.commit()
            nch = w // 64
            for m in range(16 if 'q' not in SK else 0):
                qa, qb = qz[(nq % 2) * 2], qz[(nq % 2) * 2 + 1]
                qra, qrb = f"qz{(nq % 2) * 2}", f"qz{(nq % 2) * 2 + 1}"
                nq += 1

                def qepi(mi, mm, bk, qa=qa, qb=qb, qra=qra, qrb=qrb):
                    P.op('act', 'activation', dict(out=qa[0:64, :w], in_=C.ps[bk][0:64, :w], func=AF.Copy),
                         reads=(f"ps{bk}",), writes=(qra,))
                    P.op('act', 'activation', dict(out=qb[64:128, :w], in_=C.ps[bk][64:128, :w], func=AF.Copy),
                         reads=(f"ps{bk}",), writes=(qrb,))
                linear_fm(P, C, ws, 'swa_wq', (), 16, [m], lambda k: C.h[:, k, c0:c1], hr_all(c0, c1), w, qepi)
                for hh in range(2):
                    if 'attn' in SK:
                        continue
                    head = 2 * m + hh
                    kv = head // 8
                    qt, qr = (qa, qra) if hh == 0 else (qb, qrb)
                    cslope = -slopes[head] / ATT_SCALE
                    for n in range(nch):
                        if not sample:
                            koff = 64 * n
                        else:
                            koff = 192 * n
                        kb = koff // 64
                        bs_ = P.bank()
                        P.op('pe', 'matmul', dict(out=C.ps[bs_][0:64, 0:192], lhsT=qt[:, 64 * n:64 * n + 64],
                                                  rhs=kTd[:, kv, koff:koff + 192], start=True, stop=True),
                             reads=(qr, f"kTd{kv}"), writes=(f"ps{bs_}",))
                        jj = nst % 2
                        nst += 1
                        t, e, p_, s_ = tt[jj], ee[jj], pT[jj], st[jj]
                        P.op('dve', 'scalar_tensor_tensor', dict(out=t[:, 0:192], in0=dist[0:64, :], scalar=cslope,
                                                                 in1=C.ps[bs_][0:64, 0:192], op0=ALU.mult, op1=ALU.add),
                             reads=('mixc', f"ps{bs_}"), writes=(f"att_t{jj}",))
                        if ti == 0 and n < 2:
                            P.op('dve', 'tensor_tensor', dict(out=t[:, 0:192], in0=t[:, 0:192], in1=amask[0:64, n, :],
                                                              op=ALU.add),
                                 reads=('mixc', f"att_t{jj}"), writes=(f"att_t{jj}",))
                        P.op('act', 'activation', dict(out=t[:, 192:193], in_=sink8[0:64, head:head + 1], func=AF.Copy),
                             reads=('sink8', f"att_t{jj}"), writes=(f"att_t{jj}",))
                        P.op('dve', 'reduce_max', dict(out=s_[:, 0:1], in_=t[:, 0:193], axis=AX.X),
                             reads=(f"att_t{jj}",), writes=(f"att_st{jj}",))
                        P.op('dve', 'tensor_scalar', dict(out=s_[:, 1:2], in0=s_[:, 0:1], scalar1=-ATT_SCALE, scalar2=None,
                                                          op0=ALU.mult),
                             reads=(f"att_st{jj}",), writes=(f"att_st{jj}",))
                        P.op('act', 'activation', dict(out=e[:, 0:193], in_=t[:, 0:193], func=AF.Exp, bias=s_[:, 1:2],
                                                       scale=ATT_SCALE, accum_out=s_[:, 2:3]),
                             reads=(f"att_t{jj}", f"att_st{jj}"), writes=(f"att_e{jj}", f"att_st{jj}"))
                        P.op('dve', 'reciprocal', dict(out=s_[:, 3:4], in_=s_[:, 2:3]),
                             reads=(f"att_st{jj}",), writes=(f"att_st{jj}",))
                        P.op('dve', 'tensor_scalar', dict(out=e[:, 0:192], in0=e[:, 0:192], scalar1=s_[:, 3:4],
                                                          scalar2=None, op0=ALU.mult),
                             reads=(f"att_e{jj}", f"att_st{jj}"), writes=(f"att_e{jj}",))
                        bt = P.bank()
                        P.op('pe', 'transpose', [dict(out=C.ps[bt][0:64, 64 * j:64 * j + 64], in_=e[:, 64 * j:64 * j + 64],
                                                      identity=ident[0:64, 0:64]) for j in range(3)],
                             reads=(f"att_e{jj}", 'mixc'), writes=(f"ps{bt}",))
                        P.op('act', 'activation', dict(out=p_[:, :], in_=C.ps[bt][0:64, 0:192], func=AF.Copy),
                             reads=(f"ps{bt}",), writes=(f"att_pT{jj}",))
                        bo = P.bank()
                        P.op('pe', 'matmul', [dict(out=C.ps[bo][:, 0:64], lhsT=Vt[:, kb + j, kv, :],
                                                   rhs=p_[:, 64 * j:64 * j + 64], start=(j == 0), stop=(j == 2))
                                              for j in range(3)],
                             reads=(f"att_pT{jj}",) + tuple(f"Vt{kb + j}" for j in range(3)), writes=(f"ps{bo}",))
                        lo = 64 * hh
                        P.op('act', 'activation', dict(out=o[lo:lo + 64, m, 64 * n:64 * n + 64],
                                                       in_=C.ps[bo][lo:lo + 64, 0:64], func=AF.Copy),
                             reads=(f"ps{bo}",), writes=(f"o{m}",))
            if 'wo' not in SK:
              linear_fm(P, C, ws, 'swa_wo', (), 16, range(16), lambda k: o[:, k, 0:w], [f"o{k}" for k in range(16)], w,
                      resid_epi(P, C, 1, None, rtmp, [(s0 - c0, s1 - c0, s0, si) for (s0, s1, si) in spans_of(c0, c1)]))
        for s in range(2):
            if 'd2d' in SK:
                continue
            P.op('sp', 'dma_start', dict(out=outs['ko'][1 + s, 0:64, :], in_=ins['kc'][s, 64:128, :]), sem='out', inc=16)
            P.op('sp', 'dma_start', dict(out=outs['vo'][1 + s, 0:64, :], in_=ins['vc'][s, 64:128, :]), sem='out', inc=16)
        P.fence()

def gelu_tanh(P, dst, src, n, tmps, tnames, src_res, dst_res, bias=None, bias_res=()):
    xx, a, b = tmps
    rx, ra, rb = tnames
    pr = src.shape[0]
    if bias is not None:
        if isinstance(bias, tuple):
            P.op('act', 'activation', dict(out=xx[:pr, :n], in_=src, func=AF.Identity, bias=bias[1], scale=1.0),
                 reads=tuple(src_res) + tuple(bias_res), writes=(rx,))
        else:
            P.op('dve', 'tensor_tensor', dict(out=xx[:pr, :n], in0=src, in1=bias, op=ALU.add),
                 reads=tuple(src_res) + tuple(bias_res), writes=(rx,))
    else:
        P.op('act', 'activation', dict(out=xx[:pr, :n], in_=src, func=AF.Copy), reads=tuple(src_res), writes=(rx,))
    P.op('dve', 'tensor_tensor', dict(out=a[:pr, :n], in0=xx[:pr, :n], in1=xx[:pr, :n], op=ALU.mult), reads=(rx,), writes=(ra,))
    P.op('dve', 'tensor_scalar', dict(out=a[:pr, :n], in0=a[:pr, :n], scalar1=0.044715, scalar2=1.0, op0=ALU.mult, op1=ALU.add),
         reads=(ra,), writes=(ra,))
    P.op('dve', 'tensor_tensor', dict(out=a[:pr, :n], in0=a[:pr, :n], in1=xx[:pr, :n], op=ALU.mult), reads=(ra, rx), writes=(ra,))
    P.op('act', 'activation', dict(out=b[:pr, :n], in_=a[:pr, :n], func=AF.Sigmoid, scale=1.5957691216057308),
         reads=(ra,), writes=(rb,))
    P.op('dve', 'tensor_tensor', dict(out=dst, in0=xx[:pr, :n], in1=b[:pr, :n], op=ALU.mult), reads=(rx, rb), writes=tuple(dst_res))


def gmlp_mixer(P, C, ws, ins, outs):
    nc = P.nc
    with ExitStack() as es:
        S = lambda name, shape, dt: es.enter_context(SBT(nc, name, shape, dt))
        gmt = S("gmt", [128, 176], F32)
        wsb = S("gm_ws", [128, 8, 128], BF16)
        bsx = S("gm_bs", [128, 8, 128], F32)
        vg = S("gm_vg", [128, 2, 4096], BF16)
        ug = S("gm_ug", [128, 4, 256], BF16)
        um = S("gm_um", [128, 4, 256], BF16)
        bc = S("gm_bc", [128, 3, 512], F32)
        tm = [S(f"gm_t{i}", [128, 512], F32) for i in range(4)]
        stt = S("gm_st", [128, 2, 24], F32)
        P.op('sp', 'dma_start', dict(out=gmt[:], in_=ins['gm']), writes=('mixc',), sem='in2', inc=16)
        bin_u = gmt[:, 0:32]
        bout = gmt[:, 32:48]
        gmask = gmt[:, 48:176].unsqueeze(1).to_broadcast([128, 4, 128])
        P.new_sem('bc')
        for ti, (c0, c1) in enumerate(TILES_C):
            w = c1 - c0
            sample = (ti == 4)
            ntc = 1 if sample else 2
            if ti == 0 or sample:
                kind = 1 if sample else 0
                P.op('sp', 'dma_start', dict(out=tm[3][:, 0:512], in_=ins['gmws'][kind, :, 0:512]), writes=('gm_t3',), sem='in2', inc=16)
                if not sample:
                    P.op('dve', 'tensor_tensor', dict(out=tm[3][:, 0:512].rearrange("p (g t) -> p g t", g=4),
                                                      in0=tm[3][:, 0:512].rearrange("p (g t) -> p g t", g=4), in1=gmask, op=ALU.mult),
                         reads=('gm_t3', 'mixc'), writes=('gm_t3',))
                P.op('act', 'activation', dict(out=wsb[:, 0:4, :], in_=tm[3][:, 0:512].rearrange("p (g t) -> p g t", g=4), func=AF.Copy),
                     reads=('gm_t3',), writes=('gm_ws',))
                P.op('sp', 'dma_start', dict(out=tm[3][:, 0:512], in_=ins['gmws'][kind, :, 512:1024]), reads=(), writes=('gm_t3',), sem='in2', inc=16)
                if not sample:
                    P.op('dve', 'tensor_tensor', dict(out=tm[3][:, 0:512].rearrange("p (g t) -> p g t", g=4),
                                                      in0=tm[3][:, 0:512].rearrange("p (g t) -> p g t", g=4), in1=gmask, op=ALU.mult),
                         reads=('gm_t3', 'mixc'), writes=('gm_t3',))
                P.op('act', 'activation', dict(out=wsb[:, 4:8, :], in_=tm[3][:, 0:512].rearrange("p (g t) -> p g t", g=4), func=AF.Copy),
                     reads=('gm_t3',), writes=('gm_ws',))
                P.op('sp', 'dma_start', dict(out=bsx[:], in_=ins['gmbs'][kind]), writes=('gm_bs',), sem='in2', inc=16)
            for cg in range(8):
                P.op('sp', 'dma_start', dict(out=bc[:], in_=ins['gmbc'][:, cg]), writes=('gm_bc',), sem='bc', inc=16)
                vblks = [ws.take(512, ('gmlp_w_in', (), k * 128, [(4096 + cg * 512, 512)])) for k in range(16)]
                bks = []
                for tc in range(ntc):
                    bkv = P.bank()
                    bks.append(bkv)
                    P.op('pe', 'matmul', [dict(out=C.ps[bkv][:, 0:512], lhsT=C.h[:, k, c0 + 128 * tc:c0 + 128 * tc + 128],
                                               rhs=vblks[k][0], start=(k == 0), stop=(k == 15)) for k in range(16)],
                         reads=hr_all(c0, c1) + tuple(set(b_[1] for b_ in vblks)), writes=(f"ps{bkv}",))
                ws.commit()
                for tc in range(ntc):
                    gelu_tanh(P, tm[3][:, 0:512], C.ps[bks[tc]][:, 0:512], 512, tm[0:3], ('gm_t0', 'gm_t1', 'gm_t2'),
                              (f"ps{bks[tc]}",), ('gm_t3',), bias=bc[:, 0, :], bias_res=('gm_bc',))
                    P.op('act', 'activation', dict(out=vg[:, tc, cg * 512:(cg + 1) * 512], in_=tm[3][:, 0:512], func=AF.Copy,
                                                   accum_out=stt[:, tc, cg:cg + 1]),
                         reads=('gm_t3',), writes=(f"vg{tc}", 'gm_st'))
                    P.op('act', 'activation', dict(out=tm[0][:, 0:512], in_=tm[3][:, 0:512], func=AF.Square,
                                                   accum_out=stt[:, tc, 8 + cg:9 + cg]),
                         reads=('gm_t3',), writes=('gm_t0', 'gm_st'))
            for tc in range(ntc):
                s = stt[:, tc]
                rr = ('gm_st',)
                P.op('dve', 'reduce_sum', dict(out=s[:, 16:17], in_=s[:, 0:8], axis=AX.X), reads=rr, writes=rr)
                P.op('dve', 'reduce_sum', dict(out=s[:, 17:18], in_=s[:, 8:16], axis=AX.X), reads=rr, writes=rr)
                P.op('dve', 'tensor_scalar', dict(out=s[:, 16:18], in0=s[:, 16:18], scalar1=1.0 / 4096, scalar2=None, op0=ALU.mult),
                     reads=rr, writes=rr)
                P.op('dve', 'tensor_tensor', dict(out=s[:, 18:19], in0=s[:, 16:17], in1=s[:, 16:17], op=ALU.mult), reads=rr, writes=rr)
                P.op('dve', 'tensor_tensor', dict(out=s[:, 19:20], in0=s[:, 17:18], in1=s[:, 18:19], op=ALU.subtract), reads=rr, writes=rr)
                P.op('act', 'activation', dict(out=s[:, 20:21], in_=s[:, 19:20], func=AF.Sqrt, bias=C.eps[:, 0:1], scale=1.0),
                     reads=rr + ('eps',), writes=rr)
                P.op('dve', 'reciprocal', dict(out=s[:, 21:22], in_=s[:, 20:21]), reads=rr, writes=rr)
            for cg in range(8):
                P.op('sp', 'dma_start', dict(out=bc[:], in_=ins['gmbc'][:, cg]), writes=('gm_bc',), sem='bc', inc=16)
                for tc in range(ntc):
                    s = stt[:, tc]
                    vsl = vg[:, tc, cg * 512:(cg + 1) * 512]
                    P.op('dve', 'tensor_scalar', dict(out=tm[0][:, 0:512], in0=vsl, scalar1=s[:, 16:17], scalar2=s[:, 21:22],
                                                      op0=ALU.subtract, op1=ALU.mult),
                         reads=(f"vg{tc}", 'gm_st'), writes=('gm_t0',))
                    P.op('dve', 'tensor_tensor', dict(out=tm[0][:, 0:512], in0=tm[0][:, 0:512], in1=bc[:, 1, :], op=ALU.mult),
                         reads=('gm_t0', 'gm_bc'), writes=('gm_t0',))
                    P.op('dve', 'tensor_tensor', dict(out=tm[1][:, 0:512], in0=tm[0][:, 0:512], in1=bc[:, 2, :], op=ALU.add),
                         reads=('gm_t0', 'gm_bc'), writes=('gm_t1',))
                    P.op('act', 'activation', dict(out=vsl, in_=tm[1][:, 0:512], func=AF.Copy), reads=('gm_t1',), writes=(f"vg{tc}",))
                    if sample:
                        P.op('sp', 'dma_start', dict(out=outs['gvo'][:, cg * 512:(cg + 1) * 512], in_=tm[1][:, 0:512]),
                             reads=('gm_t1',), sem='out', inc=16)
            spans = [(s0 - c0, s1 - c0, s0, si) for (s0, s1, si) in spans_of(c0, c1)]
            for g in range(8):
                def uepi(mi, m, bk):
                    gelu_tanh(P, ug[:, mi, 0:w], C.ps[bk][:, 0:w], w, tm[0:3], ('gm_t0', 'gm_t1', 'gm_t2'),
                              (f"ps{bk}",), (f"ug{mi}",), bias=('pp', bin_u[:, m:m + 1]), bias_res=('mixc',))
                linear_fm(P, C, ws, 'gmlp_w_in', (), 16, [4 * g + i for i in range(4)], lambda k: C.h[:, k, c0:c1],
                          hr_all(c0, c1), w, uepi)
                for cc in range(4):
                    f0 = g * 512 + cc * 128
                    for tc in range(ntc):
                        bk = P.bank()
                        P.op('pe', 'matmul', dict(out=C.ps[bk][:, 0:128], lhsT=vg[:, tc, f0:f0 + 128], rhs=wsb[:, g, :],
                                                  start=True, stop=True),
                             reads=(f"vg{tc}", 'gm_ws'), writes=(f"ps{bk}",))
                        P.op('dve', 'tensor_tensor', dict(out=tm[3][:, 0:128], in0=C.ps[bk][:, 0:128], in1=bsx[:, g, :], op=ALU.add),
                             reads=(f"ps{bk}", 'gm_bs'), writes=('gm_t3',))
                        P.op('dve', 'tensor_tensor', dict(out=um[:, cc, 128 * tc:128 * tc + 128], in0=tm[3][:, 0:128],
                                                          in1=ug[:, cc, 128 * tc:128 * tc + 128], op=ALU.mult),
                             reads=('gm_t3', f"ug{cc}"), writes=(f"um{cc}",))
                colmap = lambda m: [(m * 128, 128)]
                for m in range(16):
                    blks = [ws.take(128, ('gmlp_w_out', (), g * 512 + cc * 128, [(m * 128, 128)])) for cc in range(4)]
                    bk = P.bank()
                    P.op('pe', 'matmul', [dict(out=C.ps[bk][:, :w], lhsT=blks[cc][0], rhs=um[:, cc, 0:w],
                                               start=(cc == 0), stop=(cc == 3)) for cc in range(4)],
                         reads=tuple(f"um{cc}" for cc in range(4)) + tuple(set(b[1] for b in blks)), writes=(f"ps{bk}",))
                    ws.commit()
                    resid_epi(P, C, 1, bout if g == 0 else None, tm[0:2], spans, names=('gm_t0', 'gm_t1'))(m, m, bk)
        P.fence()

IN_SPECS = {
    'xT': [128, 16, T], 'small': [128, SM.n], 'cv': [128, CV.n], 'pl': [128, PL.n], 'at': [128, AT.n],
    'gm': [128, 176], 'convc': [128, 16, 2, 30], 'poolc': [128, 16, 2, 15], 'kcT': [128, 2, 4, 128],
    'kc': [2, 128, 256], 'vc': [2, 128, 256], 'gmws': [2, 128, 1024], 'gmbs': [2, 128, 8, 128],
    'gmbc': [128, 8, 3, 512],
}
OUT_SPECS = {
    'yT': [128, 16, 1152], 'convo': [128, 16, 3, 30], 'poolo': [128, 16, 3, 15],
    'ko': [3, 128, 256], 'vo': [3, 128, 256], 'gvo': [128, 4096],
}


def build(npieces, dry=False, nlayers=DEPTH, stages=None):
    nc = bass.Bass("TRN2", target_bir_lowering=False)
    ins = {k: nc.dram_tensor(k, v, F32, kind="ExternalInput").ap() for k, v in IN_SPECS.items()}
    ins['wstream'] = nc.dram_tensor("wstream", [npieces, 128, 2048], F32, kind="ExternalInput").ap()
    outs = {k: nc.dram_tensor(k, v, F32, kind="ExternalOutput").ap() for k, v in OUT_SPECS.items()}
    with ExitStack() as es:
        P = Prog(nc, es)
        C = Ctx()
        for s in ['in', 'in2', 'in3', 'out']:
            P.new_sem(s)
        setup_common(P, C, ins)
        import os
        P.nbank = int(os.environ.get('BANKOFF', '0'))
        ws = WStream(P, ins['wstream'], npieces, R=RING)
        mixers = [conv_mixer, pool_mixer, swa_mixer, gmlp_mixer]
        import os
        for layer in range(nlayers):
            if not os.environ.get('NOADA'):
                ada(P, C, ws, layer)
            tA = TILES_A if layer < 2 else (TILES_A if layer == 2 else TILES_B)
            st = stages or 'nfmg'
            last = (layer == nlayers - 1)
            if 'n' in st:
                norm_mod(P, C, 0, tA)
            if 'f' in st:
                ffn(P, C, ws, layer, 0, 0, tA)
            if 'n' in st:
                norm_mod(P, C, 1, TILES_A if layer <= 2 else TILES_B)
            import os
            om = os.environ.get('ONLY_MIXER')
            if 'm' in st and (om is None or int(om) == layer):
                mixers[layer](P, C, ws, ins, outs)
            tB = TILES_A if layer < 2 else TILES_B
            if 'g' in st and not (last and stages):
                norm_mod(P, C, 2, tB)
                ffn(P, C, ws, layer, 1, 2, tB)
        norm_mod(P, C, 0, TILES_B, final=outs['yT'])
        P.final_waits = ['out']
        nrec = ws.cur + 1
        if not dry:
            assert nrec == npieces, (nrec, npieces)
            P.emit()
    return nc, ws.rec, nrec


def fm(v):
    v = np.asarray(v, np.float32)
    n = v.shape[-1] // 128
    r = v.reshape(v.shape[:-1] + (n, 128))
    return np.ascontiguousarray(np.moveaxis(r, -1, 0))


def fm_rows(a):
    a = np.asarray(a, np.float32)
    return np.ascontiguousarray(a.T.reshape(16, 128, a.shape[0]).transpose(1, 0, 2))


def pack_stream(rec, npieces, inp):
    stream = np.zeros((npieces, 128, 2048), np.float32)
    cache = {}

    def W(name, idx):
        key = (name, idx)
        if key not in cache:
            if name == 'swa_wkv':
                cache[key] = np.concatenate([inp['swa_wk'], inp['swa_wv']], axis=1)
            else:
                w = inp[name]
                for i in idx:
                    w = w[i]
                cache[key] = w
        return cache[key]
    for (piece, col, ncols, spec) in rec:
        name, idx, r0, segs = spec
        w = W(name, tuple(idx))
        o = col
        for (c0, n) in segs:
            stream[piece, :, o:o + n] = w[r0:r0 + 128, c0:c0 + n]
            o += n
    return stream


_CACHE = {}


def kernel(**inp):
    inp = {k: np.asarray(v) for k, v in inp.items()}
    if 'prog' not in _CACHE:
        _, _, npieces = build(8000, dry=True)
        _CACHE['prog'] = build(npieces)
    nc, rec, npieces = _CACHE['prog']
    in_maps = make_inputs(inp, rec, npieces)
    res = run_bass_kernel_spmd(nc, in_maps, core_ids=list(range(8)))
    return assemble(res.results)


def make_inputs(inp, rec, npieces, cores=range(8)):
    stream = pack_stream(rec, npieces, inp)

    small = np.zeros((128, SM.n), np.float32)
    o, n = SM.off['norm_g']
    small[:, o:o + n] = fm(inp['norm_g']).reshape(128, -1)
    o, n = SM.off['ada_b']
    small[:, o:o + n] = fm(inp['ada_b'].reshape(4, 9, 2048)).reshape(128, -1)
    o, n = SM.off['final_g']
    small[:, o:o + n] = fm(inp['final_g'])
    cv = np.zeros((128, CV.n), np.float32)
    for nm, val in [('b1', fm(inp['conv_b_pw1'])), ('wdw', fm(inp['conv_w_dw']).transpose(0, 2, 1).reshape(128, -1)),
                    ('bdw', fm(inp['conv_b_dw'])), ('lng', fm(inp['conv_ln_g'])), ('lnb', fm(inp['conv_ln_b'])),
                    ('b2', fm(inp['conv_b_pw2']))]:
        o, n = CV.off[nm]
        cv[:, o:o + n] = val.reshape(128, -1)
    pl = np.zeros((128, PL.n), np.float32)
    o, n = PL.off['scale']
    pl[:, o:o + n] = fm(inp['pool_scale'])
    at = np.zeros((128, AT.n), np.float32)
    o, n = AT.off['sinks']
    at[:, o:o + n] = inp['swa_sinks'][None, :]
    o, n = AT.off['dist']
    ii = np.arange(64)[:, None]
    jj = np.arange(192)[None, :]
    at[:64, o:o + n] = np.abs(ii + 128 - jj).astype(np.float32)
    o, n = AT.off['ident']
    at[:, o:o + n] = np.eye(128, dtype=np.float32)
    gm = np.zeros((128, 176), np.float32)
    gm[:, 0:32] = fm(inp['gmlp_b_in'][:4096])
    gm[:, 32:48] = fm(inp['gmlp_b_out'])
    pos = np.arange(128)
    gm[:, 48:176] = ((pos[:, None] // 64) <= (pos[None, :] // 64)).astype(np.float32)
    wsT = np.ascontiguousarray(inp['gmlp_w_s'].transpose(2, 0, 1))
    gmws = np.zeros((2, 128, 8, 128), np.float32)
    gmws[0] = wsT
    gmws[1, :64, :, :64] = wsT[:64, :, :64]
    gmws[1, 64:, :, 64:] = wsT[:64, :, :64]
    gmws = gmws.reshape(2, 128, 1024)
    gmbs = np.zeros((2, 128, 8, 128), np.float32)
    gmbs[0] = inp['gmlp_b_s'][None, :, :]
    gmbs[1, :, :, :64] = inp['gmlp_b_s'][None, :, :64]
    gmbs[1, :, :, 64:] = inp['gmlp_b_s'][None, :, :64]
    gmbc = np.zeros((128, 8, 3, 512), np.float32)
    gmbc[:, :, 0, :] = inp['gmlp_b_in'][4096:].reshape(8, 512)[None]
    gmbc[:, :, 1, :] = inp['gmlp_ln_g'].reshape(8, 512)[None]
    gmbc[:, :, 2, :] = inp['gmlp_ln_b'].reshape(8, 512)[None]

    in_maps = []
    for core in cores:
        b, half = core // 2, core % 2
        sa, sb_ = 2 * core, 2 * core + 1
        xT = np.zeros((128, 16, T), np.float32)
        if half == 0:
            xT[:, :, HALO:NPR] = fm_rows(inp['x_prompt'][b, 0:1024])
        else:
            xT[:, :, 0:NPR] = fm_rows(inp['x_prompt'][b, 1024 - HALO:2048])
        xT[:, :, 1216:1280] = fm_rows(inp['x_sample'][sa])
        xT[:, :, 1280:1344] = fm_rows(inp['x_sample'][sb_])
        sm = small.copy()
        o, n = SM.off['cT']
        sm[:, o:o + n] = fm(np.stack([inp['c_prompt'][b], inp['c_sample'][sa], inp['c_sample'][sb_]])).transpose(0, 2, 1).reshape(128, -1)
        cvc = cv.copy()
        cvc[:, CV.off['coremask'][0]] = float(half)
        plc = pl.copy()
        plc[:, PL.off['coremask'][0]] = float(half)
        o, n = PL.off['pcorr']
        pc = np.ones((4, 16), np.float32)
        if half == 0:
            for gi, wv in enumerate((2, 4, 8, 16)):
                pc[gi] = wv / np.minimum(np.arange(16) + 1, wv)
        plc[:, o:o + n] = pc.reshape(-1)[None]
        atc = at.copy()
        o, n = AT.off['amask']
        am = np.zeros((2, 192), np.float32)
        if half == 0:
            am[0, :128] = -1e9
            am[1, :64] = -1e9
        atc[:, o:o + n] = am.reshape(-1)[None]
        convc = np.stack([fm_rows(inp['cache_conv'][sa]), fm_rows(inp['cache_conv'][sb_])], axis=2)
        poolc = np.stack([fm_rows(inp['cache_pool'][sa]), fm_rows(inp['cache_pool'][sb_])], axis=2)
        kc = np.stack([inp['cache_swa_k'][sa].reshape(128, 256), inp['cache_swa_k'][sb_].reshape(128, 256)])
        vc = np.stack([inp['cache_swa_v'][sa].reshape(128, 256), inp['cache_swa_v'][sb_].reshape(128, 256)])
        kcT = np.zeros((128, 2, 4, 128), np.float32)
        for s, sq in enumerate((sa, sb_)):
            kk = inp['cache_swa_k'][sq].transpose(2, 1, 0)
            kcT[:64, s] = kk
            kcT[64:, s] = kk
        in_maps.append({'xT': xT, 'small': sm, 'cv': cvc, 'pl': plc, 'at': atc, 'gm': gm,
                        'convc': np.ascontiguousarray(convc), 'poolc': np.ascontiguousarray(poolc), 'kcT': kcT,
                        'kc': np.ascontiguousarray(kc, np.float32), 'vc': np.ascontiguousarray(vc, np.float32),
                        'gmws': gmws, 'gmbs': gmbs, 'gmbc': gmbc, 'wstream': stream})
    return in_maps


def assemble(R):
    def tm(a):
        return a.transpose(2, 1, 0).reshape(a.shape[2], 2048)
    y_p = np.zeros((4, 2048, 2048), np.float32)
    y_s = np.zeros((16, 64, 2048), np.float32)
    conv_p = np.zeros((4, 30, 2048), np.float32)
    conv_s = np.zeros((16, 30, 2048), np.float32)
    pool_p = np.zeros((4, 15, 2048), np.float32)
    pool_s = np.zeros((16, 15, 2048), np.float32)
    k_p = np.zeros((4, 128, 4, 64), np.float32)
    v_p = np.zeros((4, 128, 4, 64), np.float32)
    k_s = np.zeros((16, 128, 4, 64), np.float32)
    v_s = np.zeros((16, 128, 4, 64), np.float32)
    g_s = np.zeros((16, 64, 4096), np.float32)
    for core in range(8):
        b, half = core // 2, core % 2
        r = R[core]
        yt = tm(r['yT'])
        y_p[b, half * 1024:(half + 1) * 1024] = yt[0:1024]
        for s in range(2):
            sq = 2 * core + s
            y_s[sq] = yt[1024 + 64 * s:1088 + 64 * s]
            conv_s[sq] = tm(r['convo'][:, :, 1 + s, :])
            pool_s[sq] = tm(r['poolo'][:, :, 1 + s, :])
            k_s[sq] = r['ko'][1 + s].reshape(128, 4, 64)
            v_s[sq] = r['vo'][1 + s].reshape(128, 4, 64)
            g_s[sq] = r['gvo'][64 * s:64 * s + 64]
        if half == 1:
            conv_p[b] = tm(r['convo'][:, :, 0, :])
            pool_p[b] = tm(r['poolo'][:, :, 0, :])
            k_p[b] = r['ko'][0].reshape(128, 4, 64)
            v_p[b] = r['vo'][0].reshape(128, 4, 64)
    return (y_p, y_s, conv_p, conv_s, pool_p, pool_s, k_p, v_p, k_s, v_s, g_s)
```

```python
import numpy as np
import concourse.bass as bass
import concourse.mybir as mybir
from contextlib import ExitStack
from concourse.bass_utils import run_bass_kernel_spmd
import ml_dtypes

F32 = mybir.dt.float32
BF16 = mybir.dt.bfloat16
AF = mybir.ActivationFunctionType
ALU = mybir.AluOpType
AX = mybir.AxisListType

ENGS = ['pe', 'act', 'dve', 'pool', 'sp']


_UNIQ = [0]


def SBT(nc, name, shape, dt):
    _UNIQ[0] += 1
    return nc.sbuf_tensor(f"sb_{name}_{_UNIQ[0]}", shape, dt)


class Prog:
    def __init__(self, nc, es):
        self.nc = nc
        self.es = es
        self.thunks = {e: [] for e in ENGS}
        self.sem = {}
        self.val = {}
        self.waited = {}
        self.res = {}
        self.nbank = 0
        self.ninstr = 0
        self.fence_deps = {}
        self.reserved = set()
        for e in ['pe', 'act', 'dve', 'pool']:
            self.new_sem(e)

    def new_sem(self, name):
        self.sem[name] = self.es.enter_context(self.nc.semaphore(name))
        self.val[name] = 0

    def sb(self, name, shape, dt):
        return self.es.enter_context(SBT(self.nc, name, shape, dt))

    def op(self, eng, meth, kw, reads=(), writes=(), sem=None, inc=1):
        fn = (meth, kw if isinstance(kw, list) else [kw])
        semname = sem or eng
        deps = dict(self.fence_deps)
        for r in reads:
            st = self.res.get(r)
            if st and st[0]:
                s, v = st[0]
                deps[s] = max(deps.get(s, 0), v)
        for w in writes:
            st = self.res.get(w)
            if st:
                if st[0]:
                    s, v = st[0]
                    deps[s] = max(deps.get(s, 0), v)
                for s, v in st[1].items():
                    deps[s] = max(deps.get(s, 0), v)
        waits = []
        for s, v in deps.items():
            if s == 'pe' and eng == 'pe':
                continue
            if self.waited.get((eng, s), 0) < v:
                self.waited[(eng, s)] = v
                waits.append((s, v))
        self.val[semname] += inc
        tok = (semname, self.val[semname])
        for r in reads:
            st = self.res.setdefault(r, [None, {}])
            st[1][tok[0]] = max(st[1].get(tok[0], 0), tok[1])
        for w in writes:
            self.res[w] = [tok, {}]
        self.thunks[eng].append((waits, fn, semname, inc))
        return tok

    def bank(self):
        while True:
            b = self.nbank % 8
            self.nbank += 1
            if b not in self.reserved:
                return b

    def reserve(self, n):
        out = []
        for _ in range(n):
            b = self.bank()
            self.reserved.add(b)
            out.append(b)
        return out

    def unreserve(self, banks):
        for b in banks:
            self.reserved.discard(b)

    def fence(self):
        for s in ['pe', 'act', 'dve', 'out']:
            if s in self.val and self.val[s] > 0:
                self.fence_deps[s] = self.val[s]

    def run_engine(self, eng, e):
        for waits, fn, semname, inc in self.thunks[eng]:
            for s, v in waits:
                e.wait_ge(self.sem[s], v)
            m = getattr(e, fn[0])
            for kw in fn[1]:
                inst = m(**kw)
                self.ninstr += 1
            inst.then_inc(self.sem[semname], inc)

    def emit(self):
        nc = self.nc
        with nc.Block() as block:
            @block.tensor
            def _(e):
                self.run_engine('pe', e)

            @block.scalar
            def _(e):
                self.run_engine('act', e)

            @block.vector
            def _(e):
                self.run_engine('dve', e)

            @block.gpsimd
            def _(e):
                self.run_engine('pool', e)

            @block.sync
            def _(e):
                self.run_engine('sp', e)
                for s in self.final_waits:
                    e.wait_ge(self.sem[s], self.val[s])


class WStream:
    def __init__(self, P, dram, npieces, R=12, eng='pool'):
        self.P = P
        self.dram = dram
        self.np = npieces
        self.R = R
        self.eng = eng
        self.slots = [P.sb(f"wslot{i}", [128, 2048], BF16) for i in range(R)]
        for i in range(R):
            P.new_sem(f"w{i}")
        self.cur = -1
        self.col = 2048
        self.loaded = 0
        self.released = 0
        self.rec = []
        for i in range(min(R, npieces)):
            self._load(i)

    def _load(self, i):
        s = i % self.R
        slot = self.slots[s]
        src = self.dram[i]
        self.P.op(self.eng, 'dma_start', dict(out=slot[:], in_=src),
                  writes=(f"ws{s}",), sem=f"w{s}", inc=16)
        self.loaded = i + 1

    def take(self, ncols, spec):
        if self.col + ncols > 2048:
            assert self.col == 2048, (self.col, ncols)
            self.cur += 1
            assert self.cur < self.np
            self.col = 0
        s = self.cur % self.R
        ap = self.slots[s][:, self.col:self.col + ncols]
        self.rec.append((self.cur, self.col, ncols, spec))
        self.col += ncols
        return ap, f"ws{s}"

    def commit(self):
        upto = self.cur if self.col == 2048 else self.cur - 1
        while self.released <= upto:
            j = self.released
            self.released += 1
            if j + self.R < self.np:
                self._load(j + self.R)

    def skip_to(self, align):
        if self.col % align:
            self.col += align - self.col % align

    def done(self):
        assert self.cur == self.np - 1, (self.cur, self.np, self.col)

T = 1344
NPR = 1216
HALO = 192
SEQ_BOUNDS = [(0, 1216), (1216, 1280), (1280, 1344)]
TILES_A = [(0, 448), (448, 896), (896, 1344)]
TILES_B = [(192, 576), (576, 960), (960, 1344)]
D = 2048
DEPTH = 4
RING = 6
EPS = 1e-6
ATT_SCALE = 0.125


def spans_of(c0, c1):
    out = []
    for si, (s0, s1) in enumerate(SEQ_BOUNDS):
        lo, hi = max(c0, s0), min(c1, s1)
        if lo < hi:
            out.append((lo, hi, si))
    return out


class Layout:
    def __init__(self):
        self.off = {}
        self.n = 0

    def add(self, name, n):
        self.off[name] = (self.n, n)
        self.n += n


SM = Layout()
SM.add('cT', 48)
SM.add('norm_g', 4 * 3 * 16)
SM.add('ada_b', 4 * 144)
SM.add('final_g', 16)
CV = Layout()
for nm, n in [('b1', 32), ('wdw', 16 * 31), ('bdw', 16), ('lng', 16), ('lnb', 16), ('b2', 16), ('coremask', 1)]:
    CV.add(nm, n)
PL = Layout()
for nm, n in [('scale', 16), ('coremask', 1), ('pcorr', 64)]:
    PL.add(nm, n)
AT = Layout()
for nm, n in [('sinks', 32), ('dist', 192), ('amask', 384), ('ident', 128)]:
    AT.add(nm, n)
GM = Layout()
for nm, n in [('bin_u', 32), ('bout', 16), ('gmask', 128), ('wsT', 1024), ('wsTs', 512), ('bs', 1024)]:
    GM.add(nm, n)


class Ctx:
    pass


def vv(tile, lay, name, a=None, b=None):
    o, n = lay.off[name]
    ap = tile[:, o:o + n]
    if a is not None:
        ap = ap.rearrange("p (a b) -> p a b", a=a)
    return ap


def setup_common(P, C, ins):
    nc = P.nc
    C.ps = [P.es.enter_context(nc.psum_tensor(f"ps{i}", [128, 512], F32)) for i in range(8)]
    C.x = P.sb("x", [128, 16, T], F32)
    C.h = P.sb("h", [128, 16, T], BF16)
    C.ones = P.sb("ones_bf", [128, 128], BF16)
    C.eps = P.sb("eps_t", [128, 1], F32)
    C.small = P.sb("small", [128, SM.n], F32)
    C.modT = P.sb("modT", [128, 9, 16, 3], F32)
    C.aT = P.sb("aT", [128, 3, 16, 3], F32)
    C.gT = P.sb("gT", [128, 3, 16, 3], F32)
    C.condb = P.sb("condb", [128, 16, 3], BF16)
    P.op('dve', 'memset', dict(ap=C.ones[:], constant=1.0), writes=('ones',))
    P.op('dve', 'memset', dict(ap=C.eps[:], constant=EPS), writes=('eps',))
    P.op('sp', 'dma_start', dict(out=C.small[:], in_=ins['small']), writes=('small',), sem='in', inc=16)
    xres = []
    for c in range(16):
        P.op('sp', 'dma_start', dict(out=C.x[:, c, :], in_=ins['xT'][:, c, :]), sem='in', inc=16)
        xres += [f"x{c}_{s}" for s in range(len(SEG))]
    tot = ('in', P.val['in'])
    for r in ['small'] + xres:
        P.res[r] = [tot, {}]
    cT = vv(C.small, SM, 'cT', 16)
    P.op('act', 'activation', dict(out=C.condb[:], in_=cT, func=AF.Silu), reads=('small',), writes=('condb',))


SEG = [(0, 192), (192, 448), (448, 576), (576, 704), (704, 896), (896, 960), (960, 1216), (1216, 1280), (1280, 1344)]


def xr(c, c0, c1, pre='x'):
    return tuple(f"{pre}{c}_{i}" for i, (a, b) in enumerate(SEG) if a < c1 and b > c0)


def hr_all(c0, c1):
    out = ()
    for k in range(16):
        out += xr(k, c0, c1, 'h')
    return out


def norm_mod(P, C, j, tiles, final=None):
    x, h = C.x, C.h
    with ExitStack() as es:
        nc = P.nc
        C.rs = es.enter_context(SBT(nc, "rstd", [128, T], F32))
        sq = [es.enter_context(SBT(nc, f"sq{i}", [128, 448], BF16)) for i in range(3)]
        tmp = [es.enter_context(SBT(nc, f"tmpf{i}", [128, 448], F32)) for i in range(3)]
        yo = [es.enter_context(SBT(nc, f"yo{i}", [128, 448], F32)) for i in range(3)] if final else None
        nsq = ntmp = 0
        for ti, (c0, c1) in enumerate(tiles):
            w = c1 - c0
            bk = P.bank()
            for c in range(16):
                jj = nsq % 3
                nsq += 1
                P.op('act', 'activation', dict(out=sq[jj][:, :w], in_=x[:, c, c0:c1], func=AF.Square),
                     reads=xr(c, c0, c1), writes=(f"sq{jj}",))
                P.op('pe', 'matmul', dict(out=C.ps[bk][:, :w], lhsT=C.ones[:], rhs=sq[jj][:, :w],
                                          start=(c == 0), stop=(c == 15)),
                     reads=(f"sq{jj}", 'ones'), writes=(f"ps{bk}",))
            P.op('act', 'activation', dict(out=C.rs[:, c0:c1], in_=C.ps[bk][:, :w], func=AF.Sqrt,
                                           bias=C.eps[:, 0:1], scale=1.0 / 2048.0),
                 reads=(f"ps{bk}", 'eps'), writes=(f"rs{ti}",))
            P.op('dve', 'reciprocal', dict(out=C.rs[:, c0:c1], in_=C.rs[:, c0:c1]),
                 reads=(f"rs{ti}",), writes=(f"rs{ti}",))
            for c in range(16):
                for (s0, s1, si) in spans_of(c0, c1):
                    jj = ntmp % 3
                    ntmp += 1
                    sw = s1 - s0
                    P.op('dve', 'tensor_tensor', dict(out=tmp[jj][:, :sw], in0=x[:, c, s0:s1], in1=C.rs[:, s0:s1],
                                                      op=ALU.mult),
                         reads=xr(c, s0, s1) + (f"rs{ti}",), writes=(f"tmp{jj}",))
                    if final is None:
                        P.op('act', 'activation', dict(out=h[:, c, s0:s1], in_=tmp[jj][:, :sw], func=AF.Identity,
                                                       bias=C.modT[:, 3 * j, c, si:si + 1],
                                                       scale=C.aT[:, j, c, si:si + 1]),
                             reads=(f"tmp{jj}", 'mod'), writes=xr(c, s0, s1, 'h'))
                    else:
                        fg = vv(C.small, SM, 'final_g')
                        P.op('act', 'activation', dict(out=yo[jj][:, :sw], in_=tmp[jj][:, :sw], func=AF.Identity,
                                                       bias=0.0, scale=fg[:, c:c + 1]),
                             reads=(f"tmp{jj}", 'small'), writes=(f"yo{jj}",))
                        P.op('sp', 'dma_start', dict(out=final[:, c, s0 - HALO:s1 - HALO], in_=yo[jj][:, :sw]),
                             reads=(f"yo{jj}",), sem='out', inc=16)
        P.fence()


def ffn(P, C, ws, layer, which, j, tiles):
    G = 4
    x, h = C.x, C.h
    nc = P.nc
    with ExitStack() as es:
        g = es.enter_context(SBT(nc, "ffn_g", [128, G, T], BF16))
        sl = [es.enter_context(SBT(nc, f"ffn_s{i}", [128, 448], F32)) for i in range(2)]
        nsl = 0
        idx = (layer, which)
        for grp in range(11):
            for fl in range(G):
                f = (grp * G + fl) * 128
                blk1 = [ws.take(128, ('ffn_w1', idx, k * 128, [(f, 128)])) for k in range(16)]
                blk3 = [ws.take(128, ('ffn_w3', idx, k * 128, [(f, 128)])) for k in range(16)]
                for ti, (c0, c1) in enumerate(tiles):
                    w = c1 - c0
                    b1 = P.bank()
                    b3 = P.bank()
                    for (blks, bk) in ((blk1, b1), (blk3, b3)):
                        P.op('pe', 'matmul',
                             [dict(out=C.ps[bk][:, :w], lhsT=blks[k][0], rhs=h[:, k, c0:c1],
                                   start=(k == 0), stop=(k == 15)) for k in range(16)],
                             reads=hr_all(c0, c1) + tuple(set(b[1] for b in blks)), writes=(f"ps{bk}",))
                    jj = nsl % 2
                    nsl += 1
                    P.op('act', 'activation', dict(out=sl[jj][:, :w], in_=C.ps[b1][:, :w], func=AF.Silu),
                         reads=(f"ps{b1}",), writes=(f"sl{jj}",))
                    P.op('dve', 'tensor_tensor', dict(out=g[:, fl, c0:c1], in0=C.ps[b3][:, :w], in1=sl[jj][:, :w],
                                                      op=ALU.mult),
                         reads=(f"ps{b3}", f"sl{jj}"), writes=(f"g{fl}_{ti}",))
                ws.commit()
            for m in range(16):
                blks = [ws.take(128, ('ffn_w2', idx, (grp * G + fl) * 128, [(m * 128, 128)])) for fl in range(G)]
                for ti, (c0, c1) in enumerate(tiles):
                    w = c1 - c0
                    bk = P.bank()
                    P.op('pe', 'matmul',
                         [dict(out=C.ps[bk][:, :w], lhsT=blks[fl][0], rhs=g[:, fl, c0:c1],
                               start=(fl == 0), stop=(fl == G - 1)) for fl in range(G)],
                         reads=tuple(f"g{fl}_{ti}" for fl in range(G)) + tuple(set(b[1] for b in blks)),
                         writes=(f"ps{bk}",))
                    for (s0, s1, si) in spans_of(c0, c1):
                        P.op('dve', 'scalar_tensor_tensor',
                             dict(out=x[:, m, s0:s1], in0=C.ps[bk][:, s0 - c0:s1 - c0],
                                  scalar=C.gT[:, j, m, si:si + 1], in1=x[:, m, s0:s1], op0=ALU.mult, op1=ALU.add),
                             reads=(f"ps{bk}", 'mod') + xr(m, s0, s1), writes=xr(m, s0, s1))
                ws.commit()
        P.fence()


def ada(P, C, ws, layer):
    adab = vv(C.small, SM, 'ada_b', 4 * 9)
    ng = vv(C.small, SM, 'norm_g', 12)
    for q in range(9):
        bk = P.bank()
        for c in range(16):
            m = q * 16 + c
            blks = [ws.take(128, ('ada_w', (layer,), k * 128, [(m * 128, 128)])) for k in range(16)]
            P.op('pe', 'matmul',
                 [dict(out=C.ps[bk][:, 3 * c:3 * c + 3], lhsT=blks[k][0], rhs=C.condb[:, k, :],
                       start=(k == 0), stop=(k == 15)) for k in range(16)],
                 reads=('condb',) + tuple(set(b[1] for b in blks)), writes=(f"ps{bk}",))
            ws.commit()
        bb = adab[:, layer * 9 + q, :].unsqueeze(2).to_broadcast([128, 16, 3])
        P.op('dve', 'tensor_tensor', dict(out=C.modT[:, q], in0=C.ps[bk][:, 0:48].rearrange("p (c s) -> p c s", s=3),
                                          in1=bb, op=ALU.add),
             reads=(f"ps{bk}", 'small'), writes=('mod',))
    for j in range(3):
        P.op('dve', 'tensor_scalar', dict(out=C.aT[:, j], in0=C.modT[:, 3 * j + 1], scalar1=1.0, scalar2=None,
                                          op0=ALU.add), reads=('mod',), writes=('mod',))
        gb = ng[:, layer * 3 + j, :].unsqueeze(2).to_broadcast([128, 16, 3])
        P.op('dve', 'tensor_tensor', dict(out=C.aT[:, j], in0=C.aT[:, j], in1=gb, op=ALU.mult),
             reads=('mod', 'small'), writes=('mod',))
        P.op('act', 'activation', dict(out=C.gT[:, j], in_=C.modT[:, 3 * j + 2], func=AF.Copy,
                                       scale=(1.0 if j == 1 else 0.5)), reads=('mod',), writes=('mod',))
    P.fence()


def linear_fm(P, C, ws, wname, idx, nk, ms, src, src_res, ncols, epi, colmap=None):
    for mi, m in enumerate(ms):
        cm = colmap(m) if colmap else [(m * 128, 128)]
        blks = [ws.take(128, (wname, idx, k * 128, cm)) for k in range(nk)]
        bk = P.bank()
        P.op('pe', 'matmul',
             [dict(out=C.ps[bk][:, :ncols], lhsT=blks[k][0], rhs=src(k), start=(k == 0), stop=(k == nk - 1))
              for k in range(nk)],
             reads=tuple(src_res) + tuple(set(b[1] for b in blks)), writes=(f"ps{bk}",))
        epi(mi, m, bk)
        ws.commit()


def resid_epi(P, C, j, bias, tmp, colspans, names=None):
    cnt = [0]

    def epi(mi, m, bk):
        for (q0, q1, x0, si) in colspans:
            n = q1 - q0
            if bias is None:
                P.op('dve', 'scalar_tensor_tensor',
                     dict(out=C.x[:, m, x0:x0 + n], in0=C.ps[bk][:, q0:q1], scalar=C.gT[:, j, m, si:si + 1],
                          in1=C.x[:, m, x0:x0 + n], op0=ALU.mult, op1=ALU.add),
                     reads=(f"ps{bk}", 'mod') + xr(m, x0, x0 + n), writes=xr(m, x0, x0 + n))
                continue
            jj = cnt[0] % len(tmp)
            cnt[0] += 1
            rn = names[jj] if names else f"rtmp{jj}"
            P.op('act', 'activation', dict(out=tmp[jj][:, :n], in_=C.ps[bk][:, q0:q1], func=AF.Identity,
                                           bias=bias[:, m:m + 1], scale=1.0),
                 reads=(f"ps{bk}", 'mixc'), writes=(rn,))
            P.op('dve', 'scalar_tensor_tensor',
                 dict(out=C.x[:, m, x0:x0 + n], in0=tmp[jj][:, :n], scalar=C.gT[:, j, m, si:si + 1],
                      in1=C.x[:, m, x0:x0 + n], op0=ALU.mult, op1=ALU.add),
                 reads=(rn, 'mod') + xr(m, x0, x0 + n), writes=xr(m, x0, x0 + n))
    return epi

TILES_C = [(192, 448), (448, 704), (704, 960), (960, 1216), (1216, 1344)]


def ext_layout(c0, c1, hl):
    out = []
    e = 0
    for (s0, s1, si) in spans_of(c0, c1):
        out.append((e, s0, s1 - s0, si))
        e += hl + (s1 - s0)
    return out, e


def conv_mixer(P, C, ws, ins, outs):
    nc = P.nc
    HL = 30
    with ExitStack() as es:
        S = lambda name, shape, dt: es.enter_context(SBT(nc, name, shape, dt))
        cvt = S("cvt", [128, CV.n], F32)
        E = S("cvE", [128, 16, 544], BF16)
        halo = S("cvhalo", [128, 16, HL], BF16)
        co32 = S("cvo32", [128, 16, 3, HL], F32)
        tf = [S(f"cvtf{i}", [128, 512], F32) for i in range(3)]
        sqb = [S(f"cvsq{i}", [128, 512], BF16) for i in range(2)]
        rt = [S(f"cvrt{i}", [128, 512], F32) for i in range(2)]
        rtmp = [S(f"cvrtmp{i}", [128, 512], F32) for i in range(2)]
        P.op('sp', 'dma_start', dict(out=cvt[:], in_=ins['cv']), writes=('mixc',), sem='in2', inc=16)
        b1 = vv(cvt, CV, 'b1')
        wdw = vv(cvt, CV, 'wdw', 16)
        bdw, lng, lnb, b2 = (vv(cvt, CV, n) for n in ('bdw', 'lng', 'lnb', 'b2'))
        cmask = vv(cvt, CV, 'coremask')
        ntf = 0
        for ti, (c0, c1) in enumerate(TILES_A):
            w = c1 - c0
            lay, NE = ext_layout(c0, c1, HL)
            NQ = NE - HL
            if ti == 0:
                P.op('dve', 'memset', dict(ap=E[:], constant=0.0), writes=tuple(f"E{c}" for c in range(16)))
            else:
                P.op('act', 'activation', dict(out=E[:, :, 0:HL], in_=halo[:], func=AF.Copy),
                     reads=('cvhalo',), writes=tuple(f"E{c}" for c in range(16)))
            if ti == 2:
                for (e0, x0, n, si) in lay[1:]:
                    P.op('pool', 'dma_start', dict(out=E[:, :, e0:e0 + HL], in_=ins['convc'][:, :, si - 1, :]),
                         writes=tuple(f"E{c}" for c in range(16)), sem='in3', inc=16)
            for c in range(16):
                blka = [ws.take(128, ('conv_w_pw1', (), k * 128, [(c * 128, 128)])) for k in range(16)]
                blkb = [ws.take(128, ('conv_w_pw1', (), k * 128, [(2048 + c * 128, 128)])) for k in range(16)]
                ba, bb = P.bank(), P.bank()
                for blks, bk in ((blka, ba), (blkb, bb)):
                    P.op('pe', 'matmul', [dict(out=C.ps[bk][:, :w], lhsT=blks[k][0], rhs=C.h[:, k, c0:c1],
                                               start=(k == 0), stop=(k == 15)) for k in range(16)],
                         reads=hr_all(c0, c1) + tuple(set(b[1] for b in blks)), writes=(f"ps{bk}",))
                ws.commit()
                jj = ntf % 3
                ntf += 1
                sg = tf[jj]
                P.op('act', 'activation', dict(out=sg[:, :w], in_=C.ps[bb][:, :w], func=AF.Sigmoid,
                                               bias=b1[:, 16 + c:17 + c], scale=1.0),
                     reads=(f"ps{bb}", 'mixc'), writes=(f"cvtf{jj}",))
                for (e0, x0, n, si) in lay:
                    P.op('dve', 'scalar_tensor_tensor',
                         dict(out=E[:, c, e0 + HL:e0 + HL + n], in0=C.ps[ba][:, x0 - c0:x0 - c0 + n],
                              scalar=b1[:, c:c + 1], in1=sg[:, x0 - c0:x0 - c0 + n], op0=ALU.add, op1=ALU.mult),
                         reads=(f"ps{ba}", 'mixc', f"cvtf{jj}"), writes=(f"E{c}",))
                    if ti == 2:
                        P.op('dve', 'scalar_tensor_tensor',
                             dict(out=co32[:, c, si, :], in0=C.ps[ba][:, x0 - c0 + n - HL:x0 - c0 + n],
                                  scalar=b1[:, c:c + 1], in1=sg[:, x0 - c0 + n - HL:x0 - c0 + n],
                                  op0=ALU.add, op1=ALU.mult),
                             reads=(f"ps{ba}", 'mixc', f"cvtf{jj}"), writes=('co32',))
                if ti == 0:
                    P.op('dve', 'tensor_scalar', dict(out=E[:, c, HL:HL + HALO], in0=E[:, c, HL:HL + HALO],
                                                      scalar1=cmask[:, 0:1], scalar2=None, op0=ALU.mult),
                         reads=(f"E{c}", 'mixc'), writes=(f"E{c}",))
            if ti < 2:
                P.op('act', 'activation', dict(out=halo[:], in_=E[:, :, NE - HL:NE], func=AF.Copy),
                     reads=tuple(f"E{c}" for c in range(16)), writes=('cvhalo',))
            s1b, s2b = P.reserve(2)
            for c in range(16):
                jj = ntf % 3
                ntf += 1
                acc = tf[jj]
                P.op('dve', 'tensor_scalar', dict(out=acc[:, :NQ], in0=E[:, c, 0:NQ], scalar1=wdw[:, c, 0:1],
                                                  scalar2=bdw[:, c:c + 1], op0=ALU.mult, op1=ALU.add),
                     reads=(f"E{c}", 'mixc'), writes=(f"cvtf{jj}",))
                for t in range(1, 31):
                    P.op('dve', 'scalar_tensor_tensor',
                         dict(out=acc[:, :NQ], in0=E[:, c, t:t + NQ], scalar=wdw[:, c, t:t + 1], in1=acc[:, :NQ],
                              op0=ALU.mult, op1=ALU.add),
                         reads=(f"E{c}", 'mixc', f"cvtf{jj}"), writes=(f"cvtf{jj}",))
                P.op('act', 'activation', dict(out=E[:, c, 0:NQ], in_=acc[:, :NQ], func=AF.Copy),
                     reads=(f"cvtf{jj}",), writes=(f"E{c}",))
                sj = c % 2
                P.op('act', 'activation', dict(out=sqb[sj][:, :NQ], in_=acc[:, :NQ], func=AF.Square),
                     reads=(f"cvtf{jj}",), writes=(f"cvsq{sj}",))
                P.op('pe', 'matmul', dict(out=C.ps[s1b][:, :NQ], lhsT=C.ones[:], rhs=E[:, c, 0:NQ],
                                          start=(c == 0), stop=(c == 15)),
                     reads=(f"E{c}", 'ones'), writes=(f"ps{s1b}",))
                P.op('pe', 'matmul', dict(out=C.ps[s2b][:, :NQ], lhsT=C.ones[:], rhs=sqb[sj][:, :NQ],
                                          start=(c == 0), stop=(c == 15)),
                     reads=(f"cvsq{sj}", 'ones'), writes=(f"ps{s2b}",))
            mean, rstd = rt
            P.op('act', 'activation', dict(out=mean[:, :NQ], in_=C.ps[s1b][:, :NQ], func=AF.Copy, scale=1.0 / 2048),
                 reads=(f"ps{s1b}",), writes=('cvmean',))
            msq = tf[ntf % 3]
            mj = ntf % 3
            ntf += 1
            P.op('dve', 'tensor_tensor', dict(out=msq[:, :NQ], in0=mean[:, :NQ], in1=mean[:, :NQ], op=ALU.mult),
                 reads=('cvmean',), writes=(f"cvtf{mj}",))
            P.op('dve', 'scalar_tensor_tensor', dict(out=rstd[:, :NQ], in0=C.ps[s2b][:, :NQ], scalar=1.0 / 2048,
                                                     in1=msq[:, :NQ], op0=ALU.mult, op1=ALU.subtract),
                 reads=(f"ps{s2b}", f"cvtf{mj}"), writes=('cvrstd',))
            P.op('act', 'activation', dict(out=rstd[:, :NQ], in_=rstd[:, :NQ], func=AF.Sqrt, bias=C.eps[:, 0:1],
                                           scale=1.0), reads=('cvrstd', 'eps'), writes=('cvrstd',))
            P.op('dve', 'reciprocal', dict(out=rstd[:, :NQ], in_=rstd[:, :NQ]), reads=('cvrstd',), writes=('cvrstd',))
            P.unreserve([s1b, s2b])
            for c in range(16):
                jj = ntf % 3
                ntf += 1
                t1 = tf[jj]
                P.op('dve', 'tensor_tensor', dict(out=t1[:, :NQ], in0=E[:, c, 0:NQ], in1=mean[:, :NQ],
                                                  op=ALU.subtract),
                     reads=(f"E{c}", 'cvmean'), writes=(f"cvtf{jj}",))
                P.op('dve', 'tensor_tensor', dict(out=t1[:, :NQ], in0=t1[:, :NQ], in1=rstd[:, :NQ], op=ALU.mult),
                     reads=(f"cvtf{jj}", 'cvrstd'), writes=(f"cvtf{jj}",))
                P.op('act', 'activation', dict(out=E[:, c, 0:NQ], in_=t1[:, :NQ], func=AF.Silu,
                                               bias=lnb[:, c:c + 1], scale=lng[:, c:c + 1]),
                     reads=(f"cvtf{jj}", 'mixc'), writes=(f"E{c}",))
            linear_fm(P, C, ws, 'conv_w_pw2', (), 16, range(16), lambda k: E[:, k, 0:NQ],
                      [f"E{c}" for c in range(16)], NQ,
                      resid_epi(P, C, 1, b2, rtmp, [(e0, e0 + n, x0, si) for (e0, x0, n, si) in lay]))
        P.op('sp', 'dma_start', dict(out=outs['convo'], in_=co32[:]), reads=('co32',), sem='out', inc=16)
        P.fence()


def sqb_f32(rt, tf):
    return [tf[0], tf[1]]


def pool_mixer(P, C, ws, ins, outs):
    nc = P.nc
    HL = 15
    with ExitStack() as es:
        S = lambda name, shape, dt: es.enter_context(SBT(nc, name, shape, dt))
        plt = S("plt", [128, PL.n], F32)
        Pb = S("plPb", [128, 16, 480], BF16)
        Ec = [S(f"plE{i}", [128, 512], F32) for i in range(2)]
        B = [S(f"plB{i}", [128, 512], F32) for i in range(2)]
        halo = S("plhalo", [128, 16, HL], F32)
        po32 = S("plo32", [128, 16, 3, HL], F32)
        cch = S("plcch", [128, 16, 2, HL], F32)
        rtmp = [S(f"plrt{i}", [128, 512], F32) for i in range(2)]
        P.op('sp', 'dma_start', dict(out=plt[:], in_=ins['pl']), writes=('mixc',), sem='in2', inc=16)
        P.op('sp', 'dma_start', dict(out=cch[:], in_=ins['poolc']), writes=('plcch',), sem='in3', inc=16)
        scale = vv(plt, PL, 'scale')
        cmask = vv(plt, PL, 'coremask')
        pcorr = vv(plt, PL, 'pcorr', 4)
        for i in range(2):
            P.op('dve', 'memset', dict(ap=B[i][:], constant=0.0), writes=(f"plB{i}",))
            P.op('dve', 'memset', dict(ap=Ec[i][:], constant=0.0), writes=(f"plE{i}",))
        for ti, (c0, c1) in enumerate(TILES_A):
            w = c1 - c0
            lay, NE = ext_layout(c0, c1, HL)
            NQ = NE - HL
            for c in range(16):
                gi = c // 4
                win = 2 << gi
                ej = c % 2
                e = Ec[ej]
                blks = [ws.take(128, ('pool_w_in', (), k * 128, [(c * 128, 128)])) for k in range(16)]
                bk = P.bank()
                P.op('pe', 'matmul', [dict(out=C.ps[bk][:, :w], lhsT=blks[k][0], rhs=C.h[:, k, c0:c1],
                                           start=(k == 0), stop=(k == 15)) for k in range(16)],
                     reads=hr_all(c0, c1) + tuple(set(b[1] for b in blks)), writes=(f"ps{bk}",))
                ws.commit()
                for (e0, x0, n, si) in lay:
                    P.op('act', 'activation', dict(out=e[:, e0 + HL:e0 + HL + n], in_=C.ps[bk][:, x0 - c0:x0 - c0 + n],
                                                   func=AF.Copy), reads=(f"ps{bk}",), writes=(f"plE{ej}",))
                    if si == 0:
                        if ti == 0:
                            P.op('dve', 'memset', dict(ap=e[:, 0:HL], constant=0.0), writes=(f"plE{ej}",))
                            P.op('dve', 'tensor_scalar', dict(out=e[:, HL:HL + HALO], in0=e[:, HL:HL + HALO],
                                                              scalar1=cmask[:, 0:1], scalar2=None, op0=ALU.mult),
                                 reads=(f"plE{ej}", 'mixc'), writes=(f"plE{ej}",))
                        else:
                            P.op('act', 'activation', dict(out=e[:, 0:HL], in_=halo[:, c, :], func=AF.Copy),
                                 reads=('plhalo%d' % c,), writes=(f"plE{ej}",))
                    else:
                        P.op('act', 'activation', dict(out=e[:, e0:e0 + HL], in_=cch[:, c, si - 1, :], func=AF.Copy),
                             reads=('plcch',), writes=(f"plE{ej}",))
                    if ti == 2:
                        P.op('act', 'activation', dict(out=po32[:, c, si, :], in_=e[:, e0 + n:e0 + n + HL],
                                                       func=AF.Copy), reads=(f"plE{ej}",), writes=('po32',))
                if ti < 2:
                    P.op('act', 'activation', dict(out=halo[:, c, :], in_=e[:, NE - HL:NE], func=AF.Copy),
                         reads=(f"plE{ej}",), writes=('plhalo%d' % c,))
                cur, curr = e, f"plE{ej}"
                d = 1
                pj = 0
                while d < win:
                    nb = B[pj]
                    P.op('dve', 'tensor_tensor', dict(out=nb[:, d:NE], in0=cur[:, d:NE], in1=cur[:, 0:NE - d],
                                                      op=ALU.add), reads=(curr,), writes=(f"plB{pj}",))
                    cur, curr = nb, f"plB{pj}"
                    pj ^= 1
                    d *= 2
                if ti == 0:
                    q0 = HL + HALO
                    P.op('dve', 'tensor_tensor', dict(out=cur[:, q0:q0 + 16], in0=cur[:, q0:q0 + 16],
                                                      in1=pcorr[:, gi, :], op=ALU.mult),
                         reads=(curr, 'mixc'), writes=(curr,))
                P.op('dve', 'scalar_tensor_tensor', dict(out=Pb[:, c, 0:NQ], in0=cur[:, HL:NE], scalar=1.0 / win,
                                                         in1=e[:, HL:NE], op0=ALU.mult, op1=ALU.subtract),
                     reads=(curr, f"plE{ej}"), writes=(f"Pb{c}",))
            for gi in range(4):
                def epi(mi, m, bk, gi=gi):
                    cc = 4 * gi + m
                    for (e0, x0, n, si) in lay:
                        P.op('act', 'activation', dict(out=C.h[:, cc, x0:x0 + n], in_=C.ps[bk][:, e0:e0 + n],
                                                       func=AF.Identity, bias=0.0, scale=scale[:, cc:cc + 1]),
                             reads=(f"ps{bk}", 'mixc'), writes=xr(cc, x0, x0 + n, 'h'))
                linear_fm(P, C, ws, 'pool_w_grp', (gi,), 4, range(4), lambda k, gi=gi: Pb[:, 4 * gi + k, 0:NQ],
                          [f"Pb{4 * gi + k}" for k in range(4)], NQ, epi)
            linear_fm(P, C, ws, 'pool_w_out', (), 16, range(16), lambda k: C.h[:, k, c0:c1], hr_all(c0, c1), w,
                      resid_epi(P, C, 1, None, rtmp, [(s0 - c0, s1 - c0, s0, si) for (s0, s1, si) in spans_of(c0, c1)]))
        P.op('sp', 'dma_start', dict(out=outs['poolo'], in_=po32[:]), reads=('po32',), sem='out', inc=16)
        P.fence()

def swa_mixer(P, C, ws, ins, outs):
    nc = P.nc
    import os
    SK = os.environ.get('SWA_SKIP', '').split(',')
    with ExitStack() as es:
        S = lambda name, shape, dt: es.enter_context(SBT(nc, name, shape, dt))
        att = S("att", [128, AT.n], F32)
        sink8 = S("sink8", [128, 32], F32)
        kTd = S("kTd", [128, 4, 384], BF16)
        Vt = S("Vt", [64, 6, 4, 128], BF16)
        o = S("att_o", [128, 16, 256], BF16)
        qz = [S(f"qz{i}", [128, 256], BF16) for i in range(4)]
        NSET = 8
        tt = [S(f"att_t{i}", [64, 196], F32) for i in range(NSET)]
        pT = [S(f"att_pT{i}", [64, 192], BF16) for i in range(NSET)]
        st = [S(f"att_st{i}", [64, 4], F32) for i in range(NSET)]
        kvo = [S(f"att_kvo{i}", [64, 512], F32) for i in range(2)]
        rtmp = [S(f"att_rt{i}", [128, 256], F32) for i in range(2)]
        P.op('sp', 'dma_start', dict(out=att[:], in_=ins['at']), writes=('mixc',), sem='in2', inc=16)
        sinks = vv(att, AT, 'sinks')
        dist = vv(att, AT, 'dist')
        amask = vv(att, AT, 'amask', 2)
        ident = vv(att, AT, 'ident')
        P.op('act', 'activation', dict(out=sink8[:], in_=sinks, func=AF.Copy, scale=1.0 / ATT_SCALE),
             reads=('mixc',), writes=('sink8',))
        for i in range(4):
            P.op('dve', 'memset', dict(ap=qz[i][:], constant=0.0), writes=(f"qz{i}",))
        slopes = [2.0 ** (-8.0 * (hd + 1) / 32.0) for hd in range(32)]
        nq = 0
        nst = 0
        nkvo = 0
        for ti, (c0, c1) in enumerate(TILES_C):
            w = c1 - c0
            sample = (ti == 4)
            if not sample:
                kc0, kc1 = c0 - 128, c1
                kdst = [(0, 0, 384)]
            else:
                kc0, kc1 = c0, c1
                kdst = [(0, 128, 64), (64, 320, 64)]
                for s in range(2):
                    if 'kcdma' in SK:
                        continue
                    P.op('pool', 'dma_start', dict(out=kTd[:, :, s * 192:s * 192 + 128], in_=ins['kcT'][:, s]),
                         writes=tuple(f"kTd{kv}" for kv in range(4)), sem='in3', inc=16)
            nkc = kc1 - kc0

            def kepi(mi, m, bk):
                for (pc, kc, n) in kdst:
                    P.op('act', 'activation', dict(out=kTd[:, m, kc:kc + n], in_=C.ps[bk][:, pc:pc + n], func=AF.Copy),
                         reads=(f"ps{bk}",), writes=(f"kTd{m}",))
            if 'kt' not in SK:
              linear_fm(P, C, ws, 'swa_wk', (), 16, range(4), lambda k: C.h[:, k, kc0:kc1], hr_all(kc0, kc1), nkc, kepi,
                      colmap=lambda m: [(m * 64, 64), (m * 64, 64)])
            if not sample:
                blocks = [(b, kc0 + 64 * b) for b in range(6)]
            else:
                blocks = [(2, 1216), (5, 1280)]
                for s in range(2):
                    for bb in range(2):
                        src = ins['vc'][s, bb * 64:(bb + 1) * 64, :].rearrange("p (k d) -> p k d", k=4)
                        for half in range(2):
                            if 'vcdma' in SK:
                                continue
                            P.op('pool', 'dma_start', dict(out=Vt[:, 3 * s + bb, :, half * 64:(half + 1) * 64], in_=src),
                                 writes=(f"Vt{3 * s + bb}",), sem='in3', inc=16)
            if 'kv' in SK:
                blocks = []
            kvblks = [ws.take(512, ('swa_wkv', (), k * 128, [(0, 512)])) for k in range(16)] if blocks else []
            for i, (b, xc) in enumerate(blocks):
                bk = P.bank()
                P.op('pe', 'matmul', [dict(out=C.ps[bk][0:64, 0:512], lhsT=C.h[:, k, xc:xc + 64], rhs=kvblks[k][0],
                                           start=(k == 0), stop=(k == 15)) for k in range(16)],
                     reads=hr_all(xc, xc + 64) + tuple(set(bb_[1] for bb_ in kvblks)), writes=(f"ps{bk}",))
                vsrc = C.ps[bk][0:64, 256:512].rearrange("p (k d) -> p k d", k=4)
                for half in range(2):
                    if 'kvepi' in SK:
                        continue
                    P.op('act', 'activation', dict(out=Vt[:, b, :, half * 64:(half + 1) * 64], in_=vsrc, func=AF.Copy),
                         reads=(f"ps{bk}",), writes=(f"Vt{b}",))
                orow = None
                if sample:
                    orow = (1 + i, 64)
                elif xc >= 1088 and ti == 3:
                    orow = (0, xc - 1088)
                if orow is not None and 'kvout' not in SK:
                    jj = nkvo % 2
                    nkvo += 1
                    P.op('act', 'activation', dict(out=kvo[jj][:], in_=C.ps[bk][0:64, 0:512], func=AF.Copy),
                         reads=(f"ps{bk}",), writes=(f"kvo{jj}",))
                    P.op('sp', 'dma_start', dict(out=outs['ko'][orow[0], orow[1]:orow[1] + 64, :], in_=kvo[jj][:, 0:256]),
                         reads=(f"kvo{jj}",), sem='out', inc=16)
                    P.op('sp', 'dma_start', dict(out=outs['vo'][orow[0], orow[1]:orow[1] + 64, :], in_=kvo[jj][:, 256:512]),
                         reads=(f"kvo{jj}",), sem='out', inc=16)
            ws.commit()
            nch = w // 64
            for m in range(16 if 'q' not in SK else 0):
                qa, qb = qz[(nq % 2) * 2], qz[(nq % 2) * 2 + 1]
                qra, qrb = f"qz{(nq % 2) * 2}", f"qz{(nq % 2) * 2 + 1}"
                nq += 1

                def qepi(mi, mm, bk, qa=qa, qb=qb, qra=qra, qrb=qrb):
                    P.op('act', 'activation', dict(out=qa[0:64, :w], in_=C.ps[bk][0:64, :w], func=AF.Copy),
                         reads=(f"ps{bk}",), writes=(qra,))
                    P.op('act', 'activation', dict(out=qb[64:128, :w], in_=C.ps[bk][64:128, :w], func=AF.Copy),
                         reads=(f"ps{bk}",), writes=(qrb,))
                linear_fm(P, C, ws, 'swa_wq', (), 16, [m], lambda k: C.h[:, k, c0:c1], hr_all(c0, c1), w, qepi)
                chains = [(hh, n) for hh in range(2) for n in range(nch)] if 'attn' not in SK else []
                info = []
                for ci, (hh, n) in enumerate(chains):
                    head = 2 * m + hh
                    kv = head // 8
                    qt, qr = (qa, qra) if hh == 0 else (qb, qrb)
                    koff = 64 * n if not sample else 192 * n
                    bs_ = P.bank()
                    P.op('pe', 'matmul', dict(out=C.ps[bs_][0:64, 0:192], lhsT=qt[:, 64 * n:64 * n + 64],
                                              rhs=kTd[:, kv, koff:koff + 192], start=True, stop=True),
                         reads=(qr, f"kTd{kv}"), writes=(f"ps{bs_}",))
                    info.append((hh, n, head, kv, koff // 64, bs_))
                for ci, (hh, n, head, kv, kb, bs_) in enumerate(info):
                    t = tt[ci]
                    P.op('dve', 'scalar_tensor_tensor', dict(out=t[:, 0:192], in0=dist[0:64, :],
                                                             scalar=-slopes[head] / ATT_SCALE,
                                                             in1=C.ps[bs_][0:64, 0:192], op0=ALU.mult, op1=ALU.add),
                         reads=('mixc', f"ps{bs_}"), writes=(f"att_t{ci}",))
                    if ti == 0 and n < 2:
                        P.op('dve', 'tensor_tensor', dict(out=t[:, 0:192], in0=t[:, 0:192], in1=amask[0:64, n, :],
                                                          op=ALU.add),
                             reads=('mixc', f"att_t{ci}"), writes=(f"att_t{ci}",))
                for ci, (hh, n, head, kv, kb, bs_) in enumerate(info):
                    P.op('act', 'activation', dict(out=tt[ci][:, 192:193], in_=sink8[0:64, head:head + 1], func=AF.Copy),
                         reads=('sink8', f"att_t{ci}"), writes=(f"att_t{ci}",))
                for ci in range(len(info)):
                    P.op('dve', 'reduce_max', dict(out=st[ci][:, 0:1], in_=tt[ci][:, 0:193], axis=AX.X),
                         reads=(f"att_t{ci}",), writes=(f"att_st{ci}",))
                    P.op('dve', 'tensor_scalar', dict(out=st[ci][:, 1:2], in0=st[ci][:, 0:1], scalar1=-ATT_SCALE,
                                                      scalar2=None, op0=ALU.mult),
                         reads=(f"att_st{ci}",), writes=(f"att_st{ci}",))
                for ci in range(len(info)):
                    P.op('act', 'activation', dict(out=tt[ci][:, 0:193], in_=tt[ci][:, 0:193], func=AF.Exp,
                                                   bias=st[ci][:, 1:2], scale=ATT_SCALE, accum_out=st[ci][:, 2:3]),
                         reads=(f"att_t{ci}", f"att_st{ci}"), writes=(f"att_t{ci}", f"att_st{ci}"))
                for ci in range(len(info)):
                    P.op('dve', 'reciprocal', dict(out=st[ci][:, 3:4], in_=st[ci][:, 2:3]),
                         reads=(f"att_st{ci}",), writes=(f"att_st{ci}",))
                    P.op('dve', 'tensor_scalar', dict(out=tt[ci][:, 0:192], in0=tt[ci][:, 0:192], scalar1=st[ci][:, 3:4],
                                                      scalar2=None, op0=ALU.mult),
                         reads=(f"att_t{ci}", f"att_st{ci}"), writes=(f"att_t{ci}",))
                bts = []
                for ci in range(len(info)):
                    bt = P.bank()
                    bts.append(bt)
                    P.op('pe', 'transpose', [dict(out=C.ps[bt][0:64, 64 * j:64 * j + 64], in_=tt[ci][:, 64 * j:64 * j + 64],
                                                  identity=ident[0:64, 0:64]) for j in range(3)],
                         reads=(f"att_t{ci}", 'mixc'), writes=(f"ps{bt}",))
                for ci in range(len(info)):
                    P.op('act', 'activation', dict(out=pT[ci][:, :], in_=C.ps[bts[ci]][0:64, 0:192], func=AF.Copy),
                         reads=(f"ps{bts[ci]}",), writes=(f"att_pT{ci}",))
                bos = []
                for ci, (hh, n, head, kv, kb, bs_) in enumerate(info):
                    bo = P.bank()
                    bos.append(bo)
                    P.op('pe', 'matmul', [dict(out=C.ps[bo][:, 0:64], lhsT=Vt[:, kb + j, kv, :],
                                               rhs=pT[ci][:, 64 * j:64 * j + 64], start=(j == 0), stop=(j == 2))
                                          for j in range(3)],
                         reads=(f"att_pT{ci}",) + tuple(f"Vt{kb + j}" for j in range(3)), writes=(f"ps{bo}",))
                for ci, (hh, n, head, kv, kb, bs_) in enumerate(info):
                    lo = 64 * hh
                    P.op('act', 'activation', dict(out=o[lo:lo + 64, m, 64 * n:64 * n + 64],
                                                   in_=C.ps[bos[ci]][lo:lo + 64, 0:64], func=AF.Copy),
                         reads=(f"ps{bos[ci]}",), writes=(f"o{m}",))
            if 'wo' not in SK:
              linear_fm(P, C, ws, 'swa_wo', (), 16, range(16), lambda k: o[:, k, 0:w], [f"o{k}" for k in range(16)], w,
                      resid_epi(P, C, 1, None, rtmp, [(s0 - c0, s1 - c0, s0, si) for (s0, s1, si) in spans_of(c0, c1)]))
        for s in range(2):
            if 'd2d' in SK:
                continue
            P.op('sp', 'dma_start', dict(out=outs['ko'][1 + s, 0:64, :], in_=ins['kc'][s, 64:128, :]), sem='out', inc=16)
            P.op('sp', 'dma_start', dict(out=outs['vo'][1 + s, 0:64, :], in_=ins['vc'][s, 64:128, :]), sem='out', inc=16)
        P.fence()

def gelu_tanh(P, dst, src, n, tmps, tnames, src_res, dst_res, bias=None, bias_res=()):
    xx, a = tmps
    rx, ra = tnames
    pr = src.shape[0]
    if isinstance(bias, tuple):
        P.op('act', 'activation', dict(out=xx[:pr, :n], in_=src, func=AF.Identity, bias=bias[1], scale=1.0),
             reads=tuple(src_res) + tuple(bias_res), writes=(rx,))
    else:
        P.op('dve', 'tensor_tensor', dict(out=xx[:pr, :n], in0=src, in1=bias, op=ALU.add),
             reads=tuple(src_res) + tuple(bias_res), writes=(rx,))
    P.op('act', 'activation', dict(out=a[:pr, :n], in_=xx[:pr, :n], func=AF.Square, scale=0.044715 ** 0.5),
         reads=(rx,), writes=(ra,))
    P.op('dve', 'scalar_tensor_tensor', dict(out=a[:pr, :n], in0=a[:pr, :n], scalar=1.0, in1=xx[:pr, :n],
                                             op0=ALU.add, op1=ALU.mult), reads=(ra, rx), writes=(ra,))
    P.op('act', 'activation', dict(out=a[:pr, :n], in_=a[:pr, :n], func=AF.Sigmoid, scale=1.5957691216057308),
         reads=(ra,), writes=(ra,))
    P.op('dve', 'tensor_tensor', dict(out=dst, in0=xx[:pr, :n], in1=a[:pr, :n], op=ALU.mult), reads=(rx, ra), writes=tuple(dst_res))


def gmlp_mixer(P, C, ws, ins, outs):
    nc = P.nc
    with ExitStack() as es:
        S = lambda name, shape, dt: es.enter_context(SBT(nc, name, shape, dt))
        gmt = S("gmt", [128, 176], F32)
        wsb = S("gm_ws", [128, 8, 128], BF16)
        bsx = S("gm_bs", [128, 8, 128], F32)
        vg = S("gm_vg", [128, 2, 4096], BF16)
        ug = S("gm_ug", [128, 4, 256], BF16)
        um = S("gm_um", [128, 4, 256], BF16)
        bc = S("gm_bc", [128, 3, 512], F32)
        tm = [S(f"gm_t{i}", [128, 512], F32) for i in range(6)]
        gsel = [0]
        stt = S("gm_st", [128, 2, 24], F32)
        P.op('sp', 'dma_start', dict(out=gmt[:], in_=ins['gm']), writes=('mixc',), sem='in2', inc=16)
        bin_u = gmt[:, 0:32]
        bout = gmt[:, 32:48]
        gmask = gmt[:, 48:176].unsqueeze(1).to_broadcast([128, 4, 128])
        P.new_sem('bc')
        for ti, (c0, c1) in enumerate(TILES_C):
            w = c1 - c0
            sample = (ti == 4)
            ntc = 1 if sample else 2
            if ti == 0 or sample:
                kind = 1 if sample else 0
                P.op('sp', 'dma_start', dict(out=tm[4][:, 0:512], in_=ins['gmws'][kind, :, 0:512]), writes=('gm_t4',), sem='in2', inc=16)
                if not sample:
                    P.op('dve', 'tensor_tensor', dict(out=tm[4][:, 0:512].rearrange("p (g t) -> p g t", g=4),
                                                      in0=tm[4][:, 0:512].rearrange("p (g t) -> p g t", g=4), in1=gmask, op=ALU.mult),
                         reads=('gm_t4', 'mixc'), writes=('gm_t4',))
                P.op('act', 'activation', dict(out=wsb[:, 0:4, :], in_=tm[4][:, 0:512].rearrange("p (g t) -> p g t", g=4), func=AF.Copy),
                     reads=('gm_t4',), writes=('gm_ws',))
                P.op('sp', 'dma_start', dict(out=tm[4][:, 0:512], in_=ins['gmws'][kind, :, 512:1024]), reads=(), writes=('gm_t4',), sem='in2', inc=16)
                if not sample:
                    P.op('dve', 'tensor_tensor', dict(out=tm[4][:, 0:512].rearrange("p (g t) -> p g t", g=4),
                                                      in0=tm[4][:, 0:512].rearrange("p (g t) -> p g t", g=4), in1=gmask, op=ALU.mult),
                         reads=('gm_t4', 'mixc'), writes=('gm_t4',))
                P.op('act', 'activation', dict(out=wsb[:, 4:8, :], in_=tm[4][:, 0:512].rearrange("p (g t) -> p g t", g=4), func=AF.Copy),
                     reads=('gm_t4',), writes=('gm_ws',))
                P.op('sp', 'dma_start', dict(out=bsx[:], in_=ins['gmbs'][kind]), writes=('gm_bs',), sem='in2', inc=16)
            for cg in range(8):
                P.op('sp', 'dma_start', dict(out=bc[:], in_=ins['gmbc'][:, cg]), writes=('gm_bc',), sem='bc', inc=16)
                vblks = [ws.take(512, ('gmlp_w_in', (), k * 128, [(4096 + cg * 512, 512)])) for k in range(16)]
                bks = []
                for tc in range(ntc):
                    bkv = P.bank()
                    bks.append(bkv)
                    P.op('pe', 'matmul', [dict(out=C.ps[bkv][:, 0:512], lhsT=C.h[:, k, c0 + 128 * tc:c0 + 128 * tc + 128],
                                               rhs=vblks[k][0], start=(k == 0), stop=(k == 15)) for k in range(16)],
                         reads=hr_all(c0, c1) + tuple(set(b_[1] for b_ in vblks)), writes=(f"ps{bkv}",))
                ws.commit()
                for tc in range(ntc):
                    gs = gsel[0] % 2
                    gsel[0] += 1
                    vd, vdn = tm[4 + gs], f"gm_t{4 + gs}"
                    gelu_tanh(P, vd[:, 0:512], C.ps[bks[tc]][:, 0:512], 512, (tm[2 * gs], tm[2 * gs + 1]),
                              (f"gm_t{2 * gs}", f"gm_t{2 * gs + 1}"), (f"ps{bks[tc]}",), (vdn,), bias=bc[:, 0, :], bias_res=('gm_bc',))
                    P.op('act', 'activation', dict(out=vg[:, tc, cg * 512:(cg + 1) * 512], in_=vd[:, 0:512], func=AF.Copy,
                                                   accum_out=stt[:, tc, cg:cg + 1]),
                         reads=(vdn,), writes=(f"vg{tc}", 'gm_st'))
                    P.op('act', 'activation', dict(out=tm[2 * gs + 1][:, 0:512], in_=vd[:, 0:512], func=AF.Square,
                                                   accum_out=stt[:, tc, 8 + cg:9 + cg]),
                         reads=(vdn,), writes=(f"gm_t{2 * gs + 1}", 'gm_st'))
            for tc in range(ntc):
                s = stt[:, tc]
                rr = ('gm_st',)
                P.op('dve', 'reduce_sum', dict(out=s[:, 16:17], in_=s[:, 0:8], axis=AX.X), reads=rr, writes=rr)
                P.op('dve', 'reduce_sum', dict(out=s[:, 17:18], in_=s[:, 8:16], axis=AX.X), reads=rr, writes=rr)
                P.op('dve', 'tensor_scalar', dict(out=s[:, 16:18], in0=s[:, 16:18], scalar1=1.0 / 4096, scalar2=None, op0=ALU.mult),
                     reads=rr, writes=rr)
                P.op('dve', 'tensor_tensor', dict(out=s[:, 18:19], in0=s[:, 16:17], in1=s[:, 16:17], op=ALU.mult), reads=rr, writes=rr)
                P.op('dve', 'tensor_tensor', dict(out=s[:, 19:20], in0=s[:, 17:18], in1=s[:, 18:19], op=ALU.subtract), reads=rr, writes=rr)
                P.op('act', 'activation', dict(out=s[:, 20:21], in_=s[:, 19:20], func=AF.Sqrt, bias=C.eps[:, 0:1], scale=1.0),
                     reads=rr + ('eps',), writes=rr)
                P.op('dve', 'reciprocal', dict(out=s[:, 21:22], in_=s[:, 20:21]), reads=rr, writes=rr)
            for cg in range(8):
                P.op('sp', 'dma_start', dict(out=bc[:], in_=ins['gmbc'][:, cg]), writes=('gm_bc',), sem='bc', inc=16)
                for tc in range(ntc):
                    s = stt[:, tc]
                    vsl = vg[:, tc, cg * 512:(cg + 1) * 512]
                    P.op('dve', 'tensor_scalar', dict(out=tm[0][:, 0:512], in0=vsl, scalar1=s[:, 16:17], scalar2=s[:, 21:22],
                                                      op0=ALU.subtract, op1=ALU.mult),
                         reads=(f"vg{tc}", 'gm_st'), writes=('gm_t0',))
                    P.op('dve', 'tensor_tensor', dict(out=tm[0][:, 0:512], in0=tm[0][:, 0:512], in1=bc[:, 1, :], op=ALU.mult),
                         reads=('gm_t0', 'gm_bc'), writes=('gm_t0',))
                    P.op('dve', 'tensor_tensor', dict(out=tm[1][:, 0:512], in0=tm[0][:, 0:512], in1=bc[:, 2, :], op=ALU.add),
                         reads=('gm_t0', 'gm_bc'), writes=('gm_t1',))
                    P.op('act', 'activation', dict(out=vsl, in_=tm[1][:, 0:512], func=AF.Copy), reads=('gm_t1',), writes=(f"vg{tc}",))
                    if sample:
                        P.op('sp', 'dma_start', dict(out=outs['gvo'][:, cg * 512:(cg + 1) * 512], in_=tm[1][:, 0:512]),
                             reads=('gm_t1',), sem='out', inc=16)
            spans = [(s0 - c0, s1 - c0, s0, si) for (s0, s1, si) in spans_of(c0, c1)]
            for g in range(8):
                def uepi(mi, m, bk):
                    gs = gsel[0] % 2
                    gsel[0] += 1
                    gelu_tanh(P, ug[:, mi, 0:w], C.ps[bk][:, 0:w], w, (tm[2 * gs], tm[2 * gs + 1]),
                              (f"gm_t{2 * gs}", f"gm_t{2 * gs + 1}"),
                              (f"ps{bk}",), (f"ug{mi}",), bias=('pp', bin_u[:, m:m + 1]), bias_res=('mixc',))
                linear_fm(P, C, ws, 'gmlp_w_in', (), 16, [4 * g + i for i in range(4)], lambda k: C.h[:, k, c0:c1],
                          hr_all(c0, c1), w, uepi)
                for cc in range(4):
                    f0 = g * 512 + cc * 128
                    for tc in range(ntc):
                        bk = P.bank()
                        P.op('pe', 'matmul', dict(out=C.ps[bk][:, 0:128], lhsT=vg[:, tc, f0:f0 + 128], rhs=wsb[:, g, :],
                                                  start=True, stop=True),
                             reads=(f"vg{tc}", 'gm_ws'), writes=(f"ps{bk}",))
                        mj = 4 + (gsel[0] % 2)
                        gsel[0] += 1
                        P.op('dve', 'tensor_tensor', dict(out=tm[mj][:, 0:128], in0=C.ps[bk][:, 0:128], in1=bsx[:, g, :], op=ALU.add),
                             reads=(f"ps{bk}", 'gm_bs'), writes=(f"gm_t{mj}",))
                        P.op('dve', 'tensor_tensor', dict(out=um[:, cc, 128 * tc:128 * tc + 128], in0=tm[mj][:, 0:128],
                                                          in1=ug[:, cc, 128 * tc:128 * tc + 128], op=ALU.mult),
                             reads=(f"gm_t{mj}", f"ug{cc}"), writes=(f"um{cc}",))
                colmap = lambda m: [(m * 128, 128)]
                for m in range(16):
                    blks = [ws.take(128, ('gmlp_w_out', (), g * 512 + cc * 128, [(m * 128, 128)])) for cc in range(4)]
                    bk = P.bank()
                    P.op('pe', 'matmul', [dict(out=C.ps[bk][:, :w], lhsT=blks[cc][0], rhs=um[:, cc, 0:w],
                                               start=(cc == 0), stop=(cc == 3)) for cc in range(4)],
                         reads=tuple(f"um{cc}" for cc in range(4)) + tuple(set(b[1] for b in blks)), writes=(f"ps{bk}",))
                    ws.commit()
                    resid_epi(P, C, 1, bout if g == 0 else None, tm[0:2], spans, names=('gm_t0', 'gm_t1'))(m, m, bk)
        P.fence()

IN_SPECS = {
    'xT': [128, 16, T], 'small': [128, SM.n], 'cv': [128, CV.n], 'pl': [128, PL.n], 'at': [128, AT.n],
    'gm': [128, 176], 'convc': [128, 16, 2, 30], 'poolc': [128, 16, 2, 15], 'kcT': [128, 2, 4, 128],
    'kc': [2, 128, 256], 'vc': [2, 128, 256], 'gmws': [2, 128, 1024], 'gmbs': [2, 128, 8, 128],
    'gmbc': [128, 8, 3, 512],
}
OUT_SPECS = {
    'yT': [128, 16, 1152], 'convo': [128, 16, 3, 30], 'poolo': [128, 16, 3, 15],
    'ko': [3, 128, 256], 'vo': [3, 128, 256], 'gvo': [128, 4096],
}


def build(npieces, dry=False, nlayers=DEPTH, stages=None):
    nc = bass.Bass("TRN2", target_bir_lowering=False)
    ins = {k: nc.dram_tensor(k, v, F32, kind="ExternalInput").ap() for k, v in IN_SPECS.items()}
    ins['wstream'] = nc.dram_tensor("wstream", [npieces, 128, 2048], F32, kind="ExternalInput").ap()
    outs = {k: nc.dram_tensor(k, v, F32, kind="ExternalOutput").ap() for k, v in OUT_SPECS.items()}
    with ExitStack() as es:
        P = Prog(nc, es)
        C = Ctx()
        for s in ['in', 'in2', 'in3', 'out']:
            P.new_sem(s)
        setup_common(P, C, ins)
        import os
        P.nbank = int(os.environ.get('BANKOFF', '0'))
        ws = WStream(P, ins['wstream'], npieces, R=RING)
        mixers = [conv_mixer, pool_mixer, swa_mixer, gmlp_mixer]
        import os
        for layer in range(nlayers):
            if not os.environ.get('NOADA'):
                ada(P, C, ws, layer)
            tA = TILES_A if layer < 2 else (TILES_A if layer == 2 else TILES_B)
            st = stages or 'nfmg'
            last = (layer == nlayers - 1)
            if 'n' in st:
                norm_mod(P, C, 0, tA)
            if 'f' in st:
                ffn(P, C, ws, layer, 0, 0, tA)
            if 'n' in st:
                norm_mod(P, C, 1, TILES_A if layer <= 2 else TILES_B)
            import os
            om = os.environ.get('ONLY_MIXER')
            if 'm' in st and (om is None or int(om) == layer):
                mixers[layer](P, C, ws, ins, outs)
            tB = TILES_A if layer < 2 else TILES_B
            if 'g' in st and not (last and stages):
                norm_mod(P, C, 2, tB)
                ffn(P, C, ws, layer, 1, 2, tB)
        norm_mod(P, C, 0, TILES_B, final=outs['yT'])
        P.final_waits = ['out']
        nrec = ws.cur + 1
        if not dry:
            assert nrec == npieces, (nrec, npieces)
            P.emit()
    return nc, ws.rec, nrec


def fm(v):
    v = np.asarray(v, np.float32)
    n = v.shape[-1] // 128
    r = v.reshape(v.shape[:-1] + (n, 128))
    return np.ascontiguousarray(np.moveaxis(r, -1, 0))


def fm_rows(a):
    a = np.asarray(a, np.float32)
    return np.ascontiguousarray(a.T.reshape(16, 128, a.shape[0]).transpose(1, 0, 2))


def pack_stream(rec, npieces, inp):
    stream = np.zeros((npieces, 128, 2048), np.float32)
    cache = {}

    def W(name, idx):
        key = (name, idx)
        if key not in cache:
            if name == 'swa_wkv':
                cache[key] = np.concatenate([inp['swa_wk'], inp['swa_wv']], axis=1)
            else:
                w = inp[name]
                for i in idx:
                    w = w[i]
                cache[key] = w
        return cache[key]
    for (piece, col, ncols, spec) in rec:
        name, idx, r0, segs = spec
        w = W(name, tuple(idx))
        o = col
        for (c0, n) in segs:
            stream[piece, :, o:o + n] = w[r0:r0 + 128, c0:c0 + n]
            o += n
    return stream


_CACHE = {}


def kernel(**inp):
    inp = {k: np.asarray(v) for k, v in inp.items()}
    if 'prog' not in _CACHE:
        _, _, npieces = build(8000, dry=True)
        _CACHE['prog'] = build(npieces)
    nc, rec, npieces = _CACHE['prog']
    in_maps = make_inputs(inp, rec, npieces)
    res = run_bass_kernel_spmd(nc, in_maps, core_ids=list(range(8)))
    return assemble(res.results)


def make_inputs(inp, rec, npieces, cores=range(8)):
    stream = pack_stream(rec, npieces, inp)

    small = np.zeros((128, SM.n), np.float32)
    o, n = SM.off['norm_g']
    small[:, o:o + n] = fm(inp['norm_g']).reshape(128, -1)
    o, n = SM.off['ada_b']
    small[:, o:o + n] = fm(inp['ada_b'].reshape(4, 9, 2048)).reshape(128, -1)
    o, n = SM.off['final_g']
    small[:, o:o + n] = fm(inp['final_g'])
    cv = np.zeros((128, CV.n), np.float32)
    for nm, val in [('b1', fm(inp['conv_b_pw1'])), ('wdw', fm(inp['conv_w_dw']).transpose(0, 2, 1).reshape(128, -1)),
                    ('bdw', fm(inp['conv_b_dw'])), ('lng', fm(inp['conv_ln_g'])), ('lnb', fm(inp['conv_ln_b'])),
                    ('b2', fm(inp['conv_b_pw2']))]:
        o, n = CV.off[nm]
        cv[:, o:o + n] = val.reshape(128, -1)
    pl = np.zeros((128, PL.n), np.float32)
    o, n = PL.off['scale']
    pl[:, o:o + n] = fm(inp['pool_scale'])
    at = np.zeros((128, AT.n), np.float32)
    o, n = AT.off['sinks']
    at[:, o:o + n] = inp['swa_sinks'][None, :]
    o, n = AT.off['dist']
    ii = np.arange(64)[:, None]
    jj = np.arange(192)[None, :]
    at[:64, o:o + n] = np.abs(ii + 128 - jj).astype(np.float32)
    o, n = AT.off['ident']
    at[:, o:o + n] = np.eye(128, dtype=np.float32)
    gm = np.zeros((128, 176), np.float32)
    gm[:, 0:32] = fm(inp['gmlp_b_in'][:4096])
    gm[:, 32:48] = fm(inp['gmlp_b_out'])
    pos = np.arange(128)
    gm[:, 48:176] = ((pos[:, None] // 64) <= (pos[None, :] // 64)).astype(np.float32)
    wsT = np.ascontiguousarray(inp['gmlp_w_s'].transpose(2, 0, 1))
    gmws = np.zeros((2, 128, 8, 128), np.float32)
    gmws[0] = wsT
    gmws[1, :64, :, :64] = wsT[:64, :, :64]
    gmws[1, 64:, :, 64:] = wsT[:64, :, :64]
    gmws = gmws.reshape(2, 128, 1024)
    gmbs = np.zeros((2, 128, 8, 128), np.float32)
    gmbs[0] = inp['gmlp_b_s'][None, :, :]
    gmbs[1, :, :, :64] = inp['gmlp_b_s'][None, :, :64]
    gmbs[1, :, :, 64:] = inp['gmlp_b_s'][None, :, :64]
    gmbc = np.zeros((128, 8, 3, 512), np.float32)
    gmbc[:, :, 0, :] = inp['gmlp_b_in'][4096:].reshape(8, 512)[None]
    gmbc[:, :, 1, :] = inp['gmlp_ln_g'].reshape(8, 512)[None]
    gmbc[:, :, 2, :] = inp['gmlp_ln_b'].reshape(8, 512)[None]

    in_maps = []
    for core in cores:
        b, half = core // 2, core % 2
        sa, sb_ = 2 * core, 2 * core + 1
        xT = np.zeros((128, 16, T), np.float32)
        if half == 0:
            xT[:, :, HALO:NPR] = fm_rows(inp['x_prompt'][b, 0:1024])
        else:
            xT[:, :, 0:NPR] = fm_rows(inp['x_prompt'][b, 1024 - HALO:2048])
        xT[:, :, 1216:1280] = fm_rows(inp['x_sample'][sa])
        xT[:, :, 1280:1344] = fm_rows(inp['x_sample'][sb_])
        sm = small.copy()
        o, n = SM.off['cT']
        sm[:, o:o + n] = fm(np.stack([inp['c_prompt'][b], inp['c_sample'][sa], inp['c_sample'][sb_]])).transpose(0, 2, 1).reshape(128, -1)
        cvc = cv.copy()
        cvc[:, CV.off['coremask'][0]] = float(half)
        plc = pl.copy()
        plc[:, PL.off['coremask'][0]] = float(half)
        o, n = PL.off['pcorr']
        pc = np.ones((4, 16), np.float32)
        if half == 0:
            for gi, wv in enumerate((2, 4, 8, 16)):
                pc[gi] = wv / np.minimum(np.arange(16) + 1, wv)
        plc[:, o:o + n] = pc.reshape(-1)[None]
        atc = at.copy()
        o, n = AT.off['amask']
        am = np.zeros((2, 192), np.float32)
        if half == 0:
            am[0, :128] = -1e9
            am[1, :64] = -1e9
        atc[:, o:o + n] = am.reshape(-1)[None]
        convc = np.stack([fm_rows(inp['cache_conv'][sa]), fm_rows(inp['cache_conv'][sb_])], axis=2)
        poolc = np.stack([fm_rows(inp['cache_pool'][sa]), fm_rows(inp['cache_pool'][sb_])], axis=2)
        kc = np.stack([inp['cache_swa_k'][sa].reshape(128, 256), inp['cache_swa_k'][sb_].reshape(128, 256)])
        vc = np.stack([inp['cache_swa_v'][sa].reshape(128, 256), inp['cache_swa_v'][sb_].reshape(128, 256)])
        kcT = np.zeros((128, 2, 4, 128), np.float32)
        for s, sq in enumerate((sa, sb_)):
            kk = inp['cache_swa_k'][sq].transpose(2, 1, 0)
            kcT[:64, s] = kk
            kcT[64:, s] = kk
        in_maps.append({'xT': xT, 'small': sm, 'cv': cvc, 'pl': plc, 'at': atc, 'gm': gm,
                        'convc': np.ascontiguousarray(convc), 'poolc': np.ascontiguousarray(poolc), 'kcT': kcT,
                        'kc': np.ascontiguousarray(kc, np.float32), 'vc': np.ascontiguousarray(vc, np.float32),
                        'gmws': gmws, 'gmbs': gmbs, 'gmbc': gmbc, 'wstream': stream})
    return in_maps


def assemble(R):
    def tm(a):
        return a.transpose(2, 1, 0).reshape(a.shape[2], 2048)
    y_p = np.zeros((4, 2048, 2048), np.float32)
    y_s = np.zeros((16, 64, 2048), np.float32)
    conv_p = np.zeros((4, 30, 2048), np.float32)
    conv_s = np.zeros((16, 30, 2048), np.float32)
    pool_p = np.zeros((4, 15, 2048), np.float32)
    pool_s = np.zeros((16, 15, 2048), np.float32)
    k_p = np.zeros((4, 128, 4, 64), np.float32)
    v_p = np.zeros((4, 128, 4, 64), np.float32)
    k_s = np.zeros((16, 128, 4, 64), np.float32)
    v_s = np.zeros((16, 128, 4, 64), np.float32)
    g_s = np.zeros((16, 64, 4096), np.float32)
    for core in range(8):
        b, half = core // 2, core % 2
        r = R[core]
        yt = tm(r['yT'])
        y_p[b, half * 1024:(half + 1) * 1024] = yt[0:1024]
        for s in range(2):
            sq = 2 * core + s
            y_s[sq] = yt[1024 + 64 * s:1088 + 64 * s]
            conv_s[sq] = tm(r['convo'][:, :, 1 + s, :])
            pool_s[sq] = tm(r['poolo'][:, :, 1 + s, :])
            k_s[sq] = r['ko'][1 + s].reshape(128, 4, 64)
            v_s[sq] = r['vo'][1 + s].reshape(128, 4, 64)
            g_s[sq] = r['gvo'][64 * s:64 * s + 64]
        if half == 1:
            conv_p[b] = tm(r['convo'][:, :, 0, :])
            pool_p[b] = tm(r['poolo'][:, :, 0, :])
            k_p[b] = r['ko'][0].reshape(128, 4, 64)
            v_p[b] = r['vo'][0].reshape(128, 4, 64)
    return (y_p, y_s, conv_p, conv_s, pool_p, pool_s, k_p, v_p, k_s, v_s, g_s)
```

```python
import numpy as np
import concourse.bass as bass
import concourse.mybir as mybir
from contextlib import ExitStack
from concourse.bass_utils import run_bass_kernel_spmd
import ml_dtypes

F32 = mybir.dt.float32
BF16 = mybir.dt.bfloat16
AF = mybir.ActivationFunctionType
ALU = mybir.AluOpType
AX = mybir.AxisListType

ENGS = ['pe', 'act', 'dve', 'pool', 'sp']


_UNIQ = [0]


def SBT(nc, name, shape, dt):
    _UNIQ[0] += 1
    return nc.sbuf_tensor(f"sb_{name}_{_UNIQ[0]}", shape, dt)


class Prog:
    def __init__(self, nc, es):
        self.nc = nc
        self.es = es
        self.thunks = {e: [] for e in ENGS}
        self.sem = {}
        self.val = {}
        self.waited = {}
        self.res = {}
        self.nbank = 0
        self.ninstr = 0
        self.fence_deps = {}
        self.reserved = set()
        for e in ['pe', 'act', 'dve', 'pool']:
            self.new_sem(e)

    def new_sem(self, name):
        self.sem[name] = self.es.enter_context(self.nc.semaphore(name))
        self.val[name] = 0

    def sb(self, name, shape, dt):
        return self.es.enter_context(SBT(self.nc, name, shape, dt))

    def op(self, eng, meth, kw, reads=(), writes=(), sem=None, inc=1):
        fn = (meth, kw if isinstance(kw, list) else [kw])
        semname = sem or eng
        deps = dict(self.fence_deps)
        for r in reads:
            st = self.res.get(r)
            if st and st[0]:
                s, v = st[0]
                deps[s] = max(deps.get(s, 0), v)
        for w in writes:
            st = self.res.get(w)
            if st:
                if st[0]:
                    s, v = st[0]
                    deps[s] = max(deps.get(s, 0), v)
                for s, v in st[1].items():
                    deps[s] = max(deps.get(s, 0), v)
        waits = []
        for s, v in deps.items():
            if s == 'pe' and eng == 'pe':
                continue
            if self.waited.get((eng, s), 0) < v:
                self.waited[(eng, s)] = v
                waits.append((s, v))
        self.val[semname] += inc
        tok = (semname, self.val[semname])
        for r in reads:
            st = self.res.setdefault(r, [None, {}])
            st[1][tok[0]] = max(st[1].get(tok[0], 0), tok[1])
        for w in writes:
            self.res[w] = [tok, {}]
        self.thunks[eng].append((waits, fn, semname, inc))
        return tok

    def bank(self):
        while True:
            b = self.nbank % 8
            self.nbank += 1
            if b not in self.reserved:
                return b

    def reserve(self, n):
        out = []
        for _ in range(n):
            b = self.bank()
            self.reserved.add(b)
            out.append(b)
        return out

    def unreserve(self, banks):
        for b in banks:
            self.reserved.discard(b)

    def fence(self):
        for s in ['pe', 'act', 'dve', 'out']:
            if s in self.val and self.val[s] > 0:
                self.fence_deps[s] = self.val[s]

    def run_engine(self, eng, e):
        for waits, fn, semname, inc in self.thunks[eng]:
            for s, v in waits:
                e.wait_ge(self.sem[s], v)
            m = getattr(e, fn[0])
            for kw in fn[1]:
                inst = m(**kw)
                self.ninstr += 1
            inst.then_inc(self.sem[semname], inc)

    def emit(self):
        nc = self.nc
        with nc.Block() as block:
            @block.tensor
            def _(e):
                self.run_engine('pe', e)

            @block.scalar
            def _(e):
                self.run_engine('act', e)

            @block.vector
            def _(e):
                self.run_engine('dve', e)

            @block.gpsimd
            def _(e):
                self.run_engine('pool', e)

            @block.sync
            def _(e):
                self.run_engine('sp', e)
                for s in self.final_waits:
                    e.wait_ge(self.sem[s], self.val[s])


class WStream:
    def __init__(self, P, dram, npieces, R=12, eng='pool'):
        self.P = P
        self.dram = dram
        self.np = npieces
        self.R = R
        self.eng = eng
        self.slots = [P.sb(f"wslot{i}", [128, 2048], BF16) for i in range(R)]
        for i in range(R):
            P.new_sem(f"w{i}")
        self.cur = -1
        self.col = 2048
        self.loaded = 0
        self.released = 0
        self.rec = []
        for i in range(min(R, npieces)):
            self._load(i)

    def _load(self, i):
        s = i % self.R
        slot = self.slots[s]
        src = self.dram[i]
        self.P.op(self.eng, 'dma_start', dict(out=slot[:], in_=src),
                  writes=(f"ws{s}",), sem=f"w{s}", inc=16)
        self.loaded = i + 1

    def take(self, ncols, spec):
        if self.col + ncols > 2048:
            assert self.col == 2048, (self.col, ncols)
            self.cur += 1
            assert self.cur < self.np
            self.col = 0
        s = self.cur % self.R
        ap = self.slots[s][:, self.col:self.col + ncols]
        self.rec.append((self.cur, self.col, ncols, spec))
        self.col += ncols
        return ap, f"ws{s}"

    def commit(self):
        upto = self.cur if self.col == 2048 else self.cur - 1
        while self.released <= upto:
            j = self.released
            self.released += 1
            if j + self.R < self.np:
                self._load(j + self.R)

    def skip_to(self, align):
        if self.col % align:
            self.col += align - self.col % align

    def done(self):
        assert self.cur == self.np - 1, (self.cur, self.np, self.col)

T = 1344
NPR = 1216
HALO = 192
SEQ_BOUNDS = [(0, 1216), (1216, 1280), (1280, 1344)]
TILES_A = [(0, 448), (448, 896), (896, 1344)]
TILES_B = [(192, 576), (576, 960), (960, 1344)]
D = 2048
DEPTH = 4
RING = 6
EPS = 1e-6
ATT_SCALE = 0.125


def spans_of(c0, c1):
    out = []
    for si, (s0, s1) in enumerate(SEQ_BOUNDS):
        lo, hi = max(c0, s0), min(c1, s1)
        if lo < hi:
            out.append((lo, hi, si))
    return out


class Layout:
    def __init__(self):
        self.off = {}
        self.n = 0

    def add(self, name, n):
        self.off[name] = (self.n, n)
        self.n += n


SM = Layout()
SM.add('cT', 48)
SM.add('norm_g', 4 * 3 * 16)
SM.add('ada_b', 4 * 144)
SM.add('final_g', 16)
CV = Layout()
for nm, n in [('b1', 32), ('wdw', 16 * 31), ('bdw', 16), ('lng', 16), ('lnb', 16), ('b2', 16), ('coremask', 1)]:
    CV.add(nm, n)
PL = Layout()
for nm, n in [('scale', 16), ('coremask', 1), ('pcorr', 64)]:
    PL.add(nm, n)
AT = Layout()
for nm, n in [('sinks', 32), ('dist', 192), ('amask', 384), ('ident', 128)]:
    AT.add(nm, n)
GM = Layout()
for nm, n in [('bin_u', 32), ('bout', 16), ('gmask', 128), ('wsT', 1024), ('wsTs', 512), ('bs', 1024)]:
    GM.add(nm, n)


class Ctx:
    pass


def vv(tile, lay, name, a=None, b=None):
    o, n = lay.off[name]
    ap = tile[:, o:o + n]
    if a is not None:
        ap = ap.rearrange("p (a b) -> p a b", a=a)
    return ap


def setup_common(P, C, ins):
    nc = P.nc
    C.ps = [P.es.enter_context(nc.psum_tensor(f"ps{i}", [128, 512], F32)) for i in range(8)]
    C.x = P.sb("x", [128, 16, T], F32)
    C.h = P.sb("h", [128, 16, T], BF16)
    C.ones = P.sb("ones_bf", [128, 128], BF16)
    C.eps = P.sb("eps_t", [128, 1], F32)
    C.small = P.sb("small", [128, SM.n], F32)
    C.modTs = [P.sb(f"modT{i}", [128, 9, 16, 3], F32) for i in range(2)]
    C.aTs = [P.sb(f"aT{i}", [128, 3, 16, 3], F32) for i in range(2)]
    C.gTs = [P.sb(f"gT{i}", [128, 3, 16, 3], F32) for i in range(2)]
    C.par = 0
    C.modT, C.aT, C.gT, C.modr = C.modTs[0], C.aTs[0], C.gTs[0], 'mod0'
    C.condb = P.sb("condb", [128, 16, 3], BF16)
    P.op('dve', 'memset', dict(ap=C.ones[:], constant=1.0), writes=('ones',))
    P.op('dve', 'memset', dict(ap=C.eps[:], constant=EPS), writes=('eps',))
    P.op('sp', 'dma_start', dict(out=C.small[:], in_=ins['small']), writes=('small',), sem='in', inc=16)
    xres = []
    for c in range(16):
        P.op('sp', 'dma_start', dict(out=C.x[:, c, :], in_=ins['xT'][:, c, :]), sem='in', inc=16)
        xres += [f"x{c}_{s}" for s in range(len(SEG))]
    tot = ('in', P.val['in'])
    for r in ['small'] + xres:
        P.res[r] = [tot, {}]
    cT = vv(C.small, SM, 'cT', 16)
    P.op('act', 'activation', dict(out=C.condb[:], in_=cT, func=AF.Silu), reads=('small',), writes=('condb',))


SEG = [(0, 192), (192, 448), (448, 576), (576, 704), (704, 896), (896, 960), (960, 1216), (1216, 1280), (1280, 1344)]


def xr(c, c0, c1, pre='x'):
    return tuple(f"{pre}{c}_{i}" for i, (a, b) in enumerate(SEG) if a < c1 and b > c0)


def hr_all(c0, c1):
    out = ()
    for k in range(16):
        out += xr(k, c0, c1, 'h')
    return out


def norm_mod(P, C, j, tiles, final=None):
    x, h = C.x, C.h
    with ExitStack() as es:
        nc = P.nc
        C.rs = es.enter_context(SBT(nc, "rstd", [128, T], F32))
        sq = [es.enter_context(SBT(nc, f"sq{i}", [128, 448], BF16)) for i in range(3)]
        tmp = [es.enter_context(SBT(nc, f"tmpf{i}", [128, 448], F32)) for i in range(3)]
        yo = [es.enter_context(SBT(nc, f"yo{i}", [128, 448], F32)) for i in range(3)] if final else None
        nsq = ntmp = 0
        for ti, (c0, c1) in enumerate(tiles):
            w = c1 - c0
            bk = P.bank()
            for c in range(16):
                jj = nsq % 3
                nsq += 1
                P.op('act', 'activation', dict(out=sq[jj][:, :w], in_=x[:, c, c0:c1], func=AF.Square),
                     reads=xr(c, c0, c1), writes=(f"sq{jj}",))
                P.op('pe', 'matmul', dict(out=C.ps[bk][:, :w], lhsT=C.ones[:], rhs=sq[jj][:, :w],
                                          start=(c == 0), stop=(c == 15)),
                     reads=(f"sq{jj}", 'ones'), writes=(f"ps{bk}",))
            P.op('act', 'activation', dict(out=C.rs[:, c0:c1], in_=C.ps[bk][:, :w], func=AF.Sqrt,
                                           bias=C.eps[:, 0:1], scale=1.0 / 2048.0),
                 reads=(f"ps{bk}", 'eps'), writes=(f"rs{ti}",))
            P.op('dve', 'reciprocal', dict(out=C.rs[:, c0:c1], in_=C.rs[:, c0:c1]),
                 reads=(f"rs{ti}",), writes=(f"rs{ti}",))
            for c in range(16):
                for (s0, s1, si) in spans_of(c0, c1):
                    jj = ntmp % 3
                    ntmp += 1
                    sw = s1 - s0
                    P.op('dve', 'tensor_tensor', dict(out=tmp[jj][:, :sw], in0=x[:, c, s0:s1], in1=C.rs[:, s0:s1],
                                                      op=ALU.mult),
                         reads=xr(c, s0, s1) + (f"rs{ti}",), writes=(f"tmp{jj}",))
                    if final is None:
                        P.op('act', 'activation', dict(out=h[:, c, s0:s1], in_=tmp[jj][:, :sw], func=AF.Identity,
                                                       bias=C.modT[:, 3 * j, c, si:si + 1],
                                                       scale=C.aT[:, j, c, si:si + 1]),
                             reads=(f"tmp{jj}", C.modr), writes=xr(c, s0, s1, 'h'))
                    else:
                        fg = vv(C.small, SM, 'final_g')
                        P.op('act', 'activation', dict(out=yo[jj][:, :sw], in_=tmp[jj][:, :sw], func=AF.Identity,
                                                       bias=0.0, scale=fg[:, c:c + 1]),
                             reads=(f"tmp{jj}", 'small'), writes=(f"yo{jj}",))
                        P.op('sp', 'dma_start', dict(out=final[:, c, s0 - HALO:s1 - HALO], in_=yo[jj][:, :sw]),
                             reads=(f"yo{jj}",), sem='out', inc=16)
        P.fence()


def ffn(P, C, ws, layer, which, j, tiles, side=None, side_n=14):
    G = 4
    x, h = C.x, C.h
    nc = P.nc
    with ExitStack() as es:
        g = es.enter_context(SBT(nc, "ffn_g", [128, G, T], BF16))
        sl = [es.enter_context(SBT(nc, f"ffn_s{i}", [128, 448], F32)) for i in range(2)]
        nsl = 0
        idx = (layer, which)
        for grp in range(11):
            for fl in range(G):
                f = (grp * G + fl) * 128
                blk1 = [ws.take(128, ('ffn_w1', idx, k * 128, [(f, 128)])) for k in range(16)]
                blk3 = [ws.take(128, ('ffn_w3', idx, k * 128, [(f, 128)])) for k in range(16)]
                for ti, (c0, c1) in enumerate(tiles):
                    w = c1 - c0
                    b1 = P.bank()
                    b3 = P.bank()
                    for (blks, bk) in ((blk1, b1), (blk3, b3)):
                        P.op('pe', 'matmul',
                             [dict(out=C.ps[bk][:, :w], lhsT=blks[k][0], rhs=h[:, k, c0:c1],
                                   start=(k == 0), stop=(k == 15)) for k in range(16)],
                             reads=hr_all(c0, c1) + tuple(set(b[1] for b in blks)), writes=(f"ps{bk}",))
                    jj = nsl % 2
                    nsl += 1
                    P.op('act', 'activation', dict(out=sl[jj][:, :w], in_=C.ps[b1][:, :w], func=AF.Silu),
                         reads=(f"ps{b1}",), writes=(f"sl{jj}",))
                    P.op('dve', 'tensor_tensor', dict(out=g[:, fl, c0:c1], in0=C.ps[b3][:, :w], in1=sl[jj][:, :w],
                                                      op=ALU.mult),
                         reads=(f"ps{b3}", f"sl{jj}"), writes=(f"g{fl}_{ti}",))
                ws.commit()
            for m in range(16):
                blks = [ws.take(128, ('ffn_w2', idx, (grp * G + fl) * 128, [(m * 128, 128)])) for fl in range(G)]
                for ti, (c0, c1) in enumerate(tiles):
                    w = c1 - c0
                    bk = P.bank()
                    P.op('pe', 'matmul',
                         [dict(out=C.ps[bk][:, :w], lhsT=blks[fl][0], rhs=g[:, fl, c0:c1],
                               start=(fl == 0), stop=(fl == G - 1)) for fl in range(G)],
                         reads=tuple(f"g{fl}_{ti}" for fl in range(G)) + tuple(set(b[1] for b in blks)),
                         writes=(f"ps{bk}",))
                    for (s0, s1, si) in spans_of(c0, c1):
                        P.op('dve', 'scalar_tensor_tensor',
                             dict(out=x[:, m, s0:s1], in0=C.ps[bk][:, s0 - c0:s1 - c0],
                                  scalar=C.gT[:, j, m, si:si + 1], in1=x[:, m, s0:s1], op0=ALU.mult, op1=ALU.add),
                             reads=(f"ps{bk}", C.modr) + xr(m, s0, s1), writes=xr(m, s0, s1))
                ws.commit()
            if side is not None:
                for _ in range(side_n):
                    next(side, None)
        if side is not None:
            for _ in side:
                pass
        P.fence()


def ada_gen(P, C, ws, layer):
    par = layer % 2
    modT, aT, gT, mr = C.modTs[par], C.aTs[par], C.gTs[par], f"mod{par}"
    adab = vv(C.small, SM, 'ada_b', 4 * 9)
    ng = vv(C.small, SM, 'norm_g', 12)
    for q in range(9):
        bk = P.reserve(1)[0]
        for c in range(16):
            m = q * 16 + c
            blks = [ws.take(128, ('ada_w', (layer,), k * 128, [(m * 128, 128)])) for k in range(16)]
            P.op('pe', 'matmul',
                 [dict(out=C.ps[bk][:, 3 * c:3 * c + 3], lhsT=blks[k][0], rhs=C.condb[:, k, :],
                       start=(k == 0), stop=(k == 15)) for k in range(16)],
                 reads=('condb',) + tuple(set(b[1] for b in blks)), writes=(f"ps{bk}",))
            ws.commit()
            if c < 15:
                yield
        bb = adab[:, layer * 9 + q, :].unsqueeze(2).to_broadcast([128, 16, 3])
        P.op('dve', 'tensor_tensor', dict(out=modT[:, q], in0=C.ps[bk][:, 0:48].rearrange("p (c s) -> p c s", s=3),
                                          in1=bb, op=ALU.add),
             reads=(f"ps{bk}", 'small'), writes=(mr,))
        P.unreserve([bk])
        yield
    for j in range(3):
        P.op('dve', 'tensor_scalar', dict(out=aT[:, j], in0=modT[:, 3 * j + 1], scalar1=1.0, scalar2=None,
                                          op0=ALU.add), reads=(mr,), writes=(mr,))
        gb = ng[:, layer * 3 + j, :].unsqueeze(2).to_broadcast([128, 16, 3])
        P.op('dve', 'tensor_tensor', dict(out=aT[:, j], in0=aT[:, j], in1=gb, op=ALU.mult),
             reads=(mr, 'small'), writes=(mr,))
        P.op('act', 'activation', dict(out=gT[:, j], in_=modT[:, 3 * j + 2], func=AF.Copy,
                                       scale=(1.0 if j == 1 else 0.5)), reads=(mr,), writes=(mr,))


def use_layer_mod(C, layer):
    par = layer % 2
    C.modT, C.aT, C.gT, C.modr = C.modTs[par], C.aTs[par], C.gTs[par], f"mod{par}"


def linear_fm(P, C, ws, wname, idx, nk, ms, src, src_res, ncols, epi, colmap=None):
    for mi, m in enumerate(ms):
        cm = colmap(m) if colmap else [(m * 128, 128)]
        blks = [ws.take(128, (wname, idx, k * 128, cm)) for k in range(nk)]
        bk = P.bank()
        P.op('pe', 'matmul',
             [dict(out=C.ps[bk][:, :ncols], lhsT=blks[k][0], rhs=src(k), start=(k == 0), stop=(k == nk - 1))
              for k in range(nk)],
             reads=tuple(src_res) + tuple(set(b[1] for b in blks)), writes=(f"ps{bk}",))
        epi(mi, m, bk)
        ws.commit()


def resid_epi(P, C, j, bias, tmp, colspans, names=None):
    cnt = [0]

    def epi(mi, m, bk):
        for (q0, q1, x0, si) in colspans:
            n = q1 - q0
            if bias is None:
                P.op('dve', 'scalar_tensor_tensor',
                     dict(out=C.x[:, m, x0:x0 + n], in0=C.ps[bk][:, q0:q1], scalar=C.gT[:, j, m, si:si + 1],
                          in1=C.x[:, m, x0:x0 + n], op0=ALU.mult, op1=ALU.add),
                     reads=(f"ps{bk}", C.modr) + xr(m, x0, x0 + n), writes=xr(m, x0, x0 + n))
                continue
            jj = cnt[0] % len(tmp)
            cnt[0] += 1
            rn = names[jj] if names else f"rtmp{jj}"
            P.op('act', 'activation', dict(out=tmp[jj][:, :n], in_=C.ps[bk][:, q0:q1], func=AF.Identity,
                                           bias=bias[:, m:m + 1], scale=1.0),
                 reads=(f"ps{bk}", 'mixc'), writes=(rn,))
            P.op('dve', 'scalar_tensor_tensor',
                 dict(out=C.x[:, m, x0:x0 + n], in0=tmp[jj][:, :n], scalar=C.gT[:, j, m, si:si + 1],
                      in1=C.x[:, m, x0:x0 + n], op0=ALU.mult, op1=ALU.add),
                 reads=(rn, C.modr) + xr(m, x0, x0 + n), writes=xr(m, x0, x0 + n))
    return epi

TILES_C = [(192, 448), (448, 704), (704, 960), (960, 1216), (1216, 1344)]


def ext_layout(c0, c1, hl):
    out = []
    e = 0
    for (s0, s1, si) in spans_of(c0, c1):
        out.append((e, s0, s1 - s0, si))
        e += hl + (s1 - s0)
    return out, e


def conv_mixer(P, C, ws, ins, outs):
    nc = P.nc
    HL = 30
    with ExitStack() as es:
        S = lambda name, shape, dt: es.enter_context(SBT(nc, name, shape, dt))
        cvt = S("cvt", [128, CV.n], F32)
        E = S("cvE", [128, 16, 544], BF16)
        halo = S("cvhalo", [128, 16, HL], BF16)
        co32 = S("cvo32", [128, 16, 3, HL], F32)
        tf = [S(f"cvtf{i}", [128, 512], F32) for i in range(3)]
        sqb = [S(f"cvsq{i}", [128, 512], BF16) for i in range(2)]
        rt = [S(f"cvrt{i}", [128, 512], F32) for i in range(2)]
        rtmp = [S(f"cvrtmp{i}", [128, 512], F32) for i in range(2)]
        P.op('sp', 'dma_start', dict(out=cvt[:], in_=ins['cv']), writes=('mixc',), sem='in2', inc=16)
        b1 = vv(cvt, CV, 'b1')
        wdw = vv(cvt, CV, 'wdw', 16)
        bdw, lng, lnb, b2 = (vv(cvt, CV, n) for n in ('bdw', 'lng', 'lnb', 'b2'))
        cmask = vv(cvt, CV, 'coremask')
        ntf = 0
        for ti, (c0, c1) in enumerate(TILES_A):
            w = c1 - c0
            lay, NE = ext_layout(c0, c1, HL)
            NQ = NE - HL
            if ti == 0:
                P.op('dve', 'memset', dict(ap=E[:], constant=0.0), writes=tuple(f"E{c}" for c in range(16)))
            else:
                P.op('act', 'activation', dict(out=E[:, :, 0:HL], in_=halo[:], func=AF.Copy),
                     reads=('cvhalo',), writes=tuple(f"E{c}" for c in range(16)))
            if ti == 2:
                for (e0, x0, n, si) in lay[1:]:
                    P.op('pool', 'dma_start', dict(out=E[:, :, e0:e0 + HL], in_=ins['convc'][:, :, si - 1, :]),
                         writes=tuple(f"E{c}" for c in range(16)), sem='in3', inc=16)
            for c in range(16):
                blka = [ws.take(128, ('conv_w_pw1', (), k * 128, [(c * 128, 128)])) for k in range(16)]
                blkb = [ws.take(128, ('conv_w_pw1', (), k * 128, [(2048 + c * 128, 128)])) for k in range(16)]
                ba, bb = P.bank(), P.bank()
                for blks, bk in ((blka, ba), (blkb, bb)):
                    P.op('pe', 'matmul', [dict(out=C.ps[bk][:, :w], lhsT=blks[k][0], rhs=C.h[:, k, c0:c1],
                                               start=(k == 0), stop=(k == 15)) for k in range(16)],
                         reads=hr_all(c0, c1) + tuple(set(b[1] for b in blks)), writes=(f"ps{bk}",))
                ws.commit()
                jj = ntf % 3
                ntf += 1
                sg = tf[jj]
                P.op('act', 'activation', dict(out=sg[:, :w], in_=C.ps[bb][:, :w], func=AF.Sigmoid,
                                               bias=b1[:, 16 + c:17 + c], scale=1.0),
                     reads=(f"ps{bb}", 'mixc'), writes=(f"cvtf{jj}",))
                for (e0, x0, n, si) in lay:
                    P.op('dve', 'scalar_tensor_tensor',
                         dict(out=E[:, c, e0 + HL:e0 + HL + n], in0=C.ps[ba][:, x0 - c0:x0 - c0 + n],
                              scalar=b1[:, c:c + 1], in1=sg[:, x0 - c0:x0 - c0 + n], op0=ALU.add, op1=ALU.mult),
                         reads=(f"ps{ba}", 'mixc', f"cvtf{jj}"), writes=(f"E{c}",))
                    if ti == 2:
                        P.op('dve', 'scalar_tensor_tensor',
                             dict(out=co32[:, c, si, :], in0=C.ps[ba][:, x0 - c0 + n - HL:x0 - c0 + n],
                                  scalar=b1[:, c:c + 1], in1=sg[:, x0 - c0 + n - HL:x0 - c0 + n],
                                  op0=ALU.add, op1=ALU.mult),
                             reads=(f"ps{ba}", 'mixc', f"cvtf{jj}"), writes=('co32',))
                if ti == 0:
                    P.op('dve', 'tensor_scalar', dict(out=E[:, c, HL:HL + HALO], in0=E[:, c, HL:HL + HALO],
                                                      scalar1=cmask[:, 0:1], scalar2=None, op0=ALU.mult),
                         reads=(f"E{c}", 'mixc'), writes=(f"E{c}",))
            if ti < 2:
                P.op('act', 'activation', dict(out=halo[:], in_=E[:, :, NE - HL:NE], func=AF.Copy),
                     reads=tuple(f"E{c}" for c in range(16)), writes=('cvhalo',))
            s1b, s2b = P.reserve(2)
            for c in range(16):
                jj = ntf % 3
                ntf += 1
                acc = tf[jj]
                P.op('dve', 'tensor_scalar', dict(out=acc[:, :NQ], in0=E[:, c, 0:NQ], scalar1=wdw[:, c, 0:1],
                                                  scalar2=bdw[:, c:c + 1], op0=ALU.mult, op1=ALU.add),
                     reads=(f"E{c}", 'mixc'), writes=(f"cvtf{jj}",))
                for t in range(1, 31):
                    P.op('dve', 'scalar_tensor_tensor',
                         dict(out=acc[:, :NQ], in0=E[:, c, t:t + NQ], scalar=wdw[:, c, t:t + 1], in1=acc[:, :NQ],
                              op0=ALU.mult, op1=ALU.add),
                         reads=(f"E{c}", 'mixc', f"cvtf{jj}"), writes=(f"cvtf{jj}",))
                P.op('act', 'activation', dict(out=E[:, c, 0:NQ], in_=acc[:, :NQ], func=AF.Copy),
                     reads=(f"cvtf{jj}",), writes=(f"E{c}",))
                sj = c % 2
                P.op('act', 'activation', dict(out=sqb[sj][:, :NQ], in_=acc[:, :NQ], func=AF.Square),
                     reads=(f"cvtf{jj}",), writes=(f"cvsq{sj}",))
                P.op('pe', 'matmul', dict(out=C.ps[s1b][:, :NQ], lhsT=C.ones[:], rhs=E[:, c, 0:NQ],
                                          start=(c == 0), stop=(c == 15)),
                     reads=(f"E{c}", 'ones'), writes=(f"ps{s1b}",))
                P.op('pe', 'matmul', dict(out=C.ps[s2b][:, :NQ], lhsT=C.ones[:], rhs=sqb[sj][:, :NQ],
                                          start=(c == 0), stop=(c == 15)),
                     reads=(f"cvsq{sj}", 'ones'), writes=(f"ps{s2b}",))
            mean, rstd = rt
            P.op('act', 'activation', dict(out=mean[:, :NQ], in_=C.ps[s1b][:, :NQ], func=AF.Copy, scale=1.0 / 2048),
                 reads=(f"ps{s1b}",), writes=('cvmean',))
            msq = tf[ntf % 3]
            mj = ntf % 3
            ntf += 1
            P.op('dve', 'tensor_tensor', dict(out=msq[:, :NQ], in0=mean[:, :NQ], in1=mean[:, :NQ], op=ALU.mult),
                 reads=('cvmean',), writes=(f"cvtf{mj}",))
            P.op('dve', 'scalar_tensor_tensor', dict(out=rstd[:, :NQ], in0=C.ps[s2b][:, :NQ], scalar=1.0 / 2048,
                                                     in1=msq[:, :NQ], op0=ALU.mult, op1=ALU.subtract),
                 reads=(f"ps{s2b}", f"cvtf{mj}"), writes=('cvrstd',))
            P.op('act', 'activation', dict(out=rstd[:, :NQ], in_=rstd[:, :NQ], func=AF.Sqrt, bias=C.eps[:, 0:1],
                                           scale=1.0), reads=('cvrstd', 'eps'), writes=('cvrstd',))
            P.op('dve', 'reciprocal', dict(out=rstd[:, :NQ], in_=rstd[:, :NQ]), reads=('cvrstd',), writes=('cvrstd',))
            P.unreserve([s1b, s2b])
            for c in range(16):
                jj = ntf % 3
                ntf += 1
                t1 = tf[jj]
                P.op('dve', 'tensor_tensor', dict(out=t1[:, :NQ], in0=E[:, c, 0:NQ], in1=mean[:, :NQ],
                                                  op=ALU.subtract),
                     reads=(f"E{c}", 'cvmean'), writes=(f"cvtf{jj}",))
                P.op('dve', 'tensor_tensor', dict(out=t1[:, :NQ], in0=t1[:, :NQ], in1=rstd[:, :NQ], op=ALU.mult),
                     reads=(f"cvtf{jj}", 'cvrstd'), writes=(f"cvtf{jj}",))
                P.op('act', 'activation', dict(out=E[:, c, 0:NQ], in_=t1[:, :NQ], func=AF.Silu,
                                               bias=lnb[:, c:c + 1], scale=lng[:, c:c + 1]),
                     reads=(f"cvtf{jj}", 'mixc'), writes=(f"E{c}",))
            linear_fm(P, C, ws, 'conv_w_pw2', (), 16, range(16), lambda k: E[:, k, 0:NQ],
                      [f"E{c}" for c in range(16)], NQ,
                      resid_epi(P, C, 1, b2, rtmp, [(e0, e0 + n, x0, si) for (e0, x0, n, si) in lay]))
        P.op('sp', 'dma_start', dict(out=outs['convo'], in_=co32[:]), reads=('co32',), sem='out', inc=16)
        P.fence()


def sqb_f32(rt, tf):
    return [tf[0], tf[1]]


def pool_mixer(P, C, ws, ins, outs):
    nc = P.nc
    HL = 15
    with ExitStack() as es:
        S = lambda name, shape, dt: es.enter_context(SBT(nc, name, shape, dt))
        plt = S("plt", [128, PL.n], F32)
        Pb = S("plPb", [128, 16, 480], BF16)
        Ec = [S(f"plE{i}", [128, 512], F32) for i in range(2)]
        B = [S(f"plB{i}", [128, 512], F32) for i in range(2)]
        halo = S("plhalo", [128, 16, HL], F32)
        po32 = S("plo32", [128, 16, 3, HL], F32)
        cch = S("plcch", [128, 16, 2, HL], F32)
        rtmp = [S(f"plrt{i}", [128, 512], F32) for i in range(2)]
        P.op('sp', 'dma_start', dict(out=plt[:], in_=ins['pl']), writes=('mixc',), sem='in2', inc=16)
        P.op('sp', 'dma_start', dict(out=cch[:], in_=ins['poolc']), writes=('plcch',), sem='in3', inc=16)
        scale = vv(plt, PL, 'scale')
        cmask = vv(plt, PL, 'coremask')
        pcorr = vv(plt, PL, 'pcorr', 4)
        for i in range(2):
            P.op('dve', 'memset', dict(ap=B[i][:], constant=0.0), writes=(f"plB{i}",))
            P.op('dve', 'memset', dict(ap=Ec[i][:], constant=0.0), writes=(f"plE{i}",))
        for ti, (c0, c1) in enumerate(TILES_A):
            w = c1 - c0
            lay, NE = ext_layout(c0, c1, HL)
            NQ = NE - HL
            for c in range(16):
                gi = c // 4
                win = 2 << gi
                ej = c % 2
                e = Ec[ej]
                blks = [ws.take(128, ('pool_w_in', (), k * 128, [(c * 128, 128)])) for k in range(16)]
                bk = P.bank()
                P.op('pe', 'matmul', [dict(out=C.ps[bk][:, :w], lhsT=blks[k][0], rhs=C.h[:, k, c0:c1],
                                           start=(k == 0), stop=(k == 15)) for k in range(16)],
                     reads=hr_all(c0, c1) + tuple(set(b[1] for b in blks)), writes=(f"ps{bk}",))
                ws.commit()
                for (e0, x0, n, si) in lay:
                    P.op('act', 'activation', dict(out=e[:, e0 + HL:e0 + HL + n], in_=C.ps[bk][:, x0 - c0:x0 - c0 + n],
                                                   func=AF.Copy), reads=(f"ps{bk}",), writes=(f"plE{ej}",))
                    if si == 0:
                        if ti == 0:
                            P.op('dve', 'memset', dict(ap=e[:, 0:HL], constant=0.0), writes=(f"plE{ej}",))
                            P.op('dve', 'tensor_scalar', dict(out=e[:, HL:HL + HALO], in0=e[:, HL:HL + HALO],
                                                              scalar1=cmask[:, 0:1], scalar2=None, op0=ALU.mult),
                                 reads=(f"plE{ej}", 'mixc'), writes=(f"plE{ej}",))
                        else:
                            P.op('act', 'activation', dict(out=e[:, 0:HL], in_=halo[:, c, :], func=AF.Copy),
                                 reads=('plhalo%d' % c,), writes=(f"plE{ej}",))
                    else:
                        P.op('act', 'activation', dict(out=e[:, e0:e0 + HL], in_=cch[:, c, si - 1, :], func=AF.Copy),
                             reads=('plcch',), writes=(f"plE{ej}",))
                    if ti == 2:
                        P.op('act', 'activation', dict(out=po32[:, c, si, :], in_=e[:, e0 + n:e0 + n + HL],
                                                       func=AF.Copy), reads=(f"plE{ej}",), writes=('po32',))
                if ti < 2:
                    P.op('act', 'activation', dict(out=halo[:, c, :], in_=e[:, NE - HL:NE], func=AF.Copy),
                         reads=(f"plE{ej}",), writes=('plhalo%d' % c,))
                cur, curr = e, f"plE{ej}"
                d = 1
                pj = 0
                while d < win:
                    nb = B[pj]
                    P.op('dve', 'tensor_tensor', dict(out=nb[:, d:NE], in0=cur[:, d:NE], in1=cur[:, 0:NE - d],
                                                      op=ALU.add), reads=(curr,), writes=(f"plB{pj}",))
                    cur, curr = nb, f"plB{pj}"
                    pj ^= 1
                    d *= 2
                if ti == 0:
                    q0 = HL + HALO
                    P.op('dve', 'tensor_tensor', dict(out=cur[:, q0:q0 + 16], in0=cur[:, q0:q0 + 16],
                                                      in1=pcorr[:, gi, :], op=ALU.mult),
                         reads=(curr, 'mixc'), writes=(curr,))
                P.op('dve', 'scalar_tensor_tensor', dict(out=Pb[:, c, 0:NQ], in0=cur[:, HL:NE], scalar=1.0 / win,
                                                         in1=e[:, HL:NE], op0=ALU.mult, op1=ALU.subtract),
                     reads=(curr, f"plE{ej}"), writes=(f"Pb{c}",))
            for gi in range(4):
                def epi(mi, m, bk, gi=gi):
                    cc = 4 * gi + m
                    for (e0, x0, n, si) in lay:
                        P.op('act', 'activation', dict(out=C.h[:, cc, x0:x0 + n], in_=C.ps[bk][:, e0:e0 + n],
                                                       func=AF.Identity, bias=0.0, scale=scale[:, cc:cc + 1]),
                             reads=(f"ps{bk}", 'mixc'), writes=xr(cc, x0, x0 + n, 'h'))
                linear_fm(P, C, ws, 'pool_w_grp', (gi,), 4, range(4), lambda k, gi=gi: Pb[:, 4 * gi + k, 0:NQ],
                          [f"Pb{4 * gi + k}" for k in range(4)], NQ, epi)
            linear_fm(P, C, ws, 'pool_w_out', (), 16, range(16), lambda k: C.h[:, k, c0:c1], hr_all(c0, c1), w,
                      resid_epi(P, C, 1, None, rtmp, [(s0 - c0, s1 - c0, s0, si) for (s0, s1, si) in spans_of(c0, c1)]))
        P.op('sp', 'dma_start', dict(out=outs['poolo'], in_=po32[:]), reads=('po32',), sem='out', inc=16)
        P.fence()

def swa_mixer(P, C, ws, ins, outs):
    nc = P.nc
    import os
    SK = os.environ.get('SWA_SKIP', '').split(',')
    with ExitStack() as es:
        S = lambda name, shape, dt: es.enter_context(SBT(nc, name, shape, dt))
        att = S("att", [128, AT.n], F32)
        sink8 = S("sink8", [128, 32], F32)
        kTd = S("kTd", [128, 4, 384], BF16)
        Vt = S("Vt", [64, 6, 4, 128], BF16)
        o = S("att_o", [128, 16, 256], BF16)
        qz = [S(f"qz{i}", [128, 256], BF16) for i in range(4)]
        NSET = 8
        tt = [S(f"att_t{i}", [64, 196], F32) for i in range(NSET)]
        pT = [S(f"att_pT{i}", [64, 192], BF16) for i in range(NSET)]
        st = [S(f"att_st{i}", [64, 4], F32) for i in range(NSET)]
        kvo = [S(f"att_kvo{i}", [64, 512], F32) for i in range(2)]
        rtmp = [S(f"att_rt{i}", [128, 256], F32) for i in range(2)]
        P.op('sp', 'dma_start', dict(out=att[:], in_=ins['at']), writes=('mixc',), sem='in2', inc=16)
        sinks = vv(att, AT, 'sinks')
        dist = vv(att, AT, 'dist')
        amask = vv(att, AT, 'amask', 2)
        ident = vv(att, AT, 'ident')
        P.op('act', 'activation', dict(out=sink8[:], in_=sinks, func=AF.Copy, scale=1.0 / ATT_SCALE),
             reads=('mixc',), writes=('sink8',))
        for i in range(4):
            P.op('dve', 'memset', dict(ap=qz[i][:], constant=0.0), writes=(f"qz{i}",))
        slopes = [2.0 ** (-8.0 * (hd + 1) / 32.0) for hd in range(32)]
        nq = 0
        nst = 0
        nkvo = 0
        for ti, (c0, c1) in enumerate(TILES_C):
            w = c1 - c0
            sample = (ti == 4)
            if not sample:
                kc0, kc1 = c0 - 128, c1
                kdst = [(0, 0, 384)]
            else:
                kc0, kc1 = c0, c1
                kdst = [(0, 128, 64), (64, 320, 64)]
                for s in range(2):
                    if 'kcdma' in SK:
                        continue
                    P.op('pool', 'dma_start', dict(out=kTd[:, :, s * 192:s * 192 + 128], in_=ins['kcT'][:, s]),
                         writes=tuple(f"kTd{kv}" for kv in range(4)), sem='in3', inc=16)
            nkc = kc1 - kc0

            def kepi(mi, m, bk):
                for (pc, kc, n) in kdst:
                    P.op('act', 'activation', dict(out=kTd[:, m, kc:kc + n], in_=C.ps[bk][:, pc:pc + n], func=AF.Copy),
                         reads=(f"ps{bk}",), writes=(f"kTd{m}",))
            if 'kt' not in SK:
              linear_fm(P, C, ws, 'swa_wk', (), 16, range(4), lambda k: C.h[:, k, kc0:kc1], hr_all(kc0, kc1), nkc, kepi,
                      colmap=lambda m: [(m * 64, 64), (m * 64, 64)])
            if not sample:
                blocks = [(b, kc0 + 64 * b) for b in range(6)]
            else:
                blocks = [(2, 1216), (5, 1280)]
                for s in range(2):
                    for bb in range(2):
                        src = ins['vc'][s, bb * 64:(bb + 1) * 64, :].rearrange("p (k d) -> p k d", k=4)
                        for half in range(2):
                            if 'vcdma' in SK:
                                continue
                            P.op('pool', 'dma_start', dict(out=Vt[:, 3 * s + bb, :, half * 64:(half + 1) * 64], in_=src),
                                 writes=(f"Vt{3 * s + bb}",), sem='in3', inc=16)
            if 'kv' in SK:
                blocks = []
            kvblks = [ws.take(512, ('swa_wkv', (), k * 128, [(0, 512)])) for k in range(16)] if blocks else []
            for i, (b, xc) in enumerate(blocks):
                bk = P.bank()
                P.op('pe', 'matmul', [dict(out=C.ps[bk][0:64, 0:512], lhsT=C.h[:, k, xc:xc + 64], rhs=kvblks[k][0],
                                           start=(k == 0), stop=(k == 15)) for k in range(16)],
                     reads=hr_all(xc, xc + 64) + tuple(set(bb_[1] for bb_ in kvblks)), writes=(f"ps{bk}",))
                vsrc = C.ps[bk][0:64, 256:512].rearrange("p (k d) -> p k d", k=4)
                for half in range(2):
                    if 'kvepi' in SK:
                        continue
                    P.op('act', 'activation', dict(out=Vt[:, b, :, half * 64:(half + 1) * 64], in_=vsrc, func=AF.Copy),
                         reads=(f"ps{bk}",), writes=(f"Vt{b}",))
                orow = None
                if sample:
                    orow = (1 + i, 64)
                elif xc >= 1088 and ti == 3:
                    orow = (0, xc - 1088)
                if orow is not None and 'kvout' not in SK:
                    jj = nkvo % 2
                    nkvo += 1
                    P.op('act', 'activation', dict(out=kvo[jj][:], in_=C.ps[bk][0:64, 0:512], func=AF.Copy),
                         reads=(f"ps{bk}",), writes=(f"kvo{jj}",))
                    P.op('sp', 'dma_start', dict(out=outs['ko'][orow[0], orow[1]:orow[1] + 64, :], in_=kvo[jj][:, 0:256]),
                         reads=(f"kvo{jj}",), sem='out', inc=16)
                    P.op('sp', 'dma_start', dict(out=outs['vo'][orow[0], orow[1]:orow[1] + 64, :], in_=kvo[jj][:, 256:512]),
                         reads=(f"kvo{jj}",), sem='out', inc=16)
            ws.commit()
            nch = w // 64
            for m in range(16 if 'q' not in SK else 0):
                qa, qb = qz[(nq % 2) * 2], qz[(nq % 2) * 2 + 1]
                qra, qrb = f"qz{(nq % 2) * 2}", f"qz{(nq % 2) * 2 + 1}"
                nq += 1

                def qepi(mi, mm, bk, qa=qa, qb=qb, qra=qra, qrb=qrb):
                    P.op('act', 'activation', dict(out=qa[0:64, :w], in_=C.ps[bk][0:64, :w], func=AF.Copy),
                         reads=(f"ps{bk}",), writes=(qra,))
                    P.op('act', 'activation', dict(out=qb[64:128, :w], in_=C.ps[bk][64:128, :w], func=AF.Copy),
                         reads=(f"ps{bk}",), writes=(qrb,))
                linear_fm(P, C, ws, 'swa_wq', (), 16, [m], lambda k: C.h[:, k, c0:c1], hr_all(c0, c1), w, qepi)
                chains = [(hh, n) for hh in range(2) for n in range(nch)] if 'attn' not in SK else []
                info = []
                for ci, (hh, n) in enumerate(chains):
                    head = 2 * m + hh
                    kv = head // 8
                    qt, qr = (qa, qra) if hh == 0 else (qb, qrb)
                    koff = 64 * n if not sample else 192 * n
                    bs_ = P.bank()
                    P.op('pe', 'matmul', dict(out=C.ps[bs_][0:64, 0:192], lhsT=qt[:, 64 * n:64 * n + 64],
                                              rhs=kTd[:, kv, koff:koff + 192], start=True, stop=True),
                         reads=(qr, f"kTd{kv}"), writes=(f"ps{bs_}",))
                    info.append((hh, n, head, kv, koff // 64, bs_))
                for ci, (hh, n, head, kv, kb, bs_) in enumerate(info):
                    t = tt[ci]
                    P.op('dve', 'scalar_tensor_tensor', dict(out=t[:, 0:192], in0=dist[0:64, :],
                                                             scalar=-slopes[head] / ATT_SCALE,
                                                             in1=C.ps[bs_][0:64, 0:192], op0=ALU.mult, op1=ALU.add),
                         reads=('mixc', f"ps{bs_}"), writes=(f"att_t{ci}",))
                    if ti == 0 and n < 2:
                        P.op('dve', 'tensor_tensor', dict(out=t[:, 0:192], in0=t[:, 0:192], in1=amask[0:64, n, :],
                                                          op=ALU.add),
                             reads=('mixc', f"att_t{ci}"), writes=(f"att_t{ci}",))
                for ci, (hh, n, head, kv, kb, bs_) in enumerate(info):
                    P.op('act', 'activation', dict(out=tt[ci][:, 192:193], in_=sink8[0:64, head:head + 1], func=AF.Copy),
                         reads=('sink8', f"att_t{ci}"), writes=(f"att_t{ci}",))
                for ci in range(len(info)):
                    P.op('dve', 'reduce_max', dict(out=st[ci][:, 0:1], in_=tt[ci][:, 0:193], axis=AX.X),
                         reads=(f"att_t{ci}",), writes=(f"att_st{ci}",))
                    P.op('dve', 'tensor_scalar', dict(out=st[ci][:, 1:2], in0=st[ci][:, 0:1], scalar1=-ATT_SCALE,
                                                      scalar2=None, op0=ALU.mult),
                         reads=(f"att_st{ci}",), writes=(f"att_st{ci}",))
                for ci in range(len(info)):
                    P.op('act', 'activation', dict(out=tt[ci][:, 0:193], in_=tt[ci][:, 0:193], func=AF.Exp,
                                                   bias=st[ci][:, 1:2], scale=ATT_SCALE, accum_out=st[ci][:, 2:3]),
                         reads=(f"att_t{ci}", f"att_st{ci}"), writes=(f"att_t{ci}", f"att_st{ci}"))
                for ci in range(len(info)):
                    P.op('dve', 'reciprocal', dict(out=st[ci][:, 3:4], in_=st[ci][:, 2:3]),
                         reads=(f"att_st{ci}",), writes=(f"att_st{ci}",))
                    P.op('dve', 'tensor_scalar', dict(out=tt[ci][:, 0:192], in0=tt[ci][:, 0:192], scalar1=st[ci][:, 3:4],
                                                      scalar2=None, op0=ALU.mult),
                         reads=(f"att_t{ci}", f"att_st{ci}"), writes=(f"att_t{ci}",))
                bts = []
                for ci in range(len(info)):
                    bt = P.bank()
                    bts.append(bt)
                    P.op('pe', 'transpose', [dict(out=C.ps[bt][0:64, 64 * j:64 * j + 64], in_=tt[ci][:, 64 * j:64 * j + 64],
                                                  identity=ident[0:64, 0:64]) for j in range(3)],
                         reads=(f"att_t{ci}", 'mixc'), writes=(f"ps{bt}",))
                for ci in range(len(info)):
                    P.op('act', 'activation', dict(out=pT[ci][:, :], in_=C.ps[bts[ci]][0:64, 0:192], func=AF.Copy),
                         reads=(f"ps{bts[ci]}",), writes=(f"att_pT{ci}",))
                bos = []
                for ci, (hh, n, head, kv, kb, bs_) in enumerate(info):
                    bo = P.bank()
                    bos.append(bo)
                    P.op('pe', 'matmul', [dict(out=C.ps[bo][:, 0:64], lhsT=Vt[:, kb + j, kv, :],
                                               rhs=pT[ci][:, 64 * j:64 * j + 64], start=(j == 0), stop=(j == 2))
                                          for j in range(3)],
                         reads=(f"att_pT{ci}",) + tuple(f"Vt{kb + j}" for j in range(3)), writes=(f"ps{bo}",))
                for ci, (hh, n, head, kv, kb, bs_) in enumerate(info):
                    lo = 64 * hh
                    P.op('act', 'activation', dict(out=o[lo:lo + 64, m, 64 * n:64 * n + 64],
                                                   in_=C.ps[bos[ci]][lo:lo + 64, 0:64], func=AF.Copy),
                         reads=(f"ps{bos[ci]}",), writes=(f"o{m}",))
            if 'wo' not in SK:
              linear_fm(P, C, ws, 'swa_wo', (), 16, range(16), lambda k: o[:, k, 0:w], [f"o{k}" for k in range(16)], w,
                      resid_epi(P, C, 1, None, rtmp, [(s0 - c0, s1 - c0, s0, si) for (s0, s1, si) in spans_of(c0, c1)]))
        for s in range(2):
            if 'd2d' in SK:
                continue
            P.op('sp', 'dma_start', dict(out=outs['ko'][1 + s, 0:64, :], in_=ins['kc'][s, 64:128, :]), sem='out', inc=16)
            P.op('sp', 'dma_start', dict(out=outs['vo'][1 + s, 0:64, :], in_=ins['vc'][s, 64:128, :]), sem='out', inc=16)
        P.fence()

def gelu_tanh(P, dst, src, n, tmps, tnames, src_res, dst_res, bias=None, bias_res=()):
    xx, a = tmps
    rx, ra = tnames
    pr = src.shape[0]
    if isinstance(bias, tuple):
        P.op('act', 'activation', dict(out=xx[:pr, :n], in_=src, func=AF.Identity, bias=bias[1], scale=1.0),
             reads=tuple(src_res) + tuple(bias_res), writes=(rx,))
    else:
        P.op('dve', 'tensor_tensor', dict(out=xx[:pr, :n], in0=src, in1=bias, op=ALU.add),
             reads=tuple(src_res) + tuple(bias_res), writes=(rx,))
    P.op('act', 'activation', dict(out=a[:pr, :n], in_=xx[:pr, :n], func=AF.Square, scale=0.044715 ** 0.5),
         reads=(rx,), writes=(ra,))
    P.op('dve', 'scalar_tensor_tensor', dict(out=a[:pr, :n], in0=a[:pr, :n], scalar=1.0, in1=xx[:pr, :n],
                                             op0=ALU.add, op1=ALU.mult), reads=(ra, rx), writes=(ra,))
    P.op('act', 'activation', dict(out=a[:pr, :n], in_=a[:pr, :n], func=AF.Sigmoid, scale=1.5957691216057308),
         reads=(ra,), writes=(ra,))
    P.op('dve', 'tensor_tensor', dict(out=dst, in0=xx[:pr, :n], in1=a[:pr, :n], op=ALU.mult), reads=(rx, ra), writes=tuple(dst_res))


def gmlp_mixer(P, C, ws, ins, outs):
    nc = P.nc
    with ExitStack() as es:
        S = lambda name, shape, dt: es.enter_context(SBT(nc, name, shape, dt))
        gmt = S("gmt", [128, 176], F32)
        wsb = S("gm_ws", [128, 8, 128], BF16)
        bsx = S("gm_bs", [128, 8, 128], F32)
        vg = S("gm_vg", [128, 2, 4096], BF16)
        ug = S("gm_ug", [128, 4, 256], BF16)
        um = S("gm_um", [128, 4, 256], BF16)
        bc = S("gm_bc", [128, 3, 512], F32)
        tm = [S(f"gm_t{i}", [128, 512], F32) for i in range(6)]
        gsel = [0]
        stt = S("gm_st", [128, 2, 24], F32)
        P.op('sp', 'dma_start', dict(out=gmt[:], in_=ins['gm']), writes=('mixc',), sem='in2', inc=16)
        bin_u = gmt[:, 0:32]
        bout = gmt[:, 32:48]
        gmask = gmt[:, 48:176].unsqueeze(1).to_broadcast([128, 4, 128])
        P.new_sem('bc')
        for ti, (c0, c1) in enumerate(TILES_C):
            w = c1 - c0
            sample = (ti == 4)
            ntc = 1 if sample else 2
            if ti == 0 or sample:
                kind = 1 if sample else 0
                P.op('sp', 'dma_start', dict(out=tm[4][:, 0:512], in_=ins['gmws'][kind, :, 0:512]), writes=('gm_t4',), sem='in2', inc=16)
                if not sample:
                    P.op('dve', 'tensor_tensor', dict(out=tm[4][:, 0:512].rearrange("p (g t) -> p g t", g=4),
                                                      in0=tm[4][:, 0:512].rearrange("p (g t) -> p g t", g=4), in1=gmask, op=ALU.mult),
                         reads=('gm_t4', 'mixc'), writes=('gm_t4',))
                P.op('act', 'activation', dict(out=wsb[:, 0:4, :], in_=tm[4][:, 0:512].rearrange("p (g t) -> p g t", g=4), func=AF.Copy),
                     reads=('gm_t4',), writes=('gm_ws',))
                P.op('sp', 'dma_start', dict(out=tm[4][:, 0:512], in_=ins['gmws'][kind, :, 512:1024]), reads=(), writes=('gm_t4',), sem='in2', inc=16)
                if not sample:
                    P.op('dve', 'tensor_tensor', dict(out=tm[4][:, 0:512].rearrange("p (g t) -> p g t", g=4),
                                                      in0=tm[4][:, 0:512].rearrange("p (g t) -> p g t", g=4), in1=gmask, op=ALU.mult),
                         reads=('gm_t4', 'mixc'), writes=('gm_t4',))
                P.op('act', 'activation', dict(out=wsb[:, 4:8, :], in_=tm[4][:, 0:512].rearrange("p (g t) -> p g t", g=4), func=AF.Copy),
                     reads=('gm_t4',), writes=('gm_ws',))
                P.op('sp', 'dma_start', dict(out=bsx[:], in_=ins['gmbs'][kind]), writes=('gm_bs',), sem='in2', inc=16)
            for cg in range(8):
                P.op('sp', 'dma_start', dict(out=bc[:], in_=ins['gmbc'][:, cg]), writes=('gm_bc',), sem='bc', inc=16)
                vblks = [ws.take(512, ('gmlp_w_in', (), k * 128, [(4096 + cg * 512, 512)])) for k in range(16)]
                bks = []
                for tc in range(ntc):
                    bkv = P.bank()
                    bks.append(bkv)
                    P.op('pe', 'matmul', [dict(out=C.ps[bkv][:, 0:512], lhsT=C.h[:, k, c0 + 128 * tc:c0 + 128 * tc + 128],
                                               rhs=vblks[k][0], start=(k == 0), stop=(k == 15)) for k in range(16)],
                         reads=hr_all(c0, c1) + tuple(set(b_[1] for b_ in vblks)), writes=(f"ps{bkv}",))
                ws.commit()
                for tc in range(ntc):
                    gs = gsel[0] % 2
                    gsel[0] += 1
                    vd, vdn = tm[4 + gs], f"gm_t{4 + gs}"
                    gelu_tanh(P, vd[:, 0:512], C.ps[bks[tc]][:, 0:512], 512, (tm[2 * gs], tm[2 * gs + 1]),
                              (f"gm_t{2 * gs}", f"gm_t{2 * gs + 1}"), (f"ps{bks[tc]}",), (vdn,), bias=bc[:, 0, :], bias_res=('gm_bc',))
                    P.op('act', 'activation', dict(out=vg[:, tc, cg * 512:(cg + 1) * 512], in_=vd[:, 0:512], func=AF.Copy,
                                                   accum_out=stt[:, tc, cg:cg + 1]),
                         reads=(vdn,), writes=(f"vg{tc}", 'gm_st'))
                    P.op('act', 'activation', dict(out=tm[2 * gs + 1][:, 0:512], in_=vd[:, 0:512], func=AF.Square,
                                                   accum_out=stt[:, tc, 8 + cg:9 + cg]),
                         reads=(vdn,), writes=(f"gm_t{2 * gs + 1}", 'gm_st'))
            for tc in range(ntc):
                s = stt[:, tc]
                rr = ('gm_st',)
                P.op('dve', 'reduce_sum', dict(out=s[:, 16:17], in_=s[:, 0:8], axis=AX.X), reads=rr, writes=rr)
                P.op('dve', 'reduce_sum', dict(out=s[:, 17:18], in_=s[:, 8:16], axis=AX.X), reads=rr, writes=rr)
                P.op('dve', 'tensor_scalar', dict(out=s[:, 16:18], in0=s[:, 16:18], scalar1=1.0 / 4096, scalar2=None, op0=ALU.mult),
                     reads=rr, writes=rr)
                P.op('dve', 'tensor_tensor', dict(out=s[:, 18:19], in0=s[:, 16:17], in1=s[:, 16:17], op=ALU.mult), reads=rr, writes=rr)
                P.op('dve', 'tensor_tensor', dict(out=s[:, 19:20], in0=s[:, 17:18], in1=s[:, 18:19], op=ALU.subtract), reads=rr, writes=rr)
                P.op('act', 'activation', dict(out=s[:, 20:21], in_=s[:, 19:20], func=AF.Sqrt, bias=C.eps[:, 0:1], scale=1.0),
                     reads=rr + ('eps',), writes=rr)
                P.op('dve', 'reciprocal', dict(out=s[:, 21:22], in_=s[:, 20:21]), reads=rr, writes=rr)
            for cg in range(8):
                P.op('sp', 'dma_start', dict(out=bc[:], in_=ins['gmbc'][:, cg]), writes=('gm_bc',), sem='bc', inc=16)
                for tc in range(ntc):
                    s = stt[:, tc]
                    vsl = vg[:, tc, cg * 512:(cg + 1) * 512]
                    P.op('dve', 'tensor_scalar', dict(out=tm[0][:, 0:512], in0=vsl, scalar1=s[:, 16:17], scalar2=s[:, 21:22],
                                                      op0=ALU.subtract, op1=ALU.mult),
                         reads=(f"vg{tc}", 'gm_st'), writes=('gm_t0',))
                    P.op('dve', 'tensor_tensor', dict(out=tm[0][:, 0:512], in0=tm[0][:, 0:512], in1=bc[:, 1, :], op=ALU.mult),
                         reads=('gm_t0', 'gm_bc'), writes=('gm_t0',))
                    P.op('dve', 'tensor_tensor', dict(out=tm[1][:, 0:512], in0=tm[0][:, 0:512], in1=bc[:, 2, :], op=ALU.add),
                         reads=('gm_t0', 'gm_bc'), writes=('gm_t1',))
                    P.op('act', 'activation', dict(out=vsl, in_=tm[1][:, 0:512], func=AF.Copy), reads=('gm_t1',), writes=(f"vg{tc}",))
                    if sample:
                        P.op('sp', 'dma_start', dict(out=outs['gvo'][:, cg * 512:(cg + 1) * 512], in_=tm[1][:, 0:512]),
                             reads=('gm_t1',), sem='out', inc=16)
            spans = [(s0 - c0, s1 - c0, s0, si) for (s0, s1, si) in spans_of(c0, c1)]
            for g in range(8):
                def uepi(mi, m, bk):
                    gs = gsel[0] % 2
                    gsel[0] += 1
                    gelu_tanh(P, ug[:, mi, 0:w], C.ps[bk][:, 0:w], w, (tm[2 * gs], tm[2 * gs + 1]),
                              (f"gm_t{2 * gs}", f"gm_t{2 * gs + 1}"),
                              (f"ps{bk}",), (f"ug{mi}",), bias=('pp', bin_u[:, m:m + 1]), bias_res=('mixc',))
                linear_fm(P, C, ws, 'gmlp_w_in', (), 16, [4 * g + i for i in range(4)], lambda k: C.h[:, k, c0:c1],
                          hr_all(c0, c1), w, uepi)
                for cc in range(4):
                    f0 = g * 512 + cc * 128
                    for tc in range(ntc):
                        bk = P.bank()
                        P.op('pe', 'matmul', dict(out=C.ps[bk][:, 0:128], lhsT=vg[:, tc, f0:f0 + 128], rhs=wsb[:, g, :],
                                                  start=True, stop=True),
                             reads=(f"vg{tc}", 'gm_ws'), writes=(f"ps{bk}",))
                        mj = 4 + (gsel[0] % 2)
                        gsel[0] += 1
                        P.op('dve', 'tensor_tensor', dict(out=tm[mj][:, 0:128], in0=C.ps[bk][:, 0:128], in1=bsx[:, g, :], op=ALU.add),
                             reads=(f"ps{bk}", 'gm_bs'), writes=(f"gm_t{mj}",))
                        P.op('dve', 'tensor_tensor', dict(out=um[:, cc, 128 * tc:128 * tc + 128], in0=tm[mj][:, 0:128],
                                                          in1=ug[:, cc, 128 * tc:128 * tc + 128], op=ALU.mult),
                             reads=(f"gm_t{mj}", f"ug{cc}"), writes=(f"um{cc}",))
                colmap = lambda m: [(m * 128, 128)]
                for m in range(16):
                    blks = [ws.take(128, ('gmlp_w_out', (), g * 512 + cc * 128, [(m * 128, 128)])) for cc in range(4)]
                    bk = P.bank()
                    P.op('pe', 'matmul', [dict(out=C.ps[bk][:, :w], lhsT=blks[cc][0], rhs=um[:, cc, 0:w],
                                               start=(cc == 0), stop=(cc == 3)) for cc in range(4)],
                         reads=tuple(f"um{cc}" for cc in range(4)) + tuple(set(b[1] for b in blks)), writes=(f"ps{bk}",))
                    ws.commit()
                    resid_epi(P, C, 1, bout if g == 0 else None, tm[0:2], spans, names=('gm_t0', 'gm_t1'))(m, m, bk)
        P.fence()

IN_SPECS = {
    'xT': [128, 16, T], 'small': [128, SM.n], 'cv': [128, CV.n], 'pl': [128, PL.n], 'at': [128, AT.n],
    'gm': [128, 176], 'convc': [128, 16, 2, 30], 'poolc': [128, 16, 2, 15], 'kcT': [128, 2, 4, 128],
    'kc': [2, 128, 256], 'vc': [2, 128, 256], 'gmws': [2, 128, 1024], 'gmbs': [2, 128, 8, 128],
    'gmbc': [128, 8, 3, 512],
}
OUT_SPECS = {
    'yT': [128, 16, 1152], 'convo': [128, 16, 3, 30], 'poolo': [128, 16, 3, 15],
    'ko': [3, 128, 256], 'vo': [3, 128, 256], 'gvo': [128, 4096],
}


def build(npieces, dry=False, nlayers=DEPTH, stages=None):
    nc = bass.Bass("TRN2", target_bir_lowering=False)
    ins = {k: nc.dram_tensor(k, v, F32, kind="ExternalInput").ap() for k, v in IN_SPECS.items()}
    ins['wstream'] = nc.dram_tensor("wstream", [npieces, 128, 2048], F32, kind="ExternalInput").ap()
    outs = {k: nc.dram_tensor(k, v, F32, kind="ExternalOutput").ap() for k, v in OUT_SPECS.items()}
    with ExitStack() as es:
        P = Prog(nc, es)
        C = Ctx()
        for s in ['in', 'in2', 'in3', 'out']:
            P.new_sem(s)
        setup_common(P, C, ins)
        import os
        P.nbank = int(os.environ.get('BANKOFF', '0'))
        ws = WStream(P, ins['wstream'], npieces, R=RING)
        mixers = [conv_mixer, pool_mixer, swa_mixer, gmlp_mixer]
        import os
        for layer in range(nlayers):
            if layer == 0:
                for _ in ada_gen(P, C, ws, 0):
                    pass
                P.fence()
            use_layer_mod(C, layer)
            tA = TILES_A if layer < 2 else (TILES_A if layer == 2 else TILES_B)
            st = stages or 'nfmg'
            last = (layer == nlayers - 1)
            if 'n' in st:
                norm_mod(P, C, 0, tA)
            if 'f' in st:
                ffn(P, C, ws, layer, 0, 0, tA)
            if 'n' in st:
                norm_mod(P, C, 1, TILES_A if layer <= 2 else TILES_B)
            import os
            om = os.environ.get('ONLY_MIXER')
            if 'm' in st and (om is None or int(om) == layer):
                mixers[layer](P, C, ws, ins, outs)
            tB = TILES_A if layer < 2 else TILES_B
            if 'g' in st and not (last and stages):
                norm_mod(P, C, 2, tB)
                side = ada_gen(P, C, ws, layer + 1) if layer + 1 < nlayers else None
                ffn(P, C, ws, layer, 1, 2, tB, side=side)
        norm_mod(P, C, 0, TILES_B, final=outs['yT'])
        P.final_waits = ['out']
        nrec = ws.cur + 1
        if not dry:
            assert nrec == npieces, (nrec, npieces)
            P.emit()
    return nc, ws.rec, nrec


def fm(v):
    v = np.asarray(v, np.float32)
    n = v.shape[-1] // 128
    r = v.reshape(v.shape[:-1] + (n, 128))
    return np.ascontiguousarray(np.moveaxis(r, -1, 0))


def fm_rows(a):
    a = np.asarray(a, np.float32)
    return np.ascontiguousarray(a.T.reshape(16, 128, a.shape[0]).transpose(1, 0, 2))


def pack_stream(rec, npieces, inp):
    stream = np.zeros((npieces, 128, 2048), np.float32)
    cache = {}

    def W(name, idx):
        key = (name, idx)
        if key not in cache:
            if name == 'swa_wkv':
                cache[key] = np.concatenate([inp['swa_wk'], inp['swa_wv']], axis=1)
            else:
                w = inp[name]
                for i in idx:
                    w = w[i]
                cache[key] = w
        return cache[key]
    for (piece, col, ncols, spec) in rec:
        name, idx, r0, segs = spec
        w = W(name, tuple(idx))
        o = col
        for (c0, n) in segs:
            stream[piece, :, o:o + n] = w[r0:r0 + 128, c0:c0 + n]
            o += n
    return stream


_CACHE = {}


def kernel(**inp):
    inp = {k: np.asarray(v) for k, v in inp.items()}
    if 'prog' not in _CACHE:
        _, _, npieces = build(8000, dry=True)
        _CACHE['prog'] = build(npieces)
    nc, rec, npieces = _CACHE['prog']
    in_maps = make_inputs(inp, rec, npieces)
    res = run_bass_kernel_spmd(nc, in_maps, core_ids=list(range(8)))
    return assemble(res.results)


def make_inputs(inp, rec, npieces, cores=range(8)):
    stream = pack_stream(rec, npieces, inp)

    small = np.zeros((128, SM.n), np.float32)
    o, n = SM.off['norm_g']
    small[:, o:o + n] = fm(inp['norm_g']).reshape(128, -1)
    o, n = SM.off['ada_b']
    small[:, o:o + n] = fm(inp['ada_b'].reshape(4, 9, 2048)).reshape(128, -1)
    o, n = SM.off['final_g']
    small[:, o:o + n] = fm(inp['final_g'])
    cv = np.zeros((128, CV.n), np.float32)
    for nm, val in [('b1', fm(inp['conv_b_pw1'])), ('wdw', fm(inp['conv_w_dw']).transpose(0, 2, 1).reshape(128, -1)),
                    ('bdw', fm(inp['conv_b_dw'])), ('lng', fm(inp['conv_ln_g'])), ('lnb', fm(inp['conv_ln_b'])),
                    ('b2', fm(inp['conv_b_pw2']))]:
        o, n = CV.off[nm]
        cv[:, o:o + n] = val.reshape(128, -1)
    pl = np.zeros((128, PL.n), np.float32)
    o, n = PL.off['scale']
    pl[:, o:o + n] = fm(inp['pool_scale'])
    at = np.zeros((128, AT.n), np.float32)
    o, n = AT.off['sinks']
    at[:, o:o + n] = inp['swa_sinks'][None, :]
    o, n = AT.off['dist']
    ii = np.arange(64)[:, None]
    jj = np.arange(192)[None, :]
    at[:64, o:o + n] = np.abs(ii + 128 - jj).astype(np.float32)
    o, n = AT.off['ident']
    at[:, o:o + n] = np.eye(128, dtype=np.float32)
    gm = np.zeros((128, 176), np.float32)
    gm[:, 0:32] = fm(inp['gmlp_b_in'][:4096])
    gm[:, 32:48] = fm(inp['gmlp_b_out'])
    pos = np.arange(128)
    gm[:, 48:176] = ((pos[:, None] // 64) <= (pos[None, :] // 64)).astype(np.float32)
    wsT = np.ascontiguousarray(inp['gmlp_w_s'].transpose(2, 0, 1))
    gmws = np.zeros((2, 128, 8, 128), np.float32)
    gmws[0] = wsT
    gmws[1, :64, :, :64] = wsT[:64, :, :64]
    gmws[1, 64:, :, 64:] = wsT[:64, :, :64]
    gmws = gmws.reshape(2, 128, 1024)
    gmbs = np.zeros((2, 128, 8, 128), np.float32)
    gmbs[0] = inp['gmlp_b_s'][None, :, :]
    gmbs[1, :, :, :64] = inp['gmlp_b_s'][None, :, :64]
    gmbs[1, :, :, 64:] = inp['gmlp_b_s'][None, :, :64]
    gmbc = np.zeros((128, 8, 3, 512), np.float32)
    gmbc[:, :, 0, :] = inp['gmlp_b_in'][4096:].reshape(8, 512)[None]
    gmbc[:, :, 1, :] = inp['gmlp_ln_g'].reshape(8, 512)[None]
    gmbc[:, :, 2, :] = inp['gmlp_ln_b'].reshape(8, 512)[None]

    in_maps = []
    for core in cores:
        b, half = core // 2, core % 2
        sa, sb_ = 2 * core, 2 * core + 1
        xT = np.zeros((128, 16, T), np.float32)
        if half == 0:
            xT[:, :, HALO:NPR] = fm_rows(inp['x_prompt'][b, 0:1024])
        else:
            xT[:, :, 0:NPR] = fm_rows(inp['x_prompt'][b, 1024 - HALO:2048])
        xT[:, :, 1216:1280] = fm_rows(inp['x_sample'][sa])
        xT[:, :, 1280:1344] = fm_rows(inp['x_sample'][sb_])
        sm = small.copy()
        o, n = SM.off['cT']
        sm[:, o:o + n] = fm(np.stack([inp['c_prompt'][b], inp['c_sample'][sa], inp['c_sample'][sb_]])).transpose(0, 2, 1).reshape(128, -1)
        cvc = cv.copy()
        cvc[:, CV.off['coremask'][0]] = float(half)
        plc = pl.copy()
        plc[:, PL.off['coremask'][0]] = float(half)
        o, n = PL.off['pcorr']
        pc = np.ones((4, 16), np.float32)
        if half == 0:
            for gi, wv in enumerate((2, 4, 8, 16)):
                pc[gi] = wv / np.minimum(np.arange(16) + 1, wv)
        plc[:, o:o + n] = pc.reshape(-1)[None]
        atc = at.copy()
        o, n = AT.off['amask']
        am = np.zeros((2, 192), np.float32)
        if half == 0:
            am[0, :128] = -1e9
            am[1, :64] = -1e9
        atc[:, o:o + n] = am.reshape(-1)[None]
        convc = np.stack([fm_rows(inp['cache_conv'][sa]), fm_rows(inp['cache_conv'][sb_])], axis=2)
        poolc = np.stack([fm_rows(inp['cache_pool'][sa]), fm_rows(inp['cache_pool'][sb_])], axis=2)
        kc = np.stack([inp['cache_swa_k'][sa].reshape(128, 256), inp['cache_swa_k'][sb_].reshape(128, 256)])
        vc = np.stack([inp['cache_swa_v'][sa].reshape(128, 256), inp['cache_swa_v'][sb_].reshape(128, 256)])
        kcT = np.zeros((128, 2, 4, 128), np.float32)
        for s, sq in enumerate((sa, sb_)):
            kk = inp['cache_swa_k'][sq].transpose(2, 1, 0)
            kcT[:64, s] = kk
            kcT[64:, s] = kk
        in_maps.append({'xT': xT, 'small': sm, 'cv': cvc, 'pl': plc, 'at': atc, 'gm': gm,
                        'convc': np.ascontiguousarray(convc), 'poolc': np.ascontiguousarray(poolc), 'kcT': kcT,
                        'kc': np.ascontiguousarray(kc, np.float32), 'vc': np.ascontiguousarray(vc, np.float32),
                        'gmws': gmws, 'gmbs': gmbs, 'gmbc': gmbc, 'wstream': stream})
    return in_maps


def assemble(R):
    def tm(a):
        return a.transpose(2, 1, 0).reshape(a.shape[2], 2048)
    y_p = np.zeros((4, 2048, 2048), np.float32)
    y_s = np.zeros((16, 64, 2048), np.float32)
    conv_p = np.zeros((4, 30, 2048), np.float32)
    conv_s = np.zeros((16, 30, 2048), np.float32)
    pool_p = np.zeros((4, 15, 2048), np.float32)
    pool_s = np.zeros((16, 15, 2048), np.float32)
    k_p = np.zeros((4, 128, 4, 64), np.float32)
    v_p = np.zeros((4, 128, 4, 64), np.float32)
    k_s = np.zeros((16, 128, 4, 64), np.float32)
    v_s = np.zeros((16, 128, 4, 64), np.float32)
    g_s = np.zeros((16, 64, 4096), np.float32)
    for core in range(8):
        b, half = core // 2, core % 2
        r = R[core]
        yt = tm(r['yT'])
        y_p[b, half * 1024:(half + 1) * 1024] = yt[0:1024]
        for s in range(2):
            sq = 2 * core + s
            y_s[sq] = yt[1024 + 64 * s:1088 + 64 * s]
            conv_s[sq] = tm(r['convo'][:, :, 1 + s, :])
            pool_s[sq] = tm(r['poolo'][:, :, 1 + s, :])
            k_s[sq] = r['ko'][1 + s].reshape(128, 4, 64)
            v_s[sq] = r['vo'][1 + s].reshape(128, 4, 64)
            g_s[sq] = r['gvo'][64 * s:64 * s + 64]
        if half == 1:
            conv_p[b] = tm(r['convo'][:, :, 0, :])
            pool_p[b] = tm(r['poolo'][:, :, 0, :])
            k_p[b] = r['ko'][0].reshape(128, 4, 64)
            v_p[b] = r['vo'][0].reshape(128, 4, 64)
    return (y_p, y_s, conv_p, conv_s, pool_p, pool_s, k_p, v_p, k_s, v_s, g_s)
```

```python
import numpy as np
import concourse.bass as bass
import concourse.mybir as mybir
from contextlib import ExitStack
from concourse.bass_utils import run_bass_kernel_spmd
import ml_dtypes

F32 = mybir.dt.float32
BF16 = mybir.dt.bfloat16
AF = mybir.ActivationFunctionType
ALU = mybir.AluOpType
AX = mybir.AxisListType

ENGS = ['pe', 'act', 'dve', 'pool', 'sp']


_UNIQ = [0]


def SBT(nc, name, shape, dt):
    _UNIQ[0] += 1
    return nc.sbuf_tensor(f"sb_{name}_{_UNIQ[0]}", shape, dt)


class Prog:
    def __init__(self, nc, es):
        self.nc = nc
        self.es = es
        self.thunks = {e: [] for e in ENGS}
        self.sem = {}
        self.val = {}
        self.waited = {}
        self.res = {}
        self.nbank = 0
        self.ninstr = 0
        self.fence_deps = {}
        self.reserved = set()
        for e in ['pe', 'act', 'dve', 'pool']:
            self.new_sem(e)

    def new_sem(self, name):
        self.sem[name] = self.es.enter_context(self.nc.semaphore(name))
        self.val[name] = 0

    def sb(self, name, shape, dt):
        return self.es.enter_context(SBT(self.nc, name, shape, dt))

    def op(self, eng, meth, kw, reads=(), writes=(), sem=None, inc=1):
        fn = (meth, kw if isinstance(kw, list) else [kw])
        semname = sem or eng
        deps = dict(self.fence_deps)
        for r in reads:
            st = self.res.get(r)
            if st and st[0]:
                s, v = st[0]
                deps[s] = max(deps.get(s, 0), v)
        for w in writes:
            st = self.res.get(w)
            if st:
                if st[0]:
                    s, v = st[0]
                    deps[s] = max(deps.get(s, 0), v)
                for s, v in st[1].items():
                    deps[s] = max(deps.get(s, 0), v)
        waits = []
        for s, v in deps.items():
            if s == 'pe' and eng == 'pe':
                continue
            if self.waited.get((eng, s), 0) < v:
                self.waited[(eng, s)] = v
                waits.append((s, v))
        self.val[semname] += inc
        tok = (semname, self.val[semname])
        for r in reads:
            st = self.res.setdefault(r, [None, {}])
            st[1][tok[0]] = max(st[1].get(tok[0], 0), tok[1])
        for w in writes:
            self.res[w] = [tok, {}]
        self.thunks[eng].append((waits, fn, semname, inc))
        return tok

    def bank(self):
        while True:
            b = self.nbank % 8
            self.nbank += 1
            if b not in self.reserved:
                return b

    def reserve(self, n):
        out = []
        for _ in range(n):
            b = self.bank()
            self.reserved.add(b)
            out.append(b)
        return out

    def unreserve(self, banks):
        for b in banks:
            self.reserved.discard(b)

    def fence(self):
        for s in ['pe', 'act', 'dve', 'out']:
            if s in self.val and self.val[s] > 0:
                self.fence_deps[s] = self.val[s]

    def run_engine(self, eng, e):
        for waits, fn, semname, inc in self.thunks[eng]:
            for s, v in waits:
                e.wait_ge(self.sem[s], v)
            m = getattr(e, fn[0])
            for kw in fn[1]:
                inst = m(**kw)
                self.ninstr += 1
            inst.then_inc(self.sem[semname], inc)

    def emit(self):
        nc = self.nc
        with nc.Block() as block:
            @block.tensor
            def _(e):
                self.run_engine('pe', e)

            @block.scalar
            def _(e):
                self.run_engine('act', e)

            @block.vector
            def _(e):
                self.run_engine('dve', e)

            @block.gpsimd
            def _(e):
                self.run_engine('pool', e)

            @block.sync
            def _(e):
                self.run_engine('sp', e)
                for s in self.final_waits:
                    e.wait_ge(self.sem[s], self.val[s])


class WStream:
    def __init__(self, P, dram, npieces, R=12, eng='pool'):
        self.P = P
        self.dram = dram
        self.np = npieces
        self.R = R
        self.eng = eng
        self.slots = [P.sb(f"wslot{i}", [128, 2048], BF16) for i in range(R)]
        for i in range(R):
            P.new_sem(f"w{i}")
        self.cur = -1
        self.col = 2048
        self.loaded = 0
        self.released = 0
        self.rec = []
        for i in range(min(R, npieces)):
            self._load(i)

    def _load(self, i):
        s = i % self.R
        slot = self.slots[s]
        src = self.dram[i]
        self.P.op(self.eng, 'dma_start', dict(out=slot[:], in_=src),
                  writes=(f"ws{s}",), sem=f"w{s}", inc=16)
        self.loaded = i + 1

    def take(self, ncols, spec):
        if self.col + ncols > 2048:
            assert self.col == 2048, (self.col, ncols)
            self.cur += 1
            assert self.cur < self.np
            self.col = 0
        s = self.cur % self.R
        ap = self.slots[s][:, self.col:self.col + ncols]
        self.rec.append((self.cur, self.col, ncols, spec))
        self.col += ncols
        return ap, f"ws{s}"

    def commit(self):
        upto = self.cur if self.col == 2048 else self.cur - 1
        while self.released <= upto:
            j = self.released
            self.released += 1
            if j + self.R < self.np:
                self._load(j + self.R)

    def skip_to(self, align):
        if self.col % align:
            self.col += align - self.col % align

    def done(self):
        assert self.cur == self.np - 1, (self.cur, self.np, self.col)

T = 1344
NPR = 1216
HALO = 192
SEQ_BOUNDS = [(0, 1216), (1216, 1280), (1280, 1344)]
TILES_A = [(0, 448), (448, 896), (896, 1344)]
TILES_B = [(192, 576), (576, 960), (960, 1344)]
D = 2048
DEPTH = 4
RING = 6
EPS = 1e-6
ATT_SCALE = 0.125


def spans_of(c0, c1):
    out = []
    for si, (s0, s1) in enumerate(SEQ_BOUNDS):
        lo, hi = max(c0, s0), min(c1, s1)
        if lo < hi:
            out.append((lo, hi, si))
    return out


class Layout:
    def __init__(self):
        self.off = {}
        self.n = 0

    def add(self, name, n):
        self.off[name] = (self.n, n)
        self.n += n


SM = Layout()
SM.add('cT', 48)
SM.add('norm_g', 4 * 3 * 16)
SM.add('ada_b', 4 * 144)
SM.add('final_g', 16)
CV = Layout()
for nm, n in [('b1', 32), ('wdw', 16 * 31), ('bdw', 16), ('lng', 16), ('lnb', 16), ('b2', 16), ('coremask', 1)]:
    CV.add(nm, n)
PL = Layout()
for nm, n in [('scale', 16), ('coremask', 1), ('pcorr', 64)]:
    PL.add(nm, n)
AT = Layout()
for nm, n in [('sinks', 32), ('dist', 192), ('amask', 384), ('ident', 128)]:
    AT.add(nm, n)
GM = Layout()
for nm, n in [('bin_u', 32), ('bout', 16), ('gmask', 128), ('wsT', 1024), ('wsTs', 512), ('bs', 1024)]:
    GM.add(nm, n)


class Ctx:
    pass


def vv(tile, lay, name, a=None, b=None):
    o, n = lay.off[name]
    ap = tile[:, o:o + n]
    if a is not None:
        ap = ap.rearrange("p (a b) -> p a b", a=a)
    return ap


def setup_common(P, C, ins):
    nc = P.nc
    C.ps = [P.es.enter_context(nc.psum_tensor(f"ps{i}", [128, 512], F32)) for i in range(8)]
    C.x = P.sb("x", [128, 16, T], F32)
    C.h = P.sb("h", [128, 16, T], BF16)
    C.ones = P.sb("ones_bf", [128, 128], BF16)
    C.eps = P.sb("eps_t", [128, 1], F32)
    C.small = P.sb("small", [128, SM.n], F32)
    C.modTs = [P.sb(f"modT{i}", [128, 9, 16, 3], F32) for i in range(2)]
    C.aTs = [P.sb(f"aT{i}", [128, 3, 16, 3], F32) for i in range(2)]
    C.gTs = [P.sb(f"gT{i}", [128, 3, 16, 3], F32) for i in range(2)]
    C.par = 0
    C.modT, C.aT, C.gT, C.modr = C.modTs[0], C.aTs[0], C.gTs[0], 'mod0'
    C.condb = P.sb("condb", [128, 16, 3], BF16)
    P.op('dve', 'memset', dict(ap=C.ones[:], constant=1.0), writes=('ones',))
    P.op('dve', 'memset', dict(ap=C.eps[:], constant=EPS), writes=('eps',))
    P.op('sp', 'dma_start', dict(out=C.small[:], in_=ins['small']), writes=('small',), sem='in', inc=16)
    xres = []
    for c in range(16):
        P.op('sp', 'dma_start', dict(out=C.x[:, c, :], in_=ins['xT'][:, c, :]), sem='in', inc=16)
        xres += [f"x{c}_{s}" for s in range(len(SEG))]
    tot = ('in', P.val['in'])
    for r in ['small'] + xres:
        P.res[r] = [tot, {}]
    cT = vv(C.small, SM, 'cT', 16)
    P.op('act', 'activation', dict(out=C.condb[:], in_=cT, func=AF.Silu), reads=('small',), writes=('condb',))


SEG = [(0, 192), (192, 448), (448, 576), (576, 704), (704, 896), (896, 960), (960, 1216), (1216, 1280), (1280, 1344)]


def xr(c, c0, c1, pre='x'):
    return tuple(f"{pre}{c}_{i}" for i, (a, b) in enumerate(SEG) if a < c1 and b > c0)


def hr_all(c0, c1):
    out = ()
    for k in range(16):
        out += xr(k, c0, c1, 'h')
    return out


def norm_mod(P, C, j, tiles, final=None):
    x, h = C.x, C.h
    with ExitStack() as es:
        nc = P.nc
        C.rs = es.enter_context(SBT(nc, "rstd", [128, T], F32))
        sq = [es.enter_context(SBT(nc, f"sq{i}", [128, 448], BF16)) for i in range(3)]
        tmp = [es.enter_context(SBT(nc, f"tmpf{i}", [128, 448], F32)) for i in range(3)]
        yo = [es.enter_context(SBT(nc, f"yo{i}", [128, 448], F32)) for i in range(3)] if final else None
        nsq = ntmp = 0
        for ti, (c0, c1) in enumerate(tiles):
            w = c1 - c0
            bk = P.bank()
            for c in range(16):
                jj = nsq % 3
                nsq += 1
                P.op('act', 'activation', dict(out=sq[jj][:, :w], in_=x[:, c, c0:c1], func=AF.Square),
                     reads=xr(c, c0, c1), writes=(f"sq{jj}",))
                P.op('pe', 'matmul', dict(out=C.ps[bk][:, :w], lhsT=C.ones[:], rhs=sq[jj][:, :w],
                                          start=(c == 0), stop=(c == 15)),
                     reads=(f"sq{jj}", 'ones'), writes=(f"ps{bk}",))
            P.op('act', 'activation', dict(out=C.rs[:, c0:c1], in_=C.ps[bk][:, :w], func=AF.Sqrt,
                                           bias=C.eps[:, 0:1], scale=1.0 / 2048.0),
                 reads=(f"ps{bk}", 'eps'), writes=(f"rs{ti}",))
            P.op('dve', 'reciprocal', dict(out=C.rs[:, c0:c1], in_=C.rs[:, c0:c1]),
                 reads=(f"rs{ti}",), writes=(f"rs{ti}",))
            for c in range(16):
                for (s0, s1, si) in spans_of(c0, c1):
                    jj = ntmp % 3
                    ntmp += 1
                    sw = s1 - s0
                    P.op('dve', 'tensor_tensor', dict(out=tmp[jj][:, :sw], in0=x[:, c, s0:s1], in1=C.rs[:, s0:s1],
                                                      op=ALU.mult),
                         reads=xr(c, s0, s1) + (f"rs{ti}",), writes=(f"tmp{jj}",))
                    if final is None:
                        P.op('act', 'activation', dict(out=h[:, c, s0:s1], in_=tmp[jj][:, :sw], func=AF.Identity,
                                                       bias=C.modT[:, 3 * j, c, si:si + 1],
                                                       scale=C.aT[:, j, c, si:si + 1]),
                             reads=(f"tmp{jj}", C.modr), writes=xr(c, s0, s1, 'h'))
                    else:
                        fg = vv(C.small, SM, 'final_g')
                        P.op('act', 'activation', dict(out=yo[jj][:, :sw], in_=tmp[jj][:, :sw], func=AF.Identity,
                                                       bias=0.0, scale=fg[:, c:c + 1]),
                             reads=(f"tmp{jj}", 'small'), writes=(f"yo{jj}",))
                        P.op('sp', 'dma_start', dict(out=final[:, c, s0 - HALO:s1 - HALO], in_=yo[jj][:, :sw]),
                             reads=(f"yo{jj}",), sem='out', inc=16)
        P.fence()


def ffn(P, C, ws, layer, which, j, tiles, side=None, side_n=14):
    G = 4
    x, h = C.x, C.h
    nc = P.nc
    with ExitStack() as es:
        g = es.enter_context(SBT(nc, "ffn_g", [128, G, T], BF16))
        sl = [es.enter_context(SBT(nc, f"ffn_s{i}", [128, 448], F32)) for i in range(2)]
        nsl = 0
        idx = (layer, which)
        for grp in range(11):
            for fl in range(G):
                f = (grp * G + fl) * 128
                blk1 = [ws.take(128, ('ffn_w1', idx, k * 128, [(f, 128)])) for k in range(16)]
                blk3 = [ws.take(128, ('ffn_w3', idx, k * 128, [(f, 128)])) for k in range(16)]
                for ti, (c0, c1) in enumerate(tiles):
                    w = c1 - c0
                    b1 = P.bank()
                    b3 = P.bank()
                    for (blks, bk) in ((blk1, b1), (blk3, b3)):
                        P.op('pe', 'matmul',
                             [dict(out=C.ps[bk][:, :w], lhsT=blks[k][0], rhs=h[:, k, c0:c1],
                                   start=(k == 0), stop=(k == 15)) for k in range(16)],
                             reads=hr_all(c0, c1) + tuple(set(b[1] for b in blks)), writes=(f"ps{bk}",))
                    jj = nsl % 2
                    nsl += 1
                    P.op('act', 'activation', dict(out=sl[jj][:, :w], in_=C.ps[b1][:, :w], func=AF.Silu),
                         reads=(f"ps{b1}",), writes=(f"sl{jj}",))
                    P.op('dve', 'tensor_tensor', dict(out=g[:, fl, c0:c1], in0=C.ps[b3][:, :w], in1=sl[jj][:, :w],
                                                      op=ALU.mult),
                         reads=(f"ps{b3}", f"sl{jj}"), writes=(f"g{fl}_{ti}",))
                ws.commit()
            for m in range(16):
                blks = [ws.take(128, ('ffn_w2', idx, (grp * G + fl) * 128, [(m * 128, 128)])) for fl in range(G)]
                for ti, (c0, c1) in enumerate(tiles):
                    w = c1 - c0
                    bk = P.bank()
                    P.op('pe', 'matmul',
                         [dict(out=C.ps[bk][:, :w], lhsT=blks[fl][0], rhs=g[:, fl, c0:c1],
                               start=(fl == 0), stop=(fl == G - 1)) for fl in range(G)],
                         reads=tuple(f"g{fl}_{ti}" for fl in range(G)) + tuple(set(b[1] for b in blks)),
                         writes=(f"ps{bk}",))
                    for (s0, s1, si) in spans_of(c0, c1):
                        P.op('dve', 'scalar_tensor_tensor',
                             dict(out=x[:, m, s0:s1], in0=C.ps[bk][:, s0 - c0:s1 - c0],
                                  scalar=C.gT[:, j, m, si:si + 1], in1=x[:, m, s0:s1], op0=ALU.mult, op1=ALU.add),
                             reads=(f"ps{bk}", C.modr) + xr(m, s0, s1), writes=xr(m, s0, s1))
                ws.commit()
            if side is not None:
                for _ in range(side_n):
                    next(side, None)
        if side is not None:
            for _ in side:
                pass
        P.fence()


def ada_gen(P, C, ws, layer):
    par = layer % 2
    modT, aT, gT, mr = C.modTs[par], C.aTs[par], C.gTs[par], f"mod{par}"
    adab = vv(C.small, SM, 'ada_b', 4 * 9)
    ng = vv(C.small, SM, 'norm_g', 12)
    for q in range(9):
        bk = P.reserve(1)[0]
        for c in range(16):
            m = q * 16 + c
            blks = [ws.take(128, ('ada_w', (layer,), k * 128, [(m * 128, 128)])) for k in range(16)]
            P.op('pe', 'matmul',
                 [dict(out=C.ps[bk][:, 3 * c:3 * c + 3], lhsT=blks[k][0], rhs=C.condb[:, k, :],
                       start=(k == 0), stop=(k == 15)) for k in range(16)],
                 reads=('condb',) + tuple(set(b[1] for b in blks)), writes=(f"ps{bk}",))
            ws.commit()
            if c < 15:
                yield
        bb = adab[:, layer * 9 + q, :].unsqueeze(2).to_broadcast([128, 16, 3])
        P.op('dve', 'tensor_tensor', dict(out=modT[:, q], in0=C.ps[bk][:, 0:48].rearrange("p (c s) -> p c s", s=3),
                                          in1=bb, op=ALU.add),
             reads=(f"ps{bk}", 'small'), writes=(mr,))
        P.unreserve([bk])
        yield
    for j in range(3):
        P.op('dve', 'tensor_scalar', dict(out=aT[:, j], in0=modT[:, 3 * j + 1], scalar1=1.0, scalar2=None,
                                          op0=ALU.add), reads=(mr,), writes=(mr,))
        gb = ng[:, layer * 3 + j, :].unsqueeze(2).to_broadcast([128, 16, 3])
        P.op('dve', 'tensor_tensor', dict(out=aT[:, j], in0=aT[:, j], in1=gb, op=ALU.mult),
             reads=(mr, 'small'), writes=(mr,))
        P.op('act', 'activation', dict(out=gT[:, j], in_=modT[:, 3 * j + 2], func=AF.Copy,
                                       scale=(1.0 if j == 1 else 0.5)), reads=(mr,), writes=(mr,))


def use_layer_mod(C, layer):
    par = layer % 2
    C.modT, C.aT, C.gT, C.modr = C.modTs[par], C.aTs[par], C.gTs[par], f"mod{par}"


def linear_fm(P, C, ws, wname, idx, nk, ms, src, src_res, ncols, epi, colmap=None):
    for mi, m in enumerate(ms):
        cm = colmap(m) if colmap else [(m * 128, 128)]
        blks = [ws.take(128, (wname, idx, k * 128, cm)) for k in range(nk)]
        bk = P.bank()
        P.op('pe', 'matmul',
             [dict(out=C.ps[bk][:, :ncols], lhsT=blks[k][0], rhs=src(k), start=(k == 0), stop=(k == nk - 1))
              for k in range(nk)],
             reads=tuple(src_res) + tuple(set(b[1] for b in blks)), writes=(f"ps{bk}",))
        epi(mi, m, bk)
        ws.commit()


def resid_epi(P, C, j, bias, tmp, colspans, names=None):
    cnt = [0]

    def epi(mi, m, bk):
        for (q0, q1, x0, si) in colspans:
            n = q1 - q0
            if bias is None:
                P.op('dve', 'scalar_tensor_tensor',
                     dict(out=C.x[:, m, x0:x0 + n], in0=C.ps[bk][:, q0:q1], scalar=C.gT[:, j, m, si:si + 1],
                          in1=C.x[:, m, x0:x0 + n], op0=ALU.mult, op1=ALU.add),
                     reads=(f"ps{bk}", C.modr) + xr(m, x0, x0 + n), writes=xr(m, x0, x0 + n))
                continue
            jj = cnt[0] % len(tmp)
            cnt[0] += 1
            rn = names[jj] if names else f"rtmp{jj}"
            P.op('act', 'activation', dict(out=tmp[jj][:, :n], in_=C.ps[bk][:, q0:q1], func=AF.Identity,
                                           bias=bias[:, m:m + 1], scale=1.0),
                 reads=(f"ps{bk}", 'mixc'), writes=(rn,))
            P.op('dve', 'scalar_tensor_tensor',
                 dict(out=C.x[:, m, x0:x0 + n], in0=tmp[jj][:, :n], scalar=C.gT[:, j, m, si:si + 1],
                      in1=C.x[:, m, x0:x0 + n], op0=ALU.mult, op1=ALU.add),
                 reads=(rn, C.modr) + xr(m, x0, x0 + n), writes=xr(m, x0, x0 + n))
    return epi

TILES_C = [(192, 448), (448, 704), (704, 960), (960, 1216), (1216, 1344)]


def ext_layout(c0, c1, hl):
    out = []
    e = 0
    for (s0, s1, si) in spans_of(c0, c1):
        out.append((e, s0, s1 - s0, si))
        e += hl + (s1 - s0)
    return out, e


def conv_mixer(P, C, ws, ins, outs):
    nc = P.nc
    HL = 30
    with ExitStack() as es:
        S = lambda name, shape, dt: es.enter_context(SBT(nc, name, shape, dt))
        cvt = S("cvt", [128, CV.n], F32)
        E = S("cvE", [128, 16, 544], BF16)
        halo = S("cvhalo", [128, 16, HL], BF16)
        co32 = S("cvo32", [128, 16, 3, HL], F32)
        tf = [S(f"cvtf{i}", [128, 512], F32) for i in range(3)]
        sqb = [S(f"cvsq{i}", [128, 512], BF16) for i in range(2)]
        rt = [S(f"cvrt{i}", [128, 512], F32) for i in range(2)]
        rtmp = [S(f"cvrtmp{i}", [128, 512], F32) for i in range(2)]
        P.op('sp', 'dma_start', dict(out=cvt[:], in_=ins['cv']), writes=('mixc',), sem='in2', inc=16)
        b1 = vv(cvt, CV, 'b1')
        wdw = vv(cvt, CV, 'wdw', 16)
        bdw, lng, lnb, b2 = (vv(cvt, CV, n) for n in ('bdw', 'lng', 'lnb', 'b2'))
        cmask = vv(cvt, CV, 'coremask')
        ntf = 0
        for ti, (c0, c1) in enumerate(TILES_A):
            w = c1 - c0
            lay, NE = ext_layout(c0, c1, HL)
            NQ = NE - HL
            if ti == 0:
                P.op('dve', 'memset', dict(ap=E[:], constant=0.0), writes=tuple(f"E{c}" for c in range(16)))
            else:
                P.op('act', 'activation', dict(out=E[:, :, 0:HL], in_=halo[:], func=AF.Copy),
                     reads=('cvhalo',), writes=tuple(f"E{c}" for c in range(16)))
            if ti == 2:
                for (e0, x0, n, si) in lay[1:]:
                    P.op('pool', 'dma_start', dict(out=E[:, :, e0:e0 + HL], in_=ins['convc'][:, :, si - 1, :]),
                         writes=tuple(f"E{c}" for c in range(16)), sem='in3', inc=16)
            for c in range(16):
                blka = [ws.take(128, ('conv_w_pw1', (), k * 128, [(c * 128, 128)])) for k in range(16)]
                blkb = [ws.take(128, ('conv_w_pw1', (), k * 128, [(2048 + c * 128, 128)])) for k in range(16)]
                ba, bb = P.bank(), P.bank()
                for blks, bk in ((blka, ba), (blkb, bb)):
                    P.op('pe', 'matmul', [dict(out=C.ps[bk][:, :w], lhsT=blks[k][0], rhs=C.h[:, k, c0:c1],
                                               start=(k == 0), stop=(k == 15)) for k in range(16)],
                         reads=hr_all(c0, c1) + tuple(set(b[1] for b in blks)), writes=(f"ps{bk}",))
                ws.commit()
                jj = ntf % 3
                ntf += 1
                sg = tf[jj]
                P.op('act', 'activation', dict(out=sg[:, :w], in_=C.ps[bb][:, :w], func=AF.Sigmoid,
                                               bias=b1[:, 16 + c:17 + c], scale=1.0),
                     reads=(f"ps{bb}", 'mixc'), writes=(f"cvtf{jj}",))
                for (e0, x0, n, si) in lay:
                    P.op('dve', 'scalar_tensor_tensor',
                         dict(out=E[:, c, e0 + HL:e0 + HL + n], in0=C.ps[ba][:, x0 - c0:x0 - c0 + n],
                              scalar=b1[:, c:c + 1], in1=sg[:, x0 - c0:x0 - c0 + n], op0=ALU.add, op1=ALU.mult),
                         reads=(f"ps{ba}", 'mixc', f"cvtf{jj}"), writes=(f"E{c}",))
                    if ti == 2:
                        P.op('dve', 'scalar_tensor_tensor',
                             dict(out=co32[:, c, si, :], in0=C.ps[ba][:, x0 - c0 + n - HL:x0 - c0 + n],
                                  scalar=b1[:, c:c + 1], in1=sg[:, x0 - c0 + n - HL:x0 - c0 + n],
                                  op0=ALU.add, op1=ALU.mult),
                             reads=(f"ps{ba}", 'mixc', f"cvtf{jj}"), writes=('co32',))
                if ti == 0:
                    P.op('dve', 'tensor_scalar', dict(out=E[:, c, HL:HL + HALO], in0=E[:, c, HL:HL + HALO],
                                                      scalar1=cmask[:, 0:1], scalar2=None, op0=ALU.mult),
                         reads=(f"E{c}", 'mixc'), writes=(f"E{c}",))
            if ti < 2:
                P.op('act', 'activation', dict(out=halo[:], in_=E[:, :, NE - HL:NE], func=AF.Copy),
                     reads=tuple(f"E{c}" for c in range(16)), writes=('cvhalo',))
            s1b, s2b = P.reserve(2)
            for c in range(16):
                jj = ntf % 3
                ntf += 1
                acc = tf[jj]
                P.op('dve', 'tensor_scalar', dict(out=acc[:, :NQ], in0=E[:, c, 0:NQ], scalar1=wdw[:, c, 0:1],
                                                  scalar2=bdw[:, c:c + 1], op0=ALU.mult, op1=ALU.add),
                     reads=(f"E{c}", 'mixc'), writes=(f"cvtf{jj}",))
                for t in range(1, 31):
                    P.op('dve', 'scalar_tensor_tensor',
                         dict(out=acc[:, :NQ], in0=E[:, c, t:t + NQ], scalar=wdw[:, c, t:t + 1], in1=acc[:, :NQ],
                              op0=ALU.mult, op1=ALU.add),
                         reads=(f"E{c}", 'mixc', f"cvtf{jj}"), writes=(f"cvtf{jj}",))
                P.op('act', 'activation', dict(out=E[:, c, 0:NQ], in_=acc[:, :NQ], func=AF.Copy),
                     reads=(f"cvtf{jj}",), writes=(f"E{c}",))
                sj = c % 2
                P.op('act', 'activation', dict(out=sqb[sj][:, :NQ], in_=acc[:, :NQ], func=AF.Square),
                     reads=(f"cvtf{jj}",), writes=(f"cvsq{sj}",))
                P.op('pe', 'matmul', dict(out=C.ps[s1b][:, :NQ], lhsT=C.ones[:], rhs=E[:, c, 0:NQ],
                                          start=(c == 0), stop=(c == 15)),
                     reads=(f"E{c}", 'ones'), writes=(f"ps{s1b}",))
                P.op('pe', 'matmul', dict(out=C.ps[s2b][:, :NQ], lhsT=C.ones[:], rhs=sqb[sj][:, :NQ],
                                          start=(c == 0), stop=(c == 15)),
                     reads=(f"cvsq{sj}", 'ones'), writes=(f"ps{s2b}",))
            mean, rstd = rt
            P.op('act', 'activation', dict(out=mean[:, :NQ], in_=C.ps[s1b][:, :NQ], func=AF.Copy, scale=1.0 / 2048),
                 reads=(f"ps{s1b}",), writes=('cvmean',))
            msq = tf[ntf % 3]
            mj = ntf % 3
            ntf += 1
            P.op('dve', 'tensor_tensor', dict(out=msq[:, :NQ], in0=mean[:, :NQ], in1=mean[:, :NQ], op=ALU.mult),
                 reads=('cvmean',), writes=(f"cvtf{mj}",))
            P.op('dve', 'scalar_tensor_tensor', dict(out=rstd[:, :NQ], in0=C.ps[s2b][:, :NQ], scalar=1.0 / 2048,
                                                     in1=msq[:, :NQ], op0=ALU.mult, op1=ALU.subtract),
                 reads=(f"ps{s2b}", f"cvtf{mj}"), writes=('cvrstd',))
            P.op('act', 'activation', dict(out=rstd[:, :NQ], in_=rstd[:, :NQ], func=AF.Sqrt, bias=C.eps[:, 0:1],
                                           scale=1.0), reads=('cvrstd', 'eps'), writes=('cvrstd',))
            P.op('dve', 'reciprocal', dict(out=rstd[:, :NQ], in_=rstd[:, :NQ]), reads=('cvrstd',), writes=('cvrstd',))
            P.unreserve([s1b, s2b])
            for c in range(16):
                jj = ntf % 3
                ntf += 1
                t1 = tf[jj]
                P.op('dve', 'tensor_tensor', dict(out=t1[:, :NQ], in0=E[:, c, 0:NQ], in1=mean[:, :NQ],
                                                  op=ALU.subtract),
                     reads=(f"E{c}", 'cvmean'), writes=(f"cvtf{jj}",))
                P.op('dve', 'tensor_tensor', dict(out=t1[:, :NQ], in0=t1[:, :NQ], in1=rstd[:, :NQ], op=ALU.mult),
                     reads=(f"cvtf{jj}", 'cvrstd'), writes=(f"cvtf{jj}",))
                P.op('act', 'activation', dict(out=E[:, c, 0:NQ], in_=t1[:, :NQ], func=AF.Silu,
                                               bias=lnb[:, c:c + 1], scale=lng[:, c:c + 1]),
                     reads=(f"cvtf{jj}", 'mixc'), writes=(f"E{c}",))
            linear_fm(P, C, ws, 'conv_w_pw2', (), 16, range(16), lambda k: E[:, k, 0:NQ],
                      [f"E{c}" for c in range(16)], NQ,
                      resid_epi(P, C, 1, b2, rtmp, [(e0, e0 + n, x0, si) for (e0, x0, n, si) in lay]))
        P.op('sp', 'dma_start', dict(out=outs['convo'], in_=co32[:]), reads=('co32',), sem='out', inc=16)
        P.fence()


def sqb_f32(rt, tf):
    return [tf[0], tf[1]]


def pool_mixer(P, C, ws, ins, outs):
    nc = P.nc
    HL = 15
    with ExitStack() as es:
        S = lambda name, shape, dt: es.enter_context(SBT(nc, name, shape, dt))
        plt = S("plt", [128, PL.n], F32)
        Pb = S("plPb", [128, 16, 480], BF16)
        Ec = [S(f"plE{i}", [128, 512], F32) for i in range(2)]
        B = [S(f"plB{i}", [128, 512], F32) for i in range(2)]
        halo = S("plhalo", [128, 16, HL], F32)
        po32 = S("plo32", [128, 16, 3, HL], F32)
        cch = S("plcch", [128, 16, 2, HL], F32)
        rtmp = [S(f"plrt{i}", [128, 512], F32) for i in range(2)]
        P.op('sp', 'dma_start', dict(out=plt[:], in_=ins['pl']), writes=('mixc',), sem='in2', inc=16)
        P.op('sp', 'dma_start', dict(out=cch[:], in_=ins['poolc']), writes=('plcch',), sem='in3', inc=16)
        scale = vv(plt, PL, 'scale')
        cmask = vv(plt, PL, 'coremask')
        pcorr = vv(plt, PL, 'pcorr', 4)
        for i in range(2):
            P.op('dve', 'memset', dict(ap=B[i][:], constant=0.0), writes=(f"plB{i}",))
            P.op('dve', 'memset', dict(ap=Ec[i][:], constant=0.0), writes=(f"plE{i}",))
        for ti, (c0, c1) in enumerate(TILES_A):
            w = c1 - c0
            lay, NE = ext_layout(c0, c1, HL)
            NQ = NE - HL
            for c in range(16):
                gi = c // 4
                win = 2 << gi
                ej = c % 2
                e = Ec[ej]
                blks = [ws.take(128, ('pool_w_in', (), k * 128, [(c * 128, 128)])) for k in range(16)]
                bk = P.bank()
                P.op('pe', 'matmul', [dict(out=C.ps[bk][:, :w], lhsT=blks[k][0], rhs=C.h[:, k, c0:c1],
                                           start=(k == 0), stop=(k == 15)) for k in range(16)],
                     reads=hr_all(c0, c1) + tuple(set(b[1] for b in blks)), writes=(f"ps{bk}",))
                ws.commit()
                for (e0, x0, n, si) in lay:
                    P.op('act', 'activation', dict(out=e[:, e0 + HL:e0 + HL + n], in_=C.ps[bk][:, x0 - c0:x0 - c0 + n],
                                                   func=AF.Copy), reads=(f"ps{bk}",), writes=(f"plE{ej}",))
                    if si == 0:
                        if ti == 0:
                            P.op('dve', 'memset', dict(ap=e[:, 0:HL], constant=0.0), writes=(f"plE{ej}",))
                            P.op('dve', 'tensor_scalar', dict(out=e[:, HL:HL + HALO], in0=e[:, HL:HL + HALO],
                                                              scalar1=cmask[:, 0:1], scalar2=None, op0=ALU.mult),
                                 reads=(f"plE{ej}", 'mixc'), writes=(f"plE{ej}",))
                        else:
                            P.op('act', 'activation', dict(out=e[:, 0:HL], in_=halo[:, c, :], func=AF.Copy),
                                 reads=('plhalo%d' % c,), writes=(f"plE{ej}",))
                    else:
                        P.op('act', 'activation', dict(out=e[:, e0:e0 + HL], in_=cch[:, c, si - 1, :], func=AF.Copy),
                             reads=('plcch',), writes=(f"plE{ej}",))
                    if ti == 2:
                        P.op('act', 'activation', dict(out=po32[:, c, si, :], in_=e[:, e0 + n:e0 + n + HL],
                                                       func=AF.Copy), reads=(f"plE{ej}",), writes=('po32',))
                if ti < 2:
                    P.op('act', 'activation', dict(out=halo[:, c, :], in_=e[:, NE - HL:NE], func=AF.Copy),
                         reads=(f"plE{ej}",), writes=('plhalo%d' % c,))
                cur, curr = e, f"plE{ej}"
                d = 1
                pj = 0
                while d < win:
                    nb = B[pj]
                    P.op('dve', 'tensor_tensor', dict(out=nb[:, d:NE], in0=cur[:, d:NE], in1=cur[:, 0:NE - d],
                                                      op=ALU.add), reads=(curr,), writes=(f"plB{pj}",))
                    cur, curr = nb, f"plB{pj}"
                    pj ^= 1
                    d *= 2
                if ti == 0:
                    q0 = HL + HALO
                    P.op('dve', 'tensor_tensor', dict(out=cur[:, q0:q0 + 16], in0=cur[:, q0:q0 + 16],
                                                      in1=pcorr[:, gi, :], op=ALU.mult),
                         reads=(curr, 'mixc'), writes=(curr,))
                P.op('dve', 'scalar_tensor_tensor', dict(out=Pb[:, c, 0:NQ], in0=cur[:, HL:NE], scalar=1.0 / win,
                                                         in1=e[:, HL:NE], op0=ALU.mult, op1=ALU.subtract),
                     reads=(curr, f"plE{ej}"), writes=(f"Pb{c}",))
            for gi in range(4):
                def epi(mi, m, bk, gi=gi):
                    cc = 4 * gi + m
                    for (e0, x0, n, si) in lay:
                        P.op('act', 'activation', dict(out=C.h[:, cc, x0:x0 + n], in_=C.ps[bk][:, e0:e0 + n],
                                                       func=AF.Identity, bias=0.0, scale=scale[:, cc:cc + 1]),
                             reads=(f"ps{bk}", 'mixc'), writes=xr(cc, x0, x0 + n, 'h'))
                linear_fm(P, C, ws, 'pool_w_grp', (gi,), 4, range(4), lambda k, gi=gi: Pb[:, 4 * gi + k, 0:NQ],
                          [f"Pb{4 * gi + k}" for k in range(4)], NQ, epi)
            linear_fm(P, C, ws, 'pool_w_out', (), 16, range(16), lambda k: C.h[:, k, c0:c1], hr_all(c0, c1), w,
                      resid_epi(P, C, 1, None, rtmp, [(s0 - c0, s1 - c0, s0, si) for (s0, s1, si) in spans_of(c0, c1)]))
        P.op('sp', 'dma_start', dict(out=outs['poolo'], in_=po32[:]), reads=('po32',), sem='out', inc=16)
        P.fence()

def swa_mixer(P, C, ws, ins, outs):
    nc = P.nc
    import os
    SK = os.environ.get('SWA_SKIP', '').split(',')
    with ExitStack() as es:
        S = lambda name, shape, dt: es.enter_context(SBT(nc, name, shape, dt))
        att = S("att", [128, AT.n], F32)
        sink8 = S("sink8", [128, 32], F32)
        kTd = S("kTd", [128, 4, 384], BF16)
        Vt = S("Vt", [64, 6, 4, 128], BF16)
        o = S("att_o", [128, 16, 256], BF16)
        qz = [S(f"qz{i}", [128, 256], BF16) for i in range(4)]
        NSET = 8
        tt = [S(f"att_t{i}", [64, 196], F32) for i in range(NSET)]
        pT = [S(f"att_pT{i}", [64, 192], BF16) for i in range(NSET)]
        st = [S(f"att_st{i}", [64, 4], F32) for i in range(NSET)]
        kvo = [S(f"att_kvo{i}", [64, 512], F32) for i in range(2)]
        rtmp = [S(f"att_rt{i}", [128, 256], F32) for i in range(2)]
        P.op('sp', 'dma_start', dict(out=att[:], in_=ins['at']), writes=('mixc',), sem='in2', inc=16)
        sinks = vv(att, AT, 'sinks')
        dist = vv(att, AT, 'dist')
        amask = vv(att, AT, 'amask', 2)
        ident = vv(att, AT, 'ident')
        P.op('act', 'activation', dict(out=sink8[:], in_=sinks, func=AF.Copy, scale=1.0 / ATT_SCALE),
             reads=('mixc',), writes=('sink8',))
        for i in range(4):
            P.op('dve', 'memset', dict(ap=qz[i][:], constant=0.0), writes=(f"qz{i}",))
        slopes = [2.0 ** (-8.0 * (hd + 1) / 32.0) for hd in range(32)]
        nq = 0
        nst = 0
        nkvo = 0
        for ti, (c0, c1) in enumerate(TILES_C):
            w = c1 - c0
            sample = (ti == 4)
            if not sample:
                kc0, kc1 = c0 - 128, c1
                kdst = [(0, 0, 384)]
            else:
                kc0, kc1 = c0, c1
                kdst = [(0, 128, 64), (64, 320, 64)]
                for s in range(2):
                    if 'kcdma' in SK:
                        continue
                    P.op('pool', 'dma_start', dict(out=kTd[:, :, s * 192:s * 192 + 128], in_=ins['kcT'][:, s]),
                         writes=tuple(f"kTd{kv}" for kv in range(4)), sem='in3', inc=16)
            nkc = kc1 - kc0

            def kepi(mi, m, bk):
                for (pc, kc, n) in kdst:
                    P.op('act', 'activation', dict(out=kTd[:, m, kc:kc + n], in_=C.ps[bk][:, pc:pc + n], func=AF.Copy),
                         reads=(f"ps{bk}",), writes=(f"kTd{m}",))
            if 'kt' not in SK:
              linear_fm(P, C, ws, 'swa_wk', (), 16, range(4), lambda k: C.h[:, k, kc0:kc1], hr_all(kc0, kc1), nkc, kepi,
                      colmap=lambda m: [(m * 64, 64), (m * 64, 64)])
            if not sample:
                blocks = [(b, kc0 + 64 * b) for b in range(6)]
            else:
                blocks = [(2, 1216), (5, 1280)]
                for s in range(2):
                    for bb in range(2):
                        src = ins['vc'][s, bb * 64:(bb + 1) * 64, :].rearrange("p (k d) -> p k d", k=4)
                        for half in range(2):
                            if 'vcdma' in SK:
                                continue
                            P.op('pool', 'dma_start', dict(out=Vt[:, 3 * s + bb, :, half * 64:(half + 1) * 64], in_=src),
                                 writes=(f"Vt{3 * s + bb}",), sem='in3', inc=16)
            if 'kv' in SK:
                blocks = []
            kvblks = [ws.take(512, ('swa_wkv', (), k * 128, [(0, 512)])) for k in range(16)] if blocks else []
            for i, (b, xc) in enumerate(blocks):
                bk = P.bank()
                P.op('pe', 'matmul', [dict(out=C.ps[bk][0:64, 0:512], lhsT=C.h[:, k, xc:xc + 64], rhs=kvblks[k][0],
                                           start=(k == 0), stop=(k == 15)) for k in range(16)],
                     reads=hr_all(xc, xc + 64) + tuple(set(bb_[1] for bb_ in kvblks)), writes=(f"ps{bk}",))
                vsrc = C.ps[bk][0:64, 256:512].rearrange("p (k d) -> p k d", k=4)
                for half in range(2):
                    if 'kvepi' in SK:
                        continue
                    P.op('act', 'activation', dict(out=Vt[:, b, :, half * 64:(half + 1) * 64], in_=vsrc, func=AF.Copy),
                         reads=(f"ps{bk}",), writes=(f"Vt{b}",))
                orow = None
                if sample:
                    orow = (1 + i, 64)
                elif xc >= 1088 and ti == 3:
                    orow = (0, xc - 1088)
                if orow is not None and 'kvout' not in SK:
                    jj = nkvo % 2
                    nkvo += 1
                    P.op('act', 'activation', dict(out=kvo[jj][:], in_=C.ps[bk][0:64, 0:512], func=AF.Copy),
                         reads=(f"ps{bk}",), writes=(f"kvo{jj}",))
                    P.op('sp', 'dma_start', dict(out=outs['ko'][orow[0], orow[1]:orow[1] + 64, :], in_=kvo[jj][:, 0:256]),
                         reads=(f"kvo{jj}",), sem='out', inc=16)
                    P.op('sp', 'dma_start', dict(out=outs['vo'][orow[0], orow[1]:orow[1] + 64, :], in_=kvo[jj][:, 256:512]),
                         reads=(f"kvo{jj}",), sem='out', inc=16)
            ws.commit()
            nch = w // 64
            for m in range(16 if 'q' not in SK else 0):
                qa, qb = qz[(nq % 2) * 2], qz[(nq % 2) * 2 + 1]
                qra, qrb = f"qz{(nq % 2) * 2}", f"qz{(nq % 2) * 2 + 1}"
                nq += 1

                def qepi(mi, mm, bk, qa=qa, qb=qb, qra=qra, qrb=qrb):
                    P.op('act', 'activation', dict(out=qa[0:64, :w], in_=C.ps[bk][0:64, :w], func=AF.Copy),
                         reads=(f"ps{bk}",), writes=(qra,))
                    P.op('act', 'activation', dict(out=qb[64:128, :w], in_=C.ps[bk][64:128, :w], func=AF.Copy),
                         reads=(f"ps{bk}",), writes=(qrb,))
                linear_fm(P, C, ws, 'swa_wq', (), 16, [m], lambda k: C.h[:, k, c0:c1], hr_all(c0, c1), w, qepi)
                chains = [(hh, n) for hh in range(2) for n in range(nch)] if 'attn' not in SK else []
                info = []
                for ci, (hh, n) in enumerate(chains):
                    head = 2 * m + hh
                    kv = head // 8
                    qt, qr = (qa, qra) if hh == 0 else (qb, qrb)
                    koff = 64 * n if not sample else 192 * n
                    bs_ = P.bank()
                    P.op('pe', 'matmul', dict(out=C.ps[bs_][0:64, 0:192], lhsT=qt[:, 64 * n:64 * n + 64],
                                              rhs=kTd[:, kv, koff:koff + 192], start=True, stop=True),
                         reads=(qr, f"kTd{kv}"), writes=(f"ps{bs_}",))
                    info.append((hh, n, head, kv, koff // 64, bs_))
                for ci, (hh, n, head, kv, kb, bs_) in enumerate(info):
                    t = tt[ci]
                    P.op('dve', 'scalar_tensor_tensor', dict(out=t[:, 0:192], in0=dist[0:64, :],
                                                             scalar=-slopes[head] / ATT_SCALE,
                                                             in1=C.ps[bs_][0:64, 0:192], op0=ALU.mult, op1=ALU.add),
                         reads=('mixc', f"ps{bs_}"), writes=(f"att_t{ci}",))
                    if ti == 0 and n < 2:
                        P.op('dve', 'tensor_tensor', dict(out=t[:, 0:192], in0=t[:, 0:192], in1=amask[0:64, n, :],
                                                          op=ALU.add),
                             reads=('mixc', f"att_t{ci}"), writes=(f"att_t{ci}",))
                for ci, (hh, n, head, kv, kb, bs_) in enumerate(info):
                    P.op('act', 'activation', dict(out=tt[ci][:, 192:193], in_=sink8[0:64, head:head + 1], func=AF.Copy),
                         reads=('sink8', f"att_t{ci}"), writes=(f"att_t{ci}",))
                for ci in range(len(info)):
                    P.op('dve', 'reduce_max', dict(out=st[ci][:, 0:1], in_=tt[ci][:, 0:193], axis=AX.X),
                         reads=(f"att_t{ci}",), writes=(f"att_st{ci}",))
                    P.op('dve', 'tensor_scalar', dict(out=st[ci][:, 1:2], in0=st[ci][:, 0:1], scalar1=-ATT_SCALE,
                                                      scalar2=None, op0=ALU.mult),
                         reads=(f"att_st{ci}",), writes=(f"att_st{ci}",))
                for ci in range(len(info)):
                    P.op('act', 'activation', dict(out=tt[ci][:, 0:193], in_=tt[ci][:, 0:193], func=AF.Exp,
                                                   bias=st[ci][:, 1:2], scale=ATT_SCALE, accum_out=st[ci][:, 2:3]),
                         reads=(f"att_t{ci}", f"att_st{ci}"), writes=(f"att_t{ci}", f"att_st{ci}"))
                for ci in range(len(info)):
                    P.op('dve', 'reciprocal', dict(out=st[ci][:, 3:4], in_=st[ci][:, 2:3]),
                         reads=(f"att_st{ci}",), writes=(f"att_st{ci}",))
                    P.op('dve', 'tensor_scalar', dict(out=tt[ci][:, 0:192], in0=tt[ci][:, 0:192], scalar1=st[ci][:, 3:4],
                                                      scalar2=None, op0=ALU.mult),
                         reads=(f"att_t{ci}", f"att_st{ci}"), writes=(f"att_t{ci}",))
                bts = []
                for ci in range(len(info)):
                    bt = P.bank()
                    bts.append(bt)
                    P.op('pe', 'transpose', [dict(out=C.ps[bt][0:64, 64 * j:64 * j + 64], in_=tt[ci][:, 64 * j:64 * j + 64],
                                                  identity=ident[0:64, 0:64]) for j in range(3)],
                         reads=(f"att_t{ci}", 'mixc'), writes=(f"ps{bt}",))
                for ci in range(len(info)):
                    P.op('act', 'activation', dict(out=pT[ci][:, :], in_=C.ps[bts[ci]][0:64, 0:192], func=AF.Copy),
                         reads=(f"ps{bts[ci]}",), writes=(f"att_pT{ci}",))
                bos = []
                for ci, (hh, n, head, kv, kb, bs_) in enumerate(info):
                    bo = P.bank()
                    bos.append(bo)
                    P.op('pe', 'matmul', [dict(out=C.ps[bo][:, 0:64], lhsT=Vt[:, kb + j, kv, :],
                                               rhs=pT[ci][:, 64 * j:64 * j + 64], start=(j == 0), stop=(j == 2))
                                          for j in range(3)],
                         reads=(f"att_pT{ci}",) + tuple(f"Vt{kb + j}" for j in range(3)), writes=(f"ps{bo}",))
                for ci, (hh, n, head, kv, kb, bs_) in enumerate(info):
                    lo = 64 * hh
                    P.op('act', 'activation', dict(out=o[lo:lo + 64, m, 64 * n:64 * n + 64],
                                                   in_=C.ps[bos[ci]][lo:lo + 64, 0:64], func=AF.Copy),
                         reads=(f"ps{bos[ci]}",), writes=(f"o{m}",))
            if 'wo' not in SK:
              linear_fm(P, C, ws, 'swa_wo', (), 16, range(16), lambda k: o[:, k, 0:w], [f"o{k}" for k in range(16)], w,
                      resid_epi(P, C, 1, None, rtmp, [(s0 - c0, s1 - c0, s0, si) for (s0, s1, si) in spans_of(c0, c1)]))
        for s in range(2):
            if 'd2d' in SK:
                continue
            P.op('sp', 'dma_start', dict(out=outs['ko'][1 + s, 0:64, :], in_=ins['kc'][s, 64:128, :]), sem='out', inc=16)
            P.op('sp', 'dma_start', dict(out=outs['vo'][1 + s, 0:64, :], in_=ins['vc'][s, 64:128, :]), sem='out', inc=16)
        P.fence()

def gelu_tanh(P, dst, src, n, tmps, tnames, src_res, dst_res, bias=None, bias_res=()):
    xx, a = tmps
    rx, ra = tnames
    pr = src.shape[0]
    if isinstance(bias, tuple):
        P.op('act', 'activation', dict(out=xx[:pr, :n], in_=src, func=AF.Identity, bias=bias[1], scale=1.0),
             reads=tuple(src_res) + tuple(bias_res), writes=(rx,))
    else:
        P.op('dve', 'tensor_tensor', dict(out=xx[:pr, :n], in0=src, in1=bias, op=ALU.add),
             reads=tuple(src_res) + tuple(bias_res), writes=(rx,))
    P.op('act', 'activation', dict(out=a[:pr, :n], in_=xx[:pr, :n], func=AF.Square, scale=0.044715 ** 0.5),
         reads=(rx,), writes=(ra,))
    P.op('dve', 'scalar_tensor_tensor', dict(out=a[:pr, :n], in0=a[:pr, :n], scalar=1.0, in1=xx[:pr, :n],
                                             op0=ALU.add, op1=ALU.mult), reads=(ra, rx), writes=(ra,))
    P.op('act', 'activation', dict(out=a[:pr, :n], in_=a[:pr, :n], func=AF.Sigmoid, scale=1.5957691216057308),
         reads=(ra,), writes=(ra,))
    P.op('dve', 'tensor_tensor', dict(out=dst, in0=xx[:pr, :n], in1=a[:pr, :n], op=ALU.mult), reads=(rx, ra), writes=tuple(dst_res))


def gmlp_mixer(P, C, ws, ins, outs):
    nc = P.nc
    with ExitStack() as es:
        S = lambda name, shape, dt: es.enter_context(SBT(nc, name, shape, dt))
        gmt = S("gmt", [128, 176], F32)
        wsb = S("gm_ws", [128, 8, 128], BF16)
        bsx = S("gm_bs", [128, 8, 128], F32)
        vg = S("gm_vg", [128, 2, 4096], BF16)
        ug = S("gm_ug", [128, 4, 256], BF16)
        um = S("gm_um", [128, 4, 256], BF16)
        bc = S("gm_bc", [128, 3, 512], F32)
        tm = [S(f"gm_t{i}", [128, 512], F32) for i in range(6)]
        gsel = [0]
        stt = S("gm_st", [128, 2, 24], F32)
        P.op('sp', 'dma_start', dict(out=gmt[:], in_=ins['gm']), writes=('mixc',), sem='in2', inc=16)
        bin_u = gmt[:, 0:32]
        bout = gmt[:, 32:48]
        gmask = gmt[:, 48:176].unsqueeze(1).to_broadcast([128, 4, 128])
        P.new_sem('bc')
        for ti, (c0, c1) in enumerate(TILES_C):
            w = c1 - c0
            sample = (ti == 4)
            ntc = 1 if sample else 2
            if ti == 0 or sample:
                kind = 1 if sample else 0
                P.op('sp', 'dma_start', dict(out=tm[4][:, 0:512], in_=ins['gmws'][kind, :, 0:512]), writes=('gm_t4',), sem='in2', inc=16)
                if not sample:
                    P.op('dve', 'tensor_tensor', dict(out=tm[4][:, 0:512].rearrange("p (g t) -> p g t", g=4),
                                                      in0=tm[4][:, 0:512].rearrange("p (g t) -> p g t", g=4), in1=gmask, op=ALU.mult),
                         reads=('gm_t4', 'mixc'), writes=('gm_t4',))
                P.op('act', 'activation', dict(out=wsb[:, 0:4, :], in_=tm[4][:, 0:512].rearrange("p (g t) -> p g t", g=4), func=AF.Copy),
                     reads=('gm_t4',), writes=('gm_ws',))
                P.op('sp', 'dma_start', dict(out=tm[4][:, 0:512], in_=ins['gmws'][kind, :, 512:1024]), reads=(), writes=('gm_t4',), sem='in2', inc=16)
                if not sample:
                    P.op('dve', 'tensor_tensor', dict(out=tm[4][:, 0:512].rearrange("p (g t) -> p g t", g=4),
                                                      in0=tm[4][:, 0:512].rearrange("p (g t) -> p g t", g=4), in1=gmask, op=ALU.mult),
                         reads=('gm_t4', 'mixc'), writes=('gm_t4',))
                P.op('act', 'activation', dict(out=wsb[:, 4:8, :], in_=tm[4][:, 0:512].rearrange("p (g t) -> p g t", g=4), func=AF.Copy),
                     reads=('gm_t4',), writes=('gm_ws',))
                P.op('sp', 'dma_start', dict(out=bsx[:], in_=ins['gmbs'][kind]), writes=('gm_bs',), sem='in2', inc=16)
            for cg in range(8):
                P.op('sp', 'dma_start', dict(out=bc[:], in_=ins['gmbc'][:, cg]), writes=('gm_bc',), sem='bc', inc=16)
                vblks = [ws.take(512, ('gmlp_w_in', (), k * 128, [(4096 + cg * 512, 512)])) for k in range(16)]
                bks = []
                for tc in range(ntc):
                    bkv = P.bank()
                    bks.append(bkv)
                    P.op('pe', 'matmul', [dict(out=C.ps[bkv][:, 0:512], lhsT=C.h[:, k, c0 + 128 * tc:c0 + 128 * tc + 128],
                                               rhs=vblks[k][0], start=(k == 0), stop=(k == 15)) for k in range(16)],
                         reads=hr_all(c0, c1) + tuple(set(b_[1] for b_ in vblks)), writes=(f"ps{bkv}",))
                ws.commit()
                for tc in range(ntc):
                    gs = gsel[0] % 2
                    gsel[0] += 1
                    vd, vdn = tm[4 + gs], f"gm_t{4 + gs}"
                    gelu_tanh(P, vd[:, 0:512], C.ps[bks[tc]][:, 0:512], 512, (tm[2 * gs], tm[2 * gs + 1]),
                              (f"gm_t{2 * gs}", f"gm_t{2 * gs + 1}"), (f"ps{bks[tc]}",), (vdn,), bias=bc[:, 0, :], bias_res=('gm_bc',))
                    P.op('act', 'activation', dict(out=vg[:, tc, cg * 512:(cg + 1) * 512], in_=vd[:, 0:512], func=AF.Copy,
                                                   accum_out=stt[:, tc, cg:cg + 1]),
                         reads=(vdn,), writes=(f"vg{tc}", 'gm_st'))
                    P.op('act', 'activation', dict(out=tm[2 * gs + 1][:, 0:512], in_=vd[:, 0:512], func=AF.Square,
                                                   accum_out=stt[:, tc, 8 + cg:9 + cg]),
                         reads=(vdn,), writes=(f"gm_t{2 * gs + 1}", 'gm_st'))
            for tc in range(ntc):
                s = stt[:, tc]
                rr = ('gm_st',)
                P.op('dve', 'reduce_sum', dict(out=s[:, 16:17], in_=s[:, 0:8], axis=AX.X), reads=rr, writes=rr)
                P.op('dve', 'reduce_sum', dict(out=s[:, 17:18], in_=s[:, 8:16], axis=AX.X), reads=rr, writes=rr)
                P.op('dve', 'tensor_scalar', dict(out=s[:, 16:18], in0=s[:, 16:18], scalar1=1.0 / 4096, scalar2=None, op0=ALU.mult),
                     reads=rr, writes=rr)
                P.op('dve', 'tensor_tensor', dict(out=s[:, 18:19], in0=s[:, 16:17], in1=s[:, 16:17], op=ALU.mult), reads=rr, writes=rr)
                P.op('dve', 'tensor_tensor', dict(out=s[:, 19:20], in0=s[:, 17:18], in1=s[:, 18:19], op=ALU.subtract), reads=rr, writes=rr)
                P.op('act', 'activation', dict(out=s[:, 20:21], in_=s[:, 19:20], func=AF.Sqrt, bias=C.eps[:, 0:1], scale=1.0),
                     reads=rr + ('eps',), writes=rr)
                P.op('dve', 'reciprocal', dict(out=s[:, 21:22], in_=s[:, 20:21]), reads=rr, writes=rr)
            for cg in range(8):
                P.op('sp', 'dma_start', dict(out=bc[:], in_=ins['gmbc'][:, cg]), writes=('gm_bc',), sem='bc', inc=16)
                for tc in range(ntc):
                    s = stt[:, tc]
                    vsl = vg[:, tc, cg * 512:(cg + 1) * 512]
                    P.op('dve', 'tensor_scalar', dict(out=tm[0][:, 0:512], in0=vsl, scalar1=s[:, 16:17], scalar2=s[:, 21:22],
                                                      op0=ALU.subtract, op1=ALU.mult),
                         reads=(f"vg{tc}", 'gm_st'), writes=('gm_t0',))
                    P.op('dve', 'tensor_tensor', dict(out=tm[0][:, 0:512], in0=tm[0][:, 0:512], in1=bc[:, 1, :], op=ALU.mult),
                         reads=('gm_t0', 'gm_bc'), writes=('gm_t0',))
                    P.op('dve', 'tensor_tensor', dict(out=tm[1][:, 0:512], in0=tm[0][:, 0:512], in1=bc[:, 2, :], op=ALU.add),
                         reads=('gm_t0', 'gm_bc'), writes=('gm_t1',))
                    P.op('act', 'activation', dict(out=vsl, in_=tm[1][:, 0:512], func=AF.Copy), reads=('gm_t1',), writes=(f"vg{tc}",))
                    if sample:
                        P.op('sp', 'dma_start', dict(out=outs['gvo'][:, cg * 512:(cg + 1) * 512], in_=tm[1][:, 0:512]),
                             reads=('gm_t1',), sem='out', inc=16)
            spans = [(s0 - c0, s1 - c0, s0, si) for (s0, s1, si) in spans_of(c0, c1)]
            for g in range(8):
                def uepi(mi, m, bk):
                    gs = gsel[0] % 2
                    gsel[0] += 1
                    gelu_tanh(P, ug[:, mi, 0:w], C.ps[bk][:, 0:w], w, (tm[2 * gs], tm[2 * gs + 1]),
                              (f"gm_t{2 * gs}", f"gm_t{2 * gs + 1}"),
                              (f"ps{bk}",), (f"ug{mi}",), bias=('pp', bin_u[:, m:m + 1]), bias_res=('mixc',))
                linear_fm(P, C, ws, 'gmlp_w_in', (), 16, [4 * g + i for i in range(4)], lambda k: C.h[:, k, c0:c1],
                          hr_all(c0, c1), w, uepi)
                for cc in range(4):
                    f0 = g * 512 + cc * 128
                    for tc in range(ntc):
                        bk = P.bank()
                        P.op('pe', 'matmul', dict(out=C.ps[bk][:, 0:128], lhsT=vg[:, tc, f0:f0 + 128], rhs=wsb[:, g, :],
                                                  start=True, stop=True),
                             reads=(f"vg{tc}", 'gm_ws'), writes=(f"ps{bk}",))
                        mj = 4 + (gsel[0] % 2)
                        gsel[0] += 1
                        P.op('dve', 'tensor_tensor', dict(out=tm[mj][:, 0:128], in0=C.ps[bk][:, 0:128], in1=bsx[:, g, :], op=ALU.add),
                             reads=(f"ps{bk}", 'gm_bs'), writes=(f"gm_t{mj}",))
                        P.op('dve', 'tensor_tensor', dict(out=um[:, cc, 128 * tc:128 * tc + 128], in0=tm[mj][:, 0:128],
                                                          in1=ug[:, cc, 128 * tc:128 * tc + 128], op=ALU.mult),
                             reads=(f"gm_t{mj}", f"ug{cc}"), writes=(f"um{cc}",))
                colmap = lambda m: [(m * 128, 128)]
                for m in range(16):
                    blks = [ws.take(128, ('gmlp_w_out', (), g * 512 + cc * 128, [(m * 128, 128)])) for cc in range(4)]
                    bk = P.bank()
                    P.op('pe', 'matmul', [dict(out=C.ps[bk][:, :w], lhsT=blks[cc][0], rhs=um[:, cc, 0:w],
                                               start=(cc == 0), stop=(cc == 3)) for cc in range(4)],
                         reads=tuple(f"um{cc}" for cc in range(4)) + tuple(set(b[1] for b in blks)), writes=(f"ps{bk}",))
                    ws.commit()
                    resid_epi(P, C, 1, bout if g == 0 else None, tm[0:2], spans, names=('gm_t0', 'gm_t1'))(m, m, bk)
        P.fence()

IN_SPECS = {
    'xT': [128, 16, T], 'small': [128, SM.n], 'cv': [128, CV.n], 'pl': [128, PL.n], 'at': [128, AT.n],
    'gm': [128, 176], 'convc': [128, 16, 2, 30], 'poolc': [128, 16, 2, 15], 'kcT': [128, 2, 4, 128],
    'kc': [2, 128, 256], 'vc': [2, 128, 256], 'gmws': [2, 128, 1024], 'gmbs': [2, 128, 8, 128],
    'gmbc': [128, 8, 3, 512],
}
OUT_SPECS = {
    'yT': [128, 16, 1152], 'convo': [128, 16, 3, 30], 'poolo': [128, 16, 3, 15],
    'ko': [3, 128, 256], 'vo': [3, 128, 256], 'gvo': [128, 4096],
}


def build(npieces, dry=False, nlayers=DEPTH, stages=None):
    nc = bass.Bass("TRN2", target_bir_lowering=False)
    ins = {k: nc.dram_tensor(k, v, F32, kind="ExternalInput").ap() for k, v in IN_SPECS.items()}
    ins['wstream'] = nc.dram_tensor("wstream", [npieces, 128, 2048], F32, kind="ExternalInput").ap()
    outs = {k: nc.dram_tensor(k, v, F32, kind="ExternalOutput").ap() for k, v in OUT_SPECS.items()}
    with ExitStack() as es:
        P = Prog(nc, es)
        C = Ctx()
        for s in ['in', 'in2', 'in3', 'out']:
            P.new_sem(s)
        setup_common(P, C, ins)
        import os
        P.nbank = int(os.environ.get('BANKOFF', '0'))
        ws = WStream(P, ins['wstream'], npieces, R=RING)
        mixers = [conv_mixer, pool_mixer, swa_mixer, gmlp_mixer]
        import os
        for layer in range(nlayers):
            if layer == 0:
                for _ in ada_gen(P, C, ws, 0):
                    pass
                P.fence()
            use_layer_mod(C, layer)
            def mk(a0):
                w_ = (T - a0) // 3
                w_ -= w_ % 2
                return [(a0, a0 + w_), (a0 + w_, a0 + 2 * w_), (a0 + 2 * w_, T)]
            tA = TILES_A if layer == 0 else (mk(48) if layer == 1 else (mk(64) if layer == 2 else TILES_B))
            st = stages or 'nfmg'
            last = (layer == nlayers - 1)
            if 'n' in st:
                norm_mod(P, C, 0, tA)
            if 'f' in st:
                ffn(P, C, ws, layer, 0, 0, tA)
            if 'n' in st:
                norm_mod(P, C, 1, TILES_A if layer <= 2 else TILES_B)
            import os
            om = os.environ.get('ONLY_MIXER')
            if 'm' in st and (om is None or int(om) == layer):
                mixers[layer](P, C, ws, ins, outs)
            tB = mk(48) if layer == 0 else (mk(64) if layer == 1 else TILES_B)
            if 'g' in st and not (last and stages):
                norm_mod(P, C, 2, tB)
                side = ada_gen(P, C, ws, layer + 1) if layer + 1 < nlayers else None
                ffn(P, C, ws, layer, 1, 2, tB, side=side)
        norm_mod(P, C, 0, TILES_B, final=outs['yT'])
        P.final_waits = ['out']
        nrec = ws.cur + 1
        if not dry:
            assert nrec == npieces, (nrec, npieces)
            P.emit()
    return nc, ws.rec, nrec


def fm(v):
    v = np.asarray(v, np.float32)
    n = v.shape[-1] // 128
    r = v.reshape(v.shape[:-1] + (n, 128))
    return np.ascontiguousarray(np.moveaxis(r, -1, 0))


def fm_rows(a):
    a = np.asarray(a, np.float32)
    return np.ascontiguousarray(a.T.reshape(16, 128, a.shape[0]).transpose(1, 0, 2))


def pack_stream(rec, npieces, inp):
    stream = np.zeros((npieces, 128, 2048), np.float32)
    cache = {}

    def W(name, idx):
        key = (name, idx)
        if key not in cache:
            if name == 'swa_wkv':
                cache[key] = np.concatenate([inp['swa_wk'], inp['swa_wv']], axis=1)
            else:
                w = inp[name]
                for i in idx:
                    w = w[i]
                cache[key] = w
        return cache[key]
    for (piece, col, ncols, spec) in rec:
        name, idx, r0, segs = spec
        w = W(name, tuple(idx))
        o = col
        for (c0, n) in segs:
            stream[piece, :, o:o + n] = w[r0:r0 + 128, c0:c0 + n]
            o += n
    return stream


_CACHE = {}


def kernel(**inp):
    inp = {k: np.asarray(v) for k, v in inp.items()}
    if 'prog' not in _CACHE:
        _, _, npieces = build(8000, dry=True)
        _CACHE['prog'] = build(npieces)
    nc, rec, npieces = _CACHE['prog']
    in_maps = make_inputs(inp, rec, npieces)
    res = run_bass_kernel_spmd(nc, in_maps, core_ids=list(range(8)))
    return assemble(res.results)


def make_inputs(inp, rec, npieces, cores=range(8)):
    stream = pack_stream(rec, npieces, inp)

    small = np.zeros((128, SM.n), np.float32)
    o, n = SM.off['norm_g']
    small[:, o:o + n] = fm(inp['norm_g']).reshape(128, -1)
    o, n = SM.off['ada_b']
    small[:, o:o + n] = fm(inp['ada_b'].reshape(4, 9, 2048)).reshape(128, -1)
    o, n = SM.off['final_g']
    small[:, o:o + n] = fm(inp['final_g'])
    cv = np.zeros((128, CV.n), np.float32)
    for nm, val in [('b1', fm(inp['conv_b_pw1'])), ('wdw', fm(inp['conv_w_dw']).transpose(0, 2, 1).reshape(128, -1)),
                    ('bdw', fm(inp['conv_b_dw'])), ('lng', fm(inp['conv_ln_g'])), ('lnb', fm(inp['conv_ln_b'])),
                    ('b2', fm(inp['conv_b_pw2']))]:
        o, n = CV.off[nm]
        cv[:, o:o + n] = val.reshape(128, -1)
    pl = np.zeros((128, PL.n), np.float32)
    o, n = PL.off['scale']
    pl[:, o:o + n] = fm(inp['pool_scale'])
    at = np.zeros((128, AT.n), np.float32)
    o, n = AT.off['sinks']
    at[:, o:o + n] = inp['swa_sinks'][None, :]
    o, n = AT.off['dist']
    ii = np.arange(64)[:, None]
    jj = np.arange(192)[None, :]
    at[:64, o:o + n] = np.abs(ii + 128 - jj).astype(np.float32)
    o, n = AT.off['ident']
    at[:, o:o + n] = np.eye(128, dtype=np.float32)
    gm = np.zeros((128, 176), np.float32)
    gm[:, 0:32] = fm(inp['gmlp_b_in'][:4096])
    gm[:, 32:48] = fm(inp['gmlp_b_out'])
    pos = np.arange(128)
    gm[:, 48:176] = ((pos[:, None] // 64) <= (pos[None, :] // 64)).astype(np.float32)
    wsT = np.ascontiguousarray(inp['gmlp_w_s'].transpose(2, 0, 1))
    gmws = np.zeros((2, 128, 8, 128), np.float32)
    gmws[0] = wsT
    gmws[1, :64, :, :64] = wsT[:64, :, :64]
    gmws[1, 64:, :, 64:] = wsT[:64, :, :64]
    gmws = gmws.reshape(2, 128, 1024)
    gmbs = np.zeros((2, 128, 8, 128), np.float32)
    gmbs[0] = inp['gmlp_b_s'][None, :, :]
    gmbs[1, :, :, :64] = inp['gmlp_b_s'][None, :, :64]
    gmbs[1, :, :, 64:] = inp['gmlp_b_s'][None, :, :64]
    gmbc = np.zeros((128, 8, 3, 512), np.float32)
    gmbc[:, :, 0, :] = inp['gmlp_b_in'][4096:].reshape(8, 512)[None]
    gmbc[:, :, 1, :] = inp['gmlp_ln_g'].reshape(8, 512)[None]
    gmbc[:, :, 2, :] = inp['gmlp_ln_b'].reshape(8, 512)[None]

    in_maps = []
    for core in cores:
        b, half = core // 2, core % 2
        sa, sb_ = 2 * core, 2 * core + 1
        xT = np.zeros((128, 16, T), np.float32)
        if half == 0:
            xT[:, :, HALO:NPR] = fm_rows(inp['x_prompt'][b, 0:1024])
        else:
            xT[:, :, 0:NPR] = fm_rows(inp['x_prompt'][b, 1024 - HALO:2048])
        xT[:, :, 1216:1280] = fm_rows(inp['x_sample'][sa])
        xT[:, :, 1280:1344] = fm_rows(inp['x_sample'][sb_])
        sm = small.copy()
        o, n = SM.off['cT']
        sm[:, o:o + n] = fm(np.stack([inp['c_prompt'][b], inp['c_sample'][sa], inp['c_sample'][sb_]])).transpose(0, 2, 1).reshape(128, -1)
        cvc = cv.copy()
        cvc[:, CV.off['coremask'][0]] = float(half)
        plc = pl.copy()
        plc[:, PL.off['coremask'][0]] = float(half)
        o, n = PL.off['pcorr']
        pc = np.ones((4, 16), np.float32)
        if half == 0:
            for gi, wv in enumerate((2, 4, 8, 16)):
                pc[gi] = wv / np.minimum(np.arange(16) + 1, wv)
        plc[:, o:o + n] = pc.reshape(-1)[None]
        atc = at.copy()
        o, n = AT.off['amask']
        am = np.zeros((2, 192), np.float32)
        if half == 0:
            am[0, :128] = -1e9
            am[1, :64] = -1e9
        atc[:, o:o + n] = am.reshape(-1)[None]
        convc = np.stack([fm_rows(inp['cache_conv'][sa]), fm_rows(inp['cache_conv'][sb_])], axis=2)
        poolc = np.stack([fm_rows(inp['cache_pool'][sa]), fm_rows(inp['cache_pool'][sb_])], axis=2)
        kc = np.stack([inp['cache_swa_k'][sa].reshape(128, 256), inp['cache_swa_k'][sb_].reshape(128, 256)])
        vc = np.stack([inp['cache_swa_v'][sa].reshape(128, 256), inp['cache_swa_v'][sb_].reshape(128, 256)])
        kcT = np.zeros((128, 2, 4, 128), np.float32)
        for s, sq in enumerate((sa, sb_)):
            kk = inp['cache_swa_k'][sq].transpose(2, 1, 0)
            kcT[:64, s] = kk
            kcT[64:, s] = kk
        in_maps.append({'xT': xT, 'small': sm, 'cv': cvc, 'pl': plc, 'at': atc, 'gm': gm,
                        'convc': np.ascontiguousarray(convc), 'poolc': np.ascontiguousarray(poolc), 'kcT': kcT,
                        'kc': np.ascontiguousarray(kc, np.float32), 'vc': np.ascontiguousarray(vc, np.float32),
                        'gmws': gmws, 'gmbs': gmbs, 'gmbc': gmbc, 'wstream': stream})
    return in_maps


def assemble(R):
    def tm(a):
        return a.transpose(2, 1, 0).reshape(a.shape[2], 2048)
    y_p = np.zeros((4, 2048, 2048), np.float32)
    y_s = np.zeros((16, 64, 2048), np.float32)
    conv_p = np.zeros((4, 30, 2048), np.float32)
    conv_s = np.zeros((16, 30, 2048), np.float32)
    pool_p = np.zeros((4, 15, 2048), np.float32)
    pool_s = np.zeros((16, 15, 2048), np.float32)
    k_p = np.zeros((4, 128, 4, 64), np.float32)
    v_p = np.zeros((4, 128, 4, 64), np.float32)
    k_s = np.zeros((16, 128, 4, 64), np.float32)
    v_s = np.zeros((16, 128, 4, 64), np.float32)
    g_s = np.zeros((16, 64, 4096), np.float32)
    for core in range(8):
        b, half = core // 2, core % 2
        r = R[core]
        yt = tm(r['yT'])
        y_p[b, half * 1024:(half + 1) * 1024] = yt[0:1024]
        for s in range(2):
            sq = 2 * core + s
            y_s[sq] = yt[1024 + 64 * s:1088 + 64 * s]
            conv_s[sq] = tm(r['convo'][:, :, 1 + s, :])
            pool_s[sq] = tm(r['poolo'][:, :, 1 + s, :])
            k_s[sq] = r['ko'][1 + s].reshape(128, 4, 64)
            v_s[sq] = r['vo'][1 + s].reshape(128, 4, 64)
            g_s[sq] = r['gvo'][64 * s:64 * s + 64]
        if half == 1:
            conv_p[b] = tm(r['convo'][:, :, 0, :])
            pool_p[b] = tm(r['poolo'][:, :, 0, :])
            k_p[b] = r['ko'][0].reshape(128, 4, 64)
            v_p[b] = r['vo'][0].reshape(128, 4, 64)
    return (y_p, y_s, conv_p, conv_s, pool_p, pool_s, k_p, v_p, k_s, v_s, g_s)
```
